# Optimizing a Trainium2 kernel written in Bass

```python
import math
import jax, jax.numpy as jnp
from jax import lax
import numpy as np

D_MODEL = 1024
BATCH = 8
SEQ = 2048
DEPTH = 4
DEC_BATCH = 32
DEC_SEQ = 32
PAST_LEN = 1024

CHUNK = 64
HEAD_DIM = 64
H_A = 8
N_IDX_HEADS = 8
D_IDX = 64
TOPK_MAX = 256
T5_BUCKETS = 32
T5_MAX_DIST = 128
H_B = 8
DK_B = 64
DV_B = 64
CONV_B = 4
H_C = 8
DK_C = 32
DV_C = 64
H_D = 8
BAND_CHUNKS = 8
REL_CLIP = 128
D_FF = 2816
CONV_FF = 3
N_EVEN = (DEPTH + 1) // 2
N_ODD = DEPTH // 2
ALPHA = (2 * DEPTH) ** 0.25
BETA_INIT = (8 * DEPTH) ** -0.25
W_A = H_A * HEAD_DIM
W_B = H_B * DV_B
W_C = H_C * DV_C
W_D = H_D * HEAD_DIM
EVEN_SIZES = (W_A, W_A, W_A, N_IDX_HEADS * D_IDX, D_IDX, N_IDX_HEADS, 2 * H_B * DK_B + W_B, H_B, H_B, W_B)
ODD_SIZES = (H_C * DK_C, H_C * DK_C, W_C, H_C, H_C, W_C, 3 * W_D)

kernel_name = 'hybrid_dsa_gdn_mlstm_band_streaming_step'


def split_cols(h, sizes):
    return jnp.split(h, [int(s) for s in np.cumsum(sizes)[:-1]], axis=-1)


def layer_norm(x, g, b, eps=1e-5):
    xf = x.astype(jnp.float32)
    mu = jnp.mean(xf, -1, keepdims=True)
    var = jnp.mean(jnp.square(xf - mu), -1, keepdims=True)
    return ((xf - mu) * lax.rsqrt(var + eps) * g.astype(jnp.float32) + b.astype(jnp.float32)).astype(x.dtype)


def head_rms_norm(x, g, eps=1e-6):
    xf = x.astype(jnp.float32)
    return xf * lax.rsqrt(jnp.mean(xf * xf, -1, keepdims=True) + eps) * g.astype(jnp.float32)


def l2_normalize(x, eps=1e-6):
    return x * lax.rsqrt(jnp.sum(x * x, -1, keepdims=True) + eps)


def causal_dwconv(x, hist, w):
    width = w.shape[0]
    t = x.shape[1]
    xp = jnp.concatenate([hist.astype(x.dtype), x], axis=1)
    y = sum(xp[:, j:j + t] * w[j] for j in range(width))
    return y, xp[:, xp.shape[1] - (width - 1):]


def chunk_scan(step, carry, xs, chunk):
    t = xs[0].shape[1]
    n = t // chunk
    def to_blocks(u):
        return jnp.swapaxes(u.reshape((u.shape[0], n, chunk) + u.shape[2:]), 0, 1)
    carry, ys = lax.scan(step, carry, tuple(to_blocks(u) for u in xs))
    ys = jnp.swapaxes(ys, 0, 1)
    return carry, ys.reshape((ys.shape[0], t) + ys.shape[3:])


def t5_bucket(rel):
    nb = T5_BUCKETS // 2
    max_exact = nb // 2
    n = jnp.abs(rel)
    n_f = jnp.maximum(n, 1).astype(jnp.float32)
    large = max_exact + (jnp.log(n_f / max_exact) / math.log(T5_MAX_DIST / max_exact) * (nb - max_exact)).astype(jnp.int32)
    large = jnp.minimum(large, nb - 1)
    return jnp.where(rel > 0, nb, 0) + jnp.where(n < max_exact, n, large)


def rel_bias(table, rel):
    return table[jnp.clip(rel, -REL_CLIP, REL_CLIP) + REL_CLIP]


def dsa_attend(qa, qi, wi, pos_q, ka, va, ki, t5_table, topk):
    bsz, length = ka.shape[0], ka.shape[1]
    key_pos = jnp.arange(length)
    admissible = (key_pos[None, :] // CHUNK) <= (pos_q[:, None] // CHUNK)
    idx_logits = jnp.einsum('bqnd,bld->bqnl', qi, ki).astype(jnp.float32) * D_IDX ** -0.5
    score = jnp.einsum('bqnl,bqn->bql', jax.nn.relu(idx_logits), wi.astype(jnp.float32) * N_IDX_HEADS ** -0.5)
    score = jnp.where(admissible[None], score, -jnp.inf)
    _, sel = lax.top_k(score, topk)
    bidx = jnp.arange(bsz)[:, None, None]
    k_sel = ka[bidx, sel]
    v_sel = va[bidx, sel]
    valid = (sel // CHUNK) <= (pos_q[None, :, None] // CHUNK)
    bias = t5_table[t5_bucket(sel - pos_q[None, :, None])].astype(jnp.float32)
    s = jnp.einsum('bqhd,bqkhd->bqhk', qa, k_sel).astype(jnp.float32) * HEAD_DIM ** -0.5 + jnp.swapaxes(bias, -1, -2)
    s = jnp.where(valid[:, :, None, :], s, -jnp.inf)
    p = jax.nn.softmax(s, axis=-1).astype(va.dtype)
    return jnp.einsum('bqhk,bqkhd->bqhd', p, v_sel)


def dsa_prompt(qa, qi, wi, ka, va, ki, t5_table):
    bsz, s = qa.shape[:2]
    nc = s // CHUNK
    topk = min(TOPK_MAX, s // 4)
    def to_blocks(u):
        return jnp.swapaxes(u.reshape((bsz, nc, CHUNK) + u.shape[2:]), 0, 1)
    def one_block(args):
        q_blk, qi_blk, wi_blk, c = args
        pos_q = c * CHUNK + jnp.arange(CHUNK)
        return dsa_attend(q_blk, qi_blk, wi_blk, pos_q, ka, va, ki, t5_table, topk)
    out = lax.map(one_block, (to_blocks(qa), to_blocks(qi), to_blocks(wi), jnp.arange(nc)))
    return jnp.swapaxes(out, 0, 1).reshape(bsz, s, H_A, HEAD_DIM)


def gdn_chunk(s_state, xs):
    q, k, v, beta, g = xs
    c = q.shape[1]
    lower = jnp.tril(jnp.ones((c, c), bool))
    strict = jnp.tril(jnp.ones((c, c), bool), -1)
    g_cum = jnp.cumsum(g, axis=1)
    g_h = jnp.moveaxis(g_cum, 1, -1)
    decay = jnp.where(lower, jnp.exp(jnp.where(lower, g_h[..., :, None] - g_h[..., None, :], 0.0)), 0.0)
    beta_h = jnp.moveaxis(beta, 1, -1)
    a_mat = jnp.where(strict, beta_h[..., :, None] * jnp.einsum('bihd,bjhd->bhij', k, k) * decay, 0.0)
    eye = jnp.eye(c, dtype=a_mat.dtype)
    t_mat = lax.linalg.triangular_solve(eye + a_mat, jnp.broadcast_to(eye, a_mat.shape), left_side=True, lower=True)
    value = jnp.einsum('bhij,bjhd->bihd', t_mat, v * beta[..., None])
    k_cum = jnp.einsum('bhij,bjhd->bihd', t_mat, k * (beta * jnp.exp(g_cum))[..., None])
    v_new = value - jnp.einsum('bihk,bhkv->bihv', k_cum, s_state)
    attn = jnp.einsum('bihd,bjhd->bhij', q, k) * decay
    o = (jnp.einsum('bihk,bhkv->bihv', q * jnp.exp(g_cum)[..., None], s_state)
         + jnp.einsum('bhij,bjhv->bihv', attn, v_new))
    g_last = g_cum[:, -1]
    s_new = (s_state * jnp.exp(g_last)[..., None, None]
             + jnp.einsum('bjhk,bjhv->bhkv', k * jnp.exp(g_last[:, None] - g_cum)[..., None], v_new))
    return s_new, o


def gated_deltanet(qkv_b, a_b, b_b, z_b, hist, s0, conv_w, a_log, dt_bias, norm_w, chunk):
    f32 = jnp.float32
    bsz, t = qkv_b.shape[:2]
    conv, new_hist = causal_dwconv(qkv_b, hist, conv_w)
    qkv = jax.nn.silu(conv.astype(f32))
    q, k, v = split_cols(qkv, (H_B * DK_B, H_B * DK_B, W_B))
    q = l2_normalize(q.reshape(bsz, t, H_B, DK_B)) * DK_B ** -0.5
    k = l2_normalize(k.reshape(bsz, t, H_B, DK_B))
    v = v.reshape(bsz, t, H_B, DV_B)
    beta = jax.nn.sigmoid(b_b.astype(f32))
    g = -jnp.exp(a_log.astype(f32)) * jax.nn.softplus(a_b.astype(f32) + dt_bias.astype(f32))
    s_new, o = chunk_scan(gdn_chunk, s0, (q, k, v, beta, g), chunk)
    y = head_rms_norm(o, norm_w) * jax.nn.silu(z_b.astype(f32)).reshape(bsz, t, H_B, DV_B)
    return y.reshape(bsz, t, W_B).astype(qkv_b.dtype), s_new, new_hist


def mlstm_chunk(carry, xs):
    c_st, n_st, m_st = carry
    q, k, v, ig, lf = xs
    length = q.shape[1]
    causal = jnp.tril(jnp.ones((length, length), bool))
    f_cum = jnp.moveaxis(jnp.cumsum(lf, axis=1), 1, -1)
    ig_h = jnp.moveaxis(ig, 1, -1)
    log_w = jnp.where(causal, f_cum[..., :, None] - f_cum[..., None, :] + ig_h[..., None, :], -jnp.inf)
    log_inter = f_cum + m_st[..., None]
    m_t = jnp.maximum(log_inter, jnp.max(log_w, axis=-1))
    w_intra = jnp.exp(log_w - m_t[..., None])
    w_inter = jnp.exp(log_inter - m_t)
    qk = jnp.einsum('bthd,bshd->bhts', q, k) * w_intra
    num = jnp.einsum('bhts,bshv->bhtv', qk, v) + w_inter[..., None] * jnp.einsum('bthk,bhkv->bhtv', q, c_st)
    den = jnp.sum(qk, -1) + w_inter * jnp.einsum('bthk,bhk->bht', q, n_st)
    h = num / jnp.maximum(jnp.abs(den), jnp.exp(-m_t))[..., None]
    m_new = m_t[..., -1]
    w_last = w_intra[..., -1, :]
    dec = w_inter[..., -1]
    c_new = dec[..., None, None] * c_st + jnp.einsum('bhs,bshk,bshv->bhkv', w_last, k, v)
    n_new = dec[..., None] * n_st + jnp.einsum('bhs,bshk->bhk', w_last, k)
    return (c_new, n_new, m_new), jnp.moveaxis(h, 1, 2)


def mlstm(qc, kc, vc, ic, fc, oc, carry, i_bias, f_bias, norm_w, chunk):
    f32 = jnp.float32
    bsz, t = qc.shape[:2]
    q = qc.astype(f32).reshape(bsz, t, H_C, DK_C)
    k = kc.astype(f32).reshape(bsz, t, H_C, DK_C) * DK_C ** -0.5
    v = vc.astype(f32).reshape(bsz, t, H_C, DV_C)
    ig = ic.astype(f32) + i_bias.astype(f32)
    lf = jax.nn.log_sigmoid(fc.astype(f32) + f_bias.astype(f32))
    carry, h = chunk_scan(mlstm_chunk, carry, (q, k, v, ig, lf), chunk)
    y = head_rms_norm(h, norm_w) * jax.nn.sigmoid(oc.astype(f32)).reshape(bsz, t, H_C, DV_C)
    return y.reshape(bsz, t, W_C).astype(qc.dtype), carry


def band_prompt(qd, kd, vd, rel_table):
    bsz, s = qd.shape[:2]
    nc = s // CHUNK
    nb = BAND_CHUNKS + 1
    qc = qd.reshape(bsz, nc, CHUNK, H_D, HEAD_DIM)
    def band(u):
        uc = jnp.pad(u.reshape(bsz, nc, CHUNK, H_D, HEAD_DIM), ((0, 0), (BAND_CHUNKS, 0), (0, 0), (0, 0), (0, 0)))
        return jnp.stack([uc[:, j:j + nc] for j in range(nb)], axis=2).reshape(bsz, nc, nb * CHUNK, H_D, HEAD_DIM)
    kb, vb = band(kd), band(vd)
    r = jnp.arange(nb * CHUNK)
    i = jnp.arange(CHUNK)
    rel = (r[None, :] - BAND_CHUNKS * CHUNK) - i[:, None]
    bias = jnp.transpose(rel_bias(rel_table, rel), (2, 0, 1)).astype(jnp.float32)
    valid = (jnp.arange(nc)[:, None] - BAND_CHUNKS + r[None, :] // CHUNK) >= 0
    s_ = jnp.einsum('bcihd,bckhd->bchik', qc, kb).astype(jnp.float32) * HEAD_DIM ** -0.5 + bias[None, None]
    s_ = jnp.where(valid[None, :, None, None, :], s_, -jnp.inf)
    p = jax.nn.softmax(s_, axis=-1).astype(vd.dtype)
    return jnp.einsum('bchik,bckhd->bcihd', p, vb).reshape(bsz, s, H_D, HEAD_DIM)


def band_sample(qd, kd, vd, cache_k, cache_v, rel_table, past):
    t = qd.shape[1]
    w = cache_k.shape[1]
    k_all = jnp.concatenate([cache_k.astype(kd.dtype), kd], axis=1)
    v_all = jnp.concatenate([cache_v.astype(vd.dtype), vd], axis=1)
    pos_q = past + jnp.arange(t)
    pos_k = jnp.concatenate([past - w + jnp.arange(w), pos_q])
    qch = pos_q // CHUNK
    kch = pos_k // CHUNK
    valid = (pos_k[None] >= 0) & (kch[None] >= qch[:, None] - BAND_CHUNKS) & (kch[None] <= qch[:, None])
    bias = jnp.transpose(rel_bias(rel_table, pos_k[None] - pos_q[:, None]), (2, 0, 1)).astype(jnp.float32)
    s_ = jnp.einsum('bihd,bkhd->bhik', qd, k_all).astype(jnp.float32) * HEAD_DIM ** -0.5 + bias[None]
    s_ = jnp.where(valid[None, None], s_, -jnp.inf)
    p = jax.nn.softmax(s_, axis=-1).astype(v_all.dtype)
    return jnp.einsum('bhik,bkhd->bihd', p, v_all)


def even_parts(x, w_in):
    bsz, t = x.shape[:2]
    qa, ka, va, qi, ki, wi, qkv_b, a_b, b_b, z_b = split_cols(x @ w_in, EVEN_SIZES)
    def heads(u):
        return u.reshape(bsz, t, H_A, HEAD_DIM)
    return heads(qa), heads(ka), heads(va), qi.reshape(bsz, t, N_IDX_HEADS, D_IDX), ki, wi, qkv_b, a_b, b_b, z_b


def odd_parts(x, w_in):
    bsz, t = x.shape[:2]
    qc, kc, vc, ic, fc, oc, qkv_d = split_cols(x @ w_in, ODD_SIZES)
    qd, kd, vd = [u.reshape(bsz, t, H_D, HEAD_DIM) for u in jnp.split(qkv_d, 3, axis=-1)]
    return qc, kc, vc, ic, fc, oc, qd, kd, vd


def conv_ffn(x, hist, w_up, conv_w, w_down):
    h, new_hist = causal_dwconv(x @ w_up, hist, conv_w)
    g, u = jnp.split(h, 2, axis=-1)
    return (jax.nn.silu(g) * u) @ w_down, new_hist


def setup_inputs(seed: int = 0) -> dict:
    key = jax.random.key(seed)
    ks = iter(jax.random.split(key, 40))
    def nrm(shape, scale=1.0):
        return jax.random.normal(next(ks), shape, jnp.float32) * scale
    d_win = min(BAND_CHUNKS * CHUNK, PAST_LEN)
    e_tot = sum(EVEN_SIZES)
    o_tot = sum(ODD_SIZES)
    conv_b_width = 2 * H_B * DK_B + W_B
    dt = jnp.exp(jax.random.uniform(next(ks), (N_EVEN, H_B), minval=math.log(1e-3), maxval=math.log(1e-1)))
    return {
        'x_prompt': nrm((BATCH, SEQ, D_MODEL)),
        'x_sample': nrm((DEC_BATCH, DEC_SEQ, D_MODEL)),
        'cache_a_k': nrm((N_EVEN, DEC_BATCH, PAST_LEN, H_A, HEAD_DIM)),
        'cache_a_v': nrm((N_EVEN, DEC_BATCH, PAST_LEN, H_A, HEAD_DIM)),
        'cache_a_kidx': nrm((N_EVEN, DEC_BATCH, PAST_LEN, D_IDX)),
        'state_b_s': nrm((N_EVEN, DEC_BATCH, H_B, DK_B, DV_B), 0.3),
        'state_b_conv': nrm((N_EVEN, DEC_BATCH, CONV_B - 1, conv_b_width)),
        'state_c_c': nrm((N_ODD, DEC_BATCH, H_C, DK_C, DV_C), 0.5),
        'state_c_n': nrm((N_ODD, DEC_BATCH, H_C, DK_C), 0.5),
        'state_c_m': nrm((N_ODD, DEC_BATCH, H_C)),
        'cache_d_k': nrm((N_ODD, DEC_BATCH, d_win, H_D, HEAD_DIM)),
        'cache_d_v': nrm((N_ODD, DEC_BATCH, d_win, H_D, HEAD_DIM)),
        'state_ffn_conv': nrm((DEPTH, DEC_BATCH, CONV_FF - 1, 2 * D_FF)),
        'w_in_even': nrm((N_EVEN, D_MODEL, e_tot), D_MODEL ** -0.5),
        'w_out_even': nrm((N_EVEN, W_A + W_B, D_MODEL), (W_A + W_B) ** -0.5 * BETA_INIT),
        't5_bias': nrm((T5_BUCKETS, H_A), 0.2),
        'b_conv_w': nrm((N_EVEN, CONV_B, conv_b_width), CONV_B ** -0.5),
        'b_a_log': jnp.log(jax.random.uniform(next(ks), (N_EVEN, H_B), minval=1.0, maxval=16.0)),
        'b_dt_bias': dt + jnp.log(-jnp.expm1(-dt)),
        'b_norm_w': 1.0 + nrm((N_EVEN, DV_B), 0.02),
        'w_in_odd': nrm((N_ODD, D_MODEL, o_tot), D_MODEL ** -0.5),
        'w_out_odd': nrm((N_ODD, W_C + W_D, D_MODEL), (W_C + W_D) ** -0.5 * BETA_INIT),
        'c_i_bias': nrm((N_ODD, H_C), 0.1),
        'c_f_bias': jax.random.uniform(next(ks), (N_ODD, H_C), minval=3.0, maxval=6.0),
        'c_norm_w': 1.0 + nrm((N_ODD, DV_C), 0.02),
        'd_rel_bias': nrm((N_ODD, 2 * REL_CLIP + 1, H_D), 0.2),
        'ln_mix_g': 1.0 + nrm((DEPTH, D_MODEL), 0.02),
        'ln_mix_b': nrm((DEPTH, D_MODEL), 0.02),
        'ln_ffn_g': 1.0 + nrm((DEPTH, D_MODEL), 0.02),
        'ln_ffn_b': nrm((DEPTH, D_MODEL), 0.02),
        'ffn_w_up': nrm((DEPTH, D_MODEL, 2 * D_FF), D_MODEL ** -0.5),
        'ffn_conv_w': nrm((DEPTH, CONV_FF, 2 * D_FF), CONV_FF ** -0.5),
        'ffn_w_down': nrm((DEPTH, D_FF, D_MODEL), D_FF ** -0.5 * BETA_INIT),
    }


def reference(x_prompt, x_sample, cache_a_k, cache_a_v, cache_a_kidx, state_b_s, state_b_conv,
              state_c_c, state_c_n, state_c_m, cache_d_k, cache_d_v, state_ffn_conv,
              w_in_even, w_out_even, t5_bias, b_conv_w, b_a_log, b_dt_bias, b_norm_w,
              w_in_odd, w_out_odd, c_i_bias, c_f_bias, c_norm_w, d_rel_bias,
              ln_mix_g, ln_mix_b, ln_ffn_g, ln_ffn_b, ffn_w_up, ffn_conv_w, ffn_w_down):
    f32 = jnp.float32
    bp, sp = x_prompt.shape[:2]
    bs, ts = x_sample.shape[:2]
    past = cache_a_k.shape[2]
    d_win_p = min(BAND_CHUNKS * CHUNK, sp)
    xp, xs = x_prompt, x_sample
    ak_p, ak_s, av_p, av_s, aki_p, aki_s = [], [], [], [], [], []
    bs_p, bs_s, bc_p, bc_s = [], [], [], []
    cc_p, cc_s, cn_p, cn_s, cm_p, cm_s = [], [], [], [], [], []
    dk_p, dk_s, dv_p, dv_s = [], [], [], []
    fc_p, fc_s = [], []
    for layer in range(DEPTH):
        if layer % 2 == 0:
            e = layer // 2
            qa, ka, va, qi, ki, wi, qkv_b, a_b, b_b, z_b = even_parts(xp, w_in_even[e])
            o_a = dsa_prompt(qa, qi, wi, ka, va, ki, t5_bias)
            y_b, s_b, h_b = gated_deltanet(
                qkv_b, a_b, b_b, z_b, jnp.zeros((bp, CONV_B - 1, qkv_b.shape[-1]), xp.dtype),
                jnp.zeros((bp, H_B, DK_B, DV_B), f32), b_conv_w[e], b_a_log[e], b_dt_bias[e], b_norm_w[e], CHUNK)
            mix_p = jnp.concatenate([o_a.reshape(bp, sp, W_A), y_b], axis=-1) @ w_out_even[e]
            ak_p.append(ka); av_p.append(va); aki_p.append(ki)
            bs_p.append(s_b.astype(state_b_s.dtype)); bc_p.append(h_b.astype(state_b_conv.dtype))
            qa, ka, va, qi, ki, wi, qkv_b, a_b, b_b, z_b = even_parts(xs, w_in_even[e])
            o_a = dsa_attend(qa, qi, wi, past + jnp.arange(ts),
                             jnp.concatenate([cache_a_k[e].astype(ka.dtype), ka], axis=1),
                             jnp.concatenate([cache_a_v[e].astype(va.dtype), va], axis=1),
                             jnp.concatenate([cache_a_kidx[e].astype(ki.dtype), ki], axis=1),
                             t5_bias, min(TOPK_MAX, (past + ts) // 4))
            y_b, s_b, h_b = gated_deltanet(
                qkv_b, a_b, b_b, z_b, state_b_conv[e], state_b_s[e].astype(f32),
                b_conv_w[e], b_a_log[e], b_dt_bias[e], b_norm_w[e], ts)
            mix_s = jnp.concatenate([o_a.reshape(bs, ts, W_A), y_b], axis=-1) @ w_out_even[e]
            ak_s.append(ka); av_s.append(va); aki_s.append(ki)
            bs_s.append(s_b.astype(state_b_s.dtype)); bc_s.append(h_b.astype(state_b_conv.dtype))
        else:
            o = layer // 2
            qc, kc, vc, ic, fc, oc, qd, kd, vd = odd_parts(xp, w_in_odd[o])
            carry0 = (jnp.zeros((bp, H_C, DK_C, DV_C), f32), jnp.zeros((bp, H_C, DK_C), f32), jnp.zeros((bp, H_C), f32))
            y_c, (c_c, c_n, c_m) = mlstm(qc, kc, vc, ic, fc, oc, carry0, c_i_bias[o], c_f_bias[o], c_norm_w[o], CHUNK)
            o_d = band_prompt(qd, kd, vd, d_rel_bias[o])
            mix_p = jnp.concatenate([y_c, o_d.reshape(bp, sp, W_D)], axis=-1) @ w_out_odd[o]
            cc_p.append(c_c.astype(state_c_c.dtype)); cn_p.append(c_n.astype(state_c_n.dtype)); cm_p.append(c_m.astype(state_c_m.dtype))
            dk_p.append(kd[:, sp - d_win_p:]); dv_p.append(vd[:, sp - d_win_p:])
            qc, kc, vc, ic, fc, oc, qd, kd, vd = odd_parts(xs, w_in_odd[o])
            carry0 = (state_c_c[o].astype(f32), state_c_n[o].astype(f32), state_c_m[o].astype(f32))
            y_c, (c_c, c_n, c_m) = mlstm(qc, kc, vc, ic, fc, oc, carry0, c_i_bias[o], c_f_bias[o], c_norm_w[o], ts)
            o_d = band_sample(qd, kd, vd, cache_d_k[o], cache_d_v[o], d_rel_bias[o], past)
            mix_s = jnp.concatenate([y_c, o_d.reshape(bs, ts, W_D)], axis=-1) @ w_out_odd[o]
            cc_s.append(c_c.astype(state_c_c.dtype)); cn_s.append(c_n.astype(state_c_n.dtype)); cm_s.append(c_m.astype(state_c_m.dtype))
            dk_s.append(kd); dv_s.append(vd)
        xp = layer_norm(ALPHA * xp + mix_p, ln_mix_g[layer], ln_mix_b[layer])
        xs = layer_norm(ALPHA * xs + mix_s, ln_mix_g[layer], ln_mix_b[layer])
        f_p, hist_p = conv_ffn(xp, jnp.zeros((bp, CONV_FF - 1, 2 * D_FF), xp.dtype), ffn_w_up[layer], ffn_conv_w[layer], ffn_w_down[layer])
        f_s, hist_s = conv_ffn(xs, state_ffn_conv[layer], ffn_w_up[layer], ffn_conv_w[layer], ffn_w_down[layer])
        fc_p.append(hist_p.astype(state_ffn_conv.dtype)); fc_s.append(hist_s.astype(state_ffn_conv.dtype))
        xp = layer_norm(ALPHA * xp + f_p, ln_ffn_g[layer], ln_ffn_b[layer])
        xs = layer_norm(ALPHA * xs + f_s, ln_ffn_g[layer], ln_ffn_b[layer])
    return (xp, xs,
            jnp.stack(ak_p), jnp.stack(ak_s), jnp.stack(av_p), jnp.stack(av_s), jnp.stack(aki_p), jnp.stack(aki_s),
            jnp.stack(bs_p), jnp.stack(bs_s), jnp.stack(bc_p), jnp.stack(bc_s),
            jnp.stack(cc_p), jnp.stack(cc_s), jnp.stack(cn_p), jnp.stack(cn_s), jnp.stack(cm_p), jnp.stack(cm_s),
            jnp.stack(dk_p), jnp.stack(dk_s), jnp.stack(dv_p), jnp.stack(dv_s),
            jnp.stack(fc_p), jnp.stack(fc_s))
```

```python
import math
from contextlib import ExitStack
import numpy as np
import concourse.bass as bass
import concourse.mybir as mybir
from concourse.bass_utils import run_bass_kernel_spmd

F32 = mybir.dt.float32
BF16 = mybir.dt.bfloat16
AF = mybir.ActivationFunctionType
ALU = mybir.AluOpType
AX = mybir.AxisListType

SAME_ENGINE_SYNC = False
DEBUG_WHERE = True
N_DMA_SEMS = 32
GRAN = 512

D = 1024
SEQ = 2048
NS = 4
TS = 32
T = SEQ + NS * TS
DEPTH = 4
PAST = 1024
DFF = 2816
ALPHA = (2 * DEPTH) ** 0.25
E_TOT = 4184
O_TOT = 3088
BIG = 30000.0


class _Op:
    __slots__ = ("eng", "fn", "reads", "writes", "is_dma", "deps", "sig", "dsem", "dval", "where", "small")

    def __init__(self, eng, fn, reads, writes, is_dma):
        self.eng = eng
        self.fn = fn
        self.reads = reads
        self.writes = writes
        self.is_dma = is_dma
        self.deps = None
        self.sig = None
        self.dsem = None
        self.dval = None
        self.small = False


class Prog:
    ENGS = ("pe", "act", "dve", "pool", "sp")

    def __init__(self, nc):
        self.nc = nc
        self.ops = []
        self.h = {"pe": nc.tensor, "act": nc.scalar, "dve": nc.vector, "pool": nc.gpsimd, "sp": nc.sync}

    def _where(self):
        import sys
        f = sys._getframe(3)
        out = []
        for _ in range(4):
            if f is None:
                break
            out.append("%s:%d" % (f.f_code.co_name, f.f_lineno))
            f = f.f_back
        return " < ".join(out)

    def op(self, eng, fn, reads=(), writes=(), small=False):
        o = _Op(eng, fn, tuple(reads), tuple(writes), False)
        o.small = small
        o.where = self._where() if DEBUG_WHERE else None
        self.ops.append(o)

    def dma(self, eng, fn, reads=(), writes=()):
        o = _Op(eng, fn, tuple(reads), tuple(writes), True)
        o.where = self._where() if DEBUG_WHERE else None
        self.ops.append(o)

    def emit(self, stack):
        nc = self.nc
        ops = self.ops
        last_w = {}
        readers = {}
        for i, o in enumerate(ops):
            deps = set()
            psr = tuple(r for r in o.reads if isinstance(r, tuple))
            if psr:
                o.reads = tuple(r for r in o.reads if not isinstance(r, tuple))
                o.writes = o.writes + psr
            for r in o.reads:
                w = last_w.get(r)
                if w is not None:
                    deps.add(w)
            for r in o.writes:
                w = last_w.get(r)
                if w is not None:
                    deps.add(w)
                rl = readers.get(r)
                if rl:
                    deps.update(rl)
            deps.discard(i)
            o.deps = deps
            for r in o.reads:
                readers.setdefault(r, []).append(i)
            for r in o.writes:
                last_w[r] = i
                readers[r] = []
        need_sig = [False] * len(ops)
        for o in ops:
            for d in o.deps:
                od = ops[d]
                if od.is_dma:
                    continue
                if od.eng == o.eng and not o.is_dma and (od.eng == "pe" or not (SAME_ENGINE_SYNC or od.small)):
                    continue
                need_sig[d] = True
        sems = {e: stack.enter_context(nc.semaphore("s_" + e)) for e in self.ENGS}
        dsems = [stack.enter_context(nc.semaphore("d%d" % k)) for k in range(N_DMA_SEMS)]
        dsem_cum = [0] * N_DMA_SEMS
        cnt = {e: 0 for e in self.ENGS}
        for i, o in enumerate(ops):
            if not o.is_dma and need_sig[i]:
                cnt[o.eng] += 1
                o.sig = cnt[o.eng]
        waited = {e: {} for e in self.ENGS}
        dwaited = {e: {} for e in self.ENGS}
        n_dma = 0
        n_wait = 0
        for i, o in enumerate(ops):
            e = o.eng
            h = self.h[e]
            need = {}
            dneed = {}
            for d in o.deps:
                od = ops[d]
                if od.is_dma:
                    if dwaited[e].get(od.dsem, 0) < od.dval:
                        dneed[od.dsem] = max(dneed.get(od.dsem, 0), od.dval)
                else:
                    if od.sig is None:
                        continue
                    if od.eng == e and not o.is_dma and (e == "pe" or not (SAME_ENGINE_SYNC or od.small)):
                        continue
                    if waited[e].get(od.eng, 0) < od.sig:
                        need[od.eng] = max(need.get(od.eng, 0), od.sig)
            slot = None
            if o.is_dma:
                slot = n_dma % N_DMA_SEMS
                n_dma += 1
                if dsem_cum[slot] > 0 and dwaited[e].get(slot, 0) < dsem_cum[slot]:
                    dneed[slot] = max(dneed.get(slot, 0), dsem_cum[slot])
            for src, v in need.items():
                h.wait_ge(sems[src], v)
                waited[e][src] = v
                n_wait += 1
            for s_, v in dneed.items():
                h.wait_ge(dsems[s_], v)
                dwaited[e][s_] = v
                n_wait += 1
            try:
                ins = o.fn(h)
            except BaseException:
                print("FAILED OP at", o.where, flush=True)
                raise
            if o.is_dma:
                dsem_cum[slot] += 16
                o.dsem = slot
                o.dval = dsem_cum[slot]
                ins.then_inc(dsems[slot], 16)
            elif o.sig is not None:
                ins.then_inc(sems[e], 1)
        h = self.h["sp"]
        for slot in range(N_DMA_SEMS):
            if dsem_cum[slot] > 0 and dwaited["sp"].get(slot, 0) < dsem_cum[slot]:
                h.wait_ge(dsems[slot], dsem_cum[slot])
        return dict(n_ops=len(ops), n_dma=n_dma, n_wait=n_wait, sig=dict(cnt))


class V:
    __slots__ = ("ap", "res")

    def __init__(self, ap, res):
        self.ap = ap
        self.res = res


class Tile:
    def __init__(self, ap, off, span, isz):
        self.ap = ap
        self.off = off
        self.span = span
        self.isz = isz
        self._res = None

    @property
    def res(self):
        if self._res is None:
            self._res = tuple(range(self.off // GRAN, (self.off + self.span - 1) // GRAN + 1))
        return self._res

    def __getitem__(self, idx):
        return V(self.ap[idx], self.res)

    def all(self):
        return V(self.ap, self.res)

    def part(self, i, n=1):
        shp = self.ap.shape
        stride = 1
        for s in shp[2:]:
            stride *= s
        sb = stride * self.isz
        if n == 1:
            return Tile(self.ap[:, i], self.off + i * sb, sb, self.isz)
        return Tile(self.ap[:, i:i + n], self.off + i * sb, sb * n, self.isz)

    def cols(self, lo, hi):
        return Tile(self.ap[:, lo:hi], self.off + lo * self.isz, (hi - lo) * self.isz, self.isz)


class Arena:
    def __init__(self, nc, kbytes):
        self.nc = nc
        self.nwords = kbytes * 256
        self.t = nc.alloc_sbuf_tensor("arena", [128, self.nwords], F32)
        self.top = 0
        self.peak = 0

    def alloc(self, shape, dt=F32):
        isz = 4 if dt == F32 else 2
        n = 1
        for s in shape[1:]:
            n *= s
        nbytes = (n * isz + 3) // 4 * 4
        off = self.top
        self.top += nbytes
        if self.top > self.peak:
            self.peak = self.top
        assert self.top <= self.nwords * 4, "arena overflow %d" % self.top
        ap = self.t[0:shape[0], off // 4: off // 4 + nbytes // 4]
        if dt != F32:
            ap = ap.bitcast(dt)
            ap = ap[:, 0:n]
        if len(shape) == 3:
            ap = ap.rearrange("p (a b) -> p a b", a=shape[1])
        elif len(shape) == 4:
            ap = ap.rearrange("p (a b c) -> p a b c", a=shape[1], b=shape[2])
        return Tile(ap, off, nbytes, isz)

    def mark(self):
        return self.top

    def release(self, m):
        self.top = m


class PT:
    def __init__(self, t, k):
        self.t = t
        self.res = (("ps", k),)

    def v(self, parts, *shape, off=0, p0=0):
        n = 1
        for s in shape:
            n *= s
        ap = self.t[p0:p0 + parts, off:off + n]
        if len(shape) == 2:
            ap = ap.rearrange("p (a b) -> p a b", a=shape[0])
        elif len(shape) == 3:
            ap = ap.rearrange("p (a b c) -> p a b c", a=shape[0], b=shape[1])
        return V(ap, self.res)


def _res(*vs):
    r = ()
    for v in vs:
        if isinstance(v, V):
            r = r + tuple(v.res)
    return r


def _a(v):
    return v.ap if isinstance(v, V) else v


SMALL_N = 256


def _small(out):
    n = 1
    for s_ in out.ap.shape[1:]:
        n *= s_
    return n <= SMALL_N


class KB:
    def __init__(self, nc, stack):
        self.nc = nc
        self.P = Prog(nc)
        self.st = stack
        self.ar = Arena(nc, 206)
        self.banks = [PT(stack.enter_context(nc.psum_tensor("psb%d" % i, [128, 512], F32)), i) for i in range(8)]
        self.bi = 0
        self.ev = 0
        self.bset = None
        self.bcnt = {}

    def ps(self):
        if self.bset is None:
            b = self.banks[self.bi % 8]
            self.bi += 1
            return b
        name, lst = self.bset
        k = self.bcnt.get(name, 0)
        self.bcnt[name] = k + 1
        return self.banks[lst[k % len(lst)]]

    def ps_po(self):
        if self.bset is None:
            return self.ps()
        return self.banks[7]

    def mm(self, out, lhsT, rhs, start=True, stop=True):
        self.P.op("pe", lambda h: h.matmul(out.ap, lhsT=lhsT.ap, rhs=rhs.ap, start=start, stop=stop),
                  reads=_res(lhsT, rhs), writes=out.res)

    def tr(self, out, in_, ident):
        self.P.op("pe", lambda h: h.transpose(out.ap, in_.ap, ident.ap), reads=_res(in_, ident), writes=out.res)

    def act(self, out, in_, func, bias=None, scale=None, accum=None):
        kw = {}
        if bias is not None:
            kw["bias"] = _a(bias)
        if scale is not None:
            kw["scale"] = _a(scale)
        if accum is not None:
            kw["accum_out"] = accum.ap
        self.P.op("act", lambda h: h.activation(out=out.ap, in_=in_.ap, func=func, **kw),
                  reads=_res(in_, bias, scale), writes=_res(out, accum), small=(_small(out) or accum is not None))

    def ts(self, out, in0, s1, op0, s2=None, op1=None, accum=None, eng="dve"):
        kw = {}
        if op1 is not None:
            kw["op1"] = op1
        if accum is not None:
            kw["accum_out"] = accum.ap
        self.P.op(eng, lambda h: h.tensor_scalar(out=out.ap, in0=in0.ap, scalar1=_a(s1), scalar2=_a(s2), op0=op0, **kw),
                  reads=_res(in0, s1, s2), writes=_res(out, accum), small=(_small(out) or accum is not None))

    def tt(self, out, in0, in1, op, eng="dve"):
        self.P.op(eng, lambda h: h.tensor_tensor(out=out.ap, in0=in0.ap, in1=in1.ap, op=op),
                  reads=_res(in0, in1), writes=out.res, small=_small(out))

    def stt(self, out, in0, sc, in1, op0, op1):
        self.P.op("dve", lambda h: h.scalar_tensor_tensor(out=out.ap, in0=in0.ap, scalar=_a(sc), in1=in1.ap, op0=op0, op1=op1),
                  reads=_res(in0, sc, in1), writes=out.res, small=_small(out))

    def cp(self, out, in_, eng=None):
        if eng is None:
            self.ev += 1
            eng = "act" if self.ev % 2 else "dve"
        if eng == "act":
            self.P.op("act", lambda h: h.copy(out=out.ap, in_=in_.ap), reads=in_.res, writes=out.res, small=_small(out))
        else:
            self.P.op(eng, lambda h: h.tensor_copy(out=out.ap, in_=in_.ap), reads=in_.res, writes=out.res, small=_small(out))

    def memset(self, out, val, eng="dve"):
        self.P.op(eng, lambda h: h.memset(out.ap, val), writes=out.res, small=_small(out))

    def red(self, out, in_, op, eng="dve"):
        self.P.op(eng, lambda h: h.tensor_reduce(out=out.ap, in_=in_.ap, axis=AX.X, op=op), reads=in_.res, writes=out.res, small=_small(out))

    def recip(self, out, in_):
        self.P.op("dve", lambda h: h.reciprocal(out=out.ap, in_=in_.ap), reads=in_.res, writes=out.res, small=_small(out))

    def dma(self, out, in_, eng="sp", nc_ok=False):
        eng = "sp"

        def f(h):
            if nc_ok:
                with self.nc.allow_non_contiguous_dma(reason="small strided"):
                    return h.dma_start(out=out.ap, in_=in_.ap)
            return h.dma_start(out=out.ap, in_=in_.ap)
        self.P.dma(eng, f, reads=in_.res, writes=out.res)

    def rsqrt(self, out, in_, eps_tile, scale=1.0):
        self.act(out, in_, AF.Ln, bias=eps_tile, scale=scale)
        self.act(out, out, AF.Exp, scale=-0.5)


def _pt_vb(self, parts, *shape, off=0):
    n = 1
    for s in shape:
        n *= s
    ap = self.t[0:parts, :].bitcast(BF16)[:, off:off + n]
    if len(shape) == 2:
        ap = ap.rearrange("p (a b) -> p a b", a=shape[0])
    elif len(shape) == 3:
        ap = ap.rearrange("p (a b c) -> p a b c", a=shape[0], b=shape[1])
    return V(ap, self.res)


PT.vb = _pt_vb

C_ID, C_ONES, C_UTRI, C_J, C_BLK = 0, 128, 256, 384, 512
C_MU64, C_ML64, C_SL64 = 640, 1152, 1664
C_MU32, C_ML32, C_SL32 = 2176, 2432, 2688
C_SEL64, C_SEL32 = 2944, 3008
C_IR64, C_IR32 = 3072, 3584
C_SEL64W, C_SEL32W = 3840, 3968
C_POW2 = 4096
NCST = 4128
NR = 704


def _t5_bucket(rel):
    nb = 16
    max_exact = 8
    n = np.abs(rel)
    n_f = np.maximum(n, 1).astype(np.float32)
    large = max_exact + (np.log(n_f / max_exact) / math.log(128 / max_exact) * (nb - max_exact)).astype(np.int32)
    large = np.minimum(large, nb - 1)
    return np.where(rel > 0, nb, 0) + np.where(n < max_exact, n, large)


def make_consts():
    c = np.zeros((128, NCST), np.float32)
    i = np.arange(128)
    c[:, C_ID:C_ID + 128] = np.eye(128)
    c[:, C_ONES:C_ONES + 128] = 1.0
    c[:, C_UTRI:C_UTRI + 128] = (i[:, None] <= i[None, :])
    c[:, C_J:C_J + 128] = np.eye(128)[::-1]
    c[:, C_BLK:C_BLK + 128] = (i[:, None] // 64 == i[None, :] // 64)
    for (cc, o_mu, o_ml, o_sl, o_ir) in ((64, C_MU64, C_ML64, C_SL64, C_IR64), (32, C_MU32, C_ML32, C_SL32, C_IR32)):
        a = np.arange(cc)
        mu = np.where(a[None, :] > a[:, None], BIG, 0.0)
        ml = np.where(a[None, :] < a[:, None], -BIG, 0.0)
        sl = (a[None, :] < a[:, None]).astype(np.float32)
        ir = np.eye(cc)
        c[:cc, o_mu:o_mu + 8 * cc] = np.tile(mu, (1, 8))
        c[:cc, o_ml:o_ml + 8 * cc] = np.tile(ml, (1, 8))
        c[:cc, o_sl:o_sl + 8 * cc] = np.tile(sl, (1, 8))
        c[:cc, o_ir:o_ir + 8 * cc] = np.tile(ir, (1, 8))
    c[63, C_SEL64:C_SEL64 + 64] = 1.0
    c[31, C_SEL32:C_SEL32 + 64] = 1.0
    c[63, C_SEL64W:C_SEL64W + 128] = 1.0
    c[:, C_POW2:C_POW2 + 32] = (0.5 ** (np.arange(32) + 1))[None, :]
    c[31, C_SEL32W:C_SEL32W + 128] = 1.0
    rel = 127 - np.arange(NR)
    oh_t5 = np.zeros((32, NR), np.float32)
    oh_t5[_t5_bucket(rel), np.arange(NR)] = 1.0
    oh_rel = np.zeros((384, NR), np.float32)
    oh_rel[np.clip(rel, -128, 128) + 128, np.arange(NR)] = 1.0
    return c, oh_t5, oh_rel


class Blk:
    def __init__(self, t0, TB, C, sample, idx):
        self.t0 = t0
        self.TB = TB
        self.C = C
        self.sample = sample
        self.idx = idx
        self.NU = TB // C


BLOCKS = [Blk(256 * b, 256, 64, False, b) for b in range(8)] + [Blk(2048, 128, 32, True, 8)]


class MK(KB):
    def __init__(self, nc, stack, n_layers=DEPTH):
        super().__init__(nc, stack)
        self.n_layers = n_layers
        self.d = {}
        self.weng = 0

    def din(self, name, shape):
        t = self.nc.dram_tensor(name, list(shape), F32, kind="ExternalInput")
        self.d[name] = t
        return t

    def dout(self, name, shape):
        t = self.nc.dram_tensor(name, list(shape), F32, kind="ExternalOutput")
        self.d[name] = t
        return t

    def decl(self):
        i, o = self.din, self.dout
        i("xp", (SEQ, D)); i("xs", (NS * TS, D))
        i("ca_k", (2, NS, PAST, 512)); i("ca_v", (2, NS, PAST, 512)); i("ca_ki", (2, NS, PAST, 64))
        i("sb_s", (2, NS, 8, 64, 64)); i("sb_conv", (2, NS, 3, 1536))
        i("sc_c", (2, NS, 8, 32, 64)); i("sc_n", (2, NS, 8, 32)); i("sc_m", (2, NS, 8))
        i("cd_k", (2, NS, 512, 512)); i("cd_v", (2, NS, 512, 512))
        i("sf_conv", (4, NS, 2, 2 * DFF))
        i("w_in_even", (2, D, E_TOT)); i("w_out_even", (2, D, D)); i("t5_bias", (32, 8))
        i("b_conv_w", (2, 4, 1536)); i("b_a_log", (2, 8)); i("b_dt_bias", (2, 8)); i("b_norm_w", (2, 64))
        i("w_in_odd", (2, D, O_TOT)); i("w_out_odd", (2, D, D))
        i("c_i_bias", (2, 8)); i("c_f_bias", (2, 8)); i("c_norm_w", (2, 64)); i("d_rel_bias", (2, 257, 8))
        i("ln_mix_g", (4, D)); i("ln_mix_b", (4, D)); i("ln_ffn_g", (4, D)); i("ln_ffn_b", (4, D))
        i("ffn_w_up", (4, D, 2 * DFF)); i("ffn_conv_w", (4, 3, 2 * DFF)); i("ffn_w_down", (4, DFF, D))
        i("cst", (128, NCST)); i("oh_t5", (32, NR)); i("oh_rel", (384, NR))
        o("y_p", (SEQ, D)); o("y_s", (NS * TS, D))
        o("a_k_p", (2, SEQ, 512)); o("a_k_s", (2, NS * TS, 512)); o("a_v_p", (2, SEQ, 512)); o("a_v_s", (2, NS * TS, 512))
        o("a_ki_p", (2, SEQ, 64)); o("a_ki_s", (2, NS * TS, 64))
        o("b_s_p", (2, 8, 64, 64)); o("b_s_s", (2, NS, 8, 64, 64)); o("b_conv_p", (2, 3, 1536)); o("b_conv_s", (2, NS, 3, 1536))
        o("c_c_p", (2, 8, 32, 64)); o("c_c_s", (2, NS, 8, 32, 64)); o("c_n_p", (2, 8, 32)); o("c_n_s", (2, NS, 8, 32))
        o("c_m_p", (2, 8)); o("c_m_s", (2, NS, 8))
        o("d_k_p", (2, 512, 512)); o("d_k_s", (2, NS * TS, 512)); o("d_v_p", (2, 512, 512)); o("d_v_s", (2, NS * TS, 512))
        o("f_conv_p", (4, 2, 2 * DFF)); o("f_conv_s", (4, NS, 2, 2 * DFF))
        import os
        self.dbg = bool(os.environ.get("MK_DBG"))
        if self.dbg:
            o("dbg_o", (4, 128, 8, T)); o("dbg_xm", (4, 128, 8, T)); o("dbg_xo", (4, 128, 8, T))
        self.scr_t5 = self.nc.dram_tensor("scr_t5", [8, NR], F32, kind="Internal")
        self.scr_rel = self.nc.dram_tensor("scr_rel", [2, 8, NR], F32, kind="Internal")

    def D_(self, name, *res):
        return V(self.d[name].ap(), tuple(res))

    def dv(self, ap, *res):
        return V(ap, tuple(res))

    def parallel(self, fA, fB):
        ar = self.ar
        base = ar.top
        main = self.P.ops
        self.P.ops = []
        ar.peak = ar.top
        fA()
        la = self.P.ops
        assert ar.top == base
        ar.top = ar.peak
        mid_ = ar.top
        self.P.ops = []
        fB()
        lb = self.P.ops
        assert ar.top == mid_
        ar.top = base
        merged = []
        ia = ib = 0
        na, nb = len(la), len(lb)
        while ia < na or ib < nb:
            if ib >= nb or (ia < na and ia * nb <= ib * na):
                merged.append(la[ia]); ia += 1
            else:
                merged.append(lb[ib]); ib += 1
        self.P.ops = main + merged

    def xres(self, ch0, nch, c0, c1):
        r = ()
        for ch in range(ch0, ch0 + nch):
            r = r + self.xT.part(ch).cols(c0, c1).res
        return r

    def xv(self, ch0, nch, c0, c1):
        return V(self.xT.ap[:, ch0:ch0 + nch, c0:c1], self.xres(ch0, nch, c0, c1))

    def cv(self, c0, n, parts=128):
        return V(self.cst.ap[0:parts, c0:c0 + n], self.cst.res)

    def cv3(self, c0, parts, a, b):
        return V(self.cst.ap[0:parts, c0:c0 + a * b].rearrange("p (a b) -> p a b", a=a), self.cst.res)

    def load_w(self, dram_ap, nrows, ncols, dst, pad_heads=False):
        k = nrows // 128
        st = self.wstage[self.weng % 2]
        sv = V(st.ap[:, 0:k * ncols].rearrange("p (k n) -> p k n", k=k), st.res)
        self.dma(sv, self.dv(dram_ap.rearrange("(k p) n -> p k n", p=128)))
        eng = ("pool", "act")[self.weng % 2]
        self.weng += 1
        if pad_heads:
            dstv = V(dst.ap.rearrange("p k (h e) -> p k h e", h=8)[:, :, :, 0:32], dst.res)
            srcv = V(sv.ap.rearrange("p k (h e) -> p k h e", h=8), sv.res)
            self.cp(dstv, srcv, "pool")
        else:
            self.cp(V(dst.ap[:, :, 0:ncols], dst.res), sv, eng)

    def proj_fm(self, wt, c0, ncol, xb, n0, n1):
        ps = self.ps()
        o = ps.v(ncol, n1 - n0)
        for ch in range(8):
            self.mm(o, wt.part(ch)[:, c0:c0 + ncol], xb.part(ch)[:, n0:n1], start=(ch == 0), stop=(ch == 7))
        return ps

    def proj_tm(self, wt, c0, ncol, xb, n0, n1):
        ps = self.ps()
        o = ps.v(n1 - n0, ncol)
        for ch in range(8):
            self.mm(o, xb.part(ch)[:, n0:n1], wt.part(ch)[:, c0:c0 + ncol], start=(ch == 0), stop=(ch == 7))
        return ps

    def bcast_row(self, dst, dram_row_ap, parts):
        self.dma(dst, self.dv(dram_row_ap.partition_broadcast(parts)))

    def setup(self):
        ar = self.ar
        self.xT = ar.alloc([128, 8, T])
        self.cst = ar.alloc([128, NCST])
        self.dma(self.cst.all(), self.D_("cst"))
        self.idb = ar.alloc([128, 128], BF16)
        self.onesb = ar.alloc([128, 128], BF16)
        self.blkb = ar.alloc([128, 128], BF16)
        self.cp(self.idb.all(), self.cv(C_ID, 128), "dve")
        self.cp(self.onesb.all(), self.cv(C_ONES, 128), "dve")
        self.cp(self.blkb.all(), self.cv(C_BLK, 128), "dve")
        self.idf = self.cv(C_ID, 128)
        self.onesf = self.cv(C_ONES, 128)
        self.eps6 = ar.alloc([128, 1]); self.memset(self.eps6.all(), 1e-6, "dve")
        self.eps5 = ar.alloc([128, 1]); self.memset(self.eps5.all(), 1e-5, "dve")
        self.one1 = ar.alloc([128, 1]); self.memset(self.one1.all(), 1.0, "dve")
        self.t5t = ar.alloc([128, 8, 5, 64], BF16)
        self.bandt = ar.alloc([128, 8, 9, 64], BF16)
        self.KT = ar.alloc([128, 4, SEQ], BF16)
        self.Vbuf = ar.alloc([128, 16 * 528], BF16)
        self.kiT = ar.alloc([128, SEQ], BF16)
        self.wstage = [ar.alloc([128, 2048]), ar.alloc([128, 2048])]
        self.lnp = ar.alloc([128, 4, 8])
        m0 = ar.mark()
        xin = [ar.alloc([128, D]), ar.alloc([128, D])]
        for tt in range(17):
            src = self.d["xp"].ap()[tt * 128:(tt + 1) * 128, :] if tt < 16 else self.d["xs"].ap()
            xi = xin[tt % 2]
            self.dma(xi.all(), self.dv(src))
            for g in range(2):
                ps = self.ps()
                for k in range(4):
                    ch = g * 4 + k
                    self.tr(ps.v(128, 128, off=k * 128), xi[:, ch * 128:(ch + 1) * 128], self.idf)
                self.cp(self.xv(g * 4, 4, tt * 128, (tt + 1) * 128), ps.v(128, 4, 128))
        tab = ar.alloc([32, 8])
        oh = ar.alloc([32, NR])
        wr = ar.alloc([8, NR])
        self.dma(tab.all(), self.D_("t5_bias"))
        self.dma(oh.all(), self.D_("oh_t5"))
        for n0 in (0, 352):
            ps = self.ps()
            self.mm(ps.v(8, 352), tab.all(), oh[:, n0:n0 + 352])
            self.cp(wr[:, n0:n0 + 352], ps.v(8, 352))
        self.dma(V(self.scr_t5.ap(), ("scr_t5",)), wr.all())
        hk = ar.alloc([128, 8, 64])
        for di in range(5):
            src = bass.AP(self.scr_t5, 64 * di, [[1, 128], [NR, 8], [1, 64]])
            self.dma(hk.all(), V(src, ("scr_t5",)))
            ps = self.ps()
            self.mm(ps.v(128, 512), self.cv(C_J, 128), V(hk.ap.rearrange("p a b -> p (a b)"), hk.res))
            self.cp(V(self.t5t.ap[:, :, di, :], self.t5t.res), ps.v(128, 8, 64))
        ar.release(m0)

    def band_tiles(self, o):
        ar = self.ar
        m0 = ar.mark()
        tab = ar.alloc([128, 3, 8])
        self.memset(tab.all(), 0.0, "dve")
        src = self.d["d_rel_bias"].ap()[o]
        self.dma(V(tab.ap[:, 0:2, :], tab.res), self.dv(src[0:256, :].rearrange("(k p) h -> p k h", p=128)))
        self.dma(V(tab.ap[0:1, 2, :], tab.res), self.dv(src[256:257, :]))
        wr = ar.alloc([8, NR])
        oh = ar.alloc([128, 3, 352])
        for n0 in (0, 352):
            self.dma(oh.all(), self.dv(self.d["oh_rel"].ap()[:, n0:n0 + 352].rearrange("(k p) n -> p k n", p=128)))
            ps = self.ps()
            for k in range(3):
                self.mm(ps.v(8, 352), tab.part(k).all(), oh.part(k).all(), start=(k == 0), stop=(k == 2))
            self.cp(wr[:, n0:n0 + 352], ps.v(8, 352))
        key = "scr_rel%d" % o
        self.dma(V(self.scr_rel.ap()[o], (key,)), wr.all())
        hk = ar.alloc([128, 8, 64])
        for k in range(9):
            src = bass.AP(self.scr_rel, o * 8 * NR + 64 * k, [[1, 128], [NR, 8], [1, 64]])
            self.dma(hk.all(), V(src, (key,)))
            ps = self.ps()
            self.mm(ps.v(128, 512), self.cv(C_J, 128), V(hk.ap.rearrange("p a b -> p (a b)"), hk.res))
            self.cp(V(self.bandt.ap[:, :, k, :], self.bandt.res), ps.v(128, 8, 64))
        ar.release(m0)

    def load_rows_T(self, dst, src2d, nrows, ncol_tiles):
        ar = self.ar
        m1 = ar.mark()
        nr2 = nrows + (nrows % 2)
        hrow = ar.alloc([nr2, ncol_tiles * 128])
        if nr2 != nrows:
            self.memset(hrow.all(), 0.0)
        self.dma(hrow[0:nrows, :], self.dv(src2d))
        ps = self.ps()
        for f in range(ncol_tiles):
            self.tr(ps.v(128, nr2, off=f * nr2), hrow[:, f * 128:(f + 1) * 128], self.cv(C_ID, nr2, parts=nr2))
        self.cp(dst, V(ps.v(128, ncol_tiles, nr2).ap[:, :, 0:nrows], ps.res))
        ar.release(m1)

    def load_ln(self, L):
        ar = self.ar
        m1 = ar.mark()
        ln4 = ar.alloc([32, 128])
        for k, nm in enumerate(("ln_mix_g", "ln_mix_b", "ln_ffn_g", "ln_ffn_b")):
            self.dma(ln4[8 * k:8 * k + 8, :], self.dv(self.d[nm].ap()[L].rearrange("(c p) -> c p", p=128)))
        ps = self.ps()
        self.tr(ps.v(128, 32), ln4.all(), self.cv(C_ID, 32, parts=32))
        self.cp(self.lnp.all(), ps.v(128, 4, 8))
        ar.release(m1)

    def layer_norm(self, c0, n, gk, bk):
        ar = self.ar
        m0 = ar.mark()
        sq = [ar.alloc([128, n]), ar.alloc([128, n])]
        p1 = self.ps()
        p2 = self.ps()
        for ch in range(8):
            xc = self.xv(ch, 1, c0, c0 + n)
            xc = V(self.xT.ap[:, ch, c0:c0 + n], xc.res)
            s = sq[ch % 2]
            self.act(s.all(), xc, AF.Square)
            self.mm(p1.v(128, n), self.onesf, xc, start=(ch == 0), stop=(ch == 7))
            self.mm(p2.v(128, n), self.onesf, s.all(), start=(ch == 0), stop=(ch == 7))
        mean = ar.alloc([128, n])
        rstd = ar.alloc([128, n])
        self.ts(mean.all(), p1.v(128, n), 1.0 / D, ALU.mult)
        self.tt(rstd.all(), mean.all(), mean.all(), ALU.mult)
        self.stt(rstd.all(), p2.v(128, n), 1.0 / D, rstd.all(), ALU.mult, ALU.subtract)
        self.rsqrt(rstd.all(), rstd.all(), self.eps5[:, 0:1])
        for ch in range(8):
            xc = V(self.xT.ap[:, ch, c0:c0 + n], self.xres(ch, 1, c0, c0 + n))
            self.tt(xc, xc, mean.all(), ALU.subtract)
            self.tt(xc, xc, rstd.all(), ALU.mult, eng=("pool" if ch % 2 else "dve"))
            self.act(xc, xc, AF.Identity, bias=V(self.lnp.ap[:, bk, ch:ch + 1], self.lnp.res),
                     scale=V(self.lnp.ap[:, gk, ch:ch + 1], self.lnp.res))
        ar.release(m0)

    def tile_at(self, off, shape, dt):
        isz = 4 if dt == F32 else 2
        n = 1
        for s in shape[1:]:
            n *= s
        nbytes = n * isz
        ap = self.ar.t[0:shape[0], off // 4: off // 4 + (nbytes + 3) // 4]
        if dt != F32:
            ap = ap.bitcast(dt)[:, 0:n]
        if len(shape) == 3:
            ap = ap.rearrange("p (a b) -> p a b", a=shape[1])
        return Tile(ap, off, nbytes, isz)

    def ffn(self, L):
        ar = self.ar
        m0 = ar.mark()
        SBS = [(512 * i, 512 * (i + 1)) for i in range(4)] + [(SEQ, T)]
        xb = self.tile_at(self.KT.off, [128, 8, T], BF16)
        for ch in range(8):
            for (n0, n1) in SBS:
                self.cp(xb.part(ch)[:, n0:n1], V(self.xT.ap[:, ch, n0:n1], self.xres(ch, 1, n0, n1)))
        cwf = ar.alloc([128, 44, 3])
        for q in range(4):
            self.load_rows_T(V(cwf.ap[:, 11 * q:11 * q + 11, :], cwf.res), self.d["ffn_conv_w"].ap()[L][:, 1408 * q:1408 * q + 1408], 3, 11)
        fhist = ar.alloc([128, 44, 8])
        for q in range(4):
            m1 = ar.mark()
            hrow = ar.alloc([8, 1408])
            self.dma(hrow.all(), self.dv(self.d["sf_conv"].ap()[L].rearrange("s j n -> (s j) n")[:, q * 1408:(q + 1) * 1408]))
            ps = self.ps()
            for f in range(11):
                self.tr(ps.v(128, 8, off=f * 8), hrow[:, f * 128:(f + 1) * 128], self.cv(C_ID, 8, parts=8))
            self.cp(V(fhist.ap[:, q * 11:(q + 1) * 11, :], fhist.res), ps.v(128, 11, 8))
            ar.release(m1)
        xsel = ar.alloc([128, 8, 10], BF16)
        self.cp(V(xsel.ap[:, :, 0:2], xsel.res), V(xb.ap[:, :, SEQ - 2:SEQ], xb.res), "dve")
        for s in range(NS):
            c = SEQ + TS * s + TS - 2
            self.cp(V(xsel.ap[:, :, 2 + 2 * s:4 + 2 * s], xsel.res), V(xb.ap[:, :, c:c + 2], xb.res), "dve")
        wg = [ar.alloc([128, 8, 256], BF16) for _ in range(2)]
        wu = [ar.alloc([128, 8, 256], BF16) for _ in range(2)]
        wd = [ar.alloc([128, 2, D], BF16) for _ in range(2)]
        actT = [ar.alloc([128, 2, T], BF16) for _ in range(1)]
        hb = [ar.alloc([128, 516]) for _ in range(3)]
        yb = [ar.alloc([128, 512]) for _ in range(3)]
        carry = ar.alloc([128, 2, 2])
        fo = [ar.alloc([10, 512]) for _ in range(2)]
        wup = self.d["ffn_w_up"].ap()[L]
        wdn = self.d["ffn_w_down"].ap()[L]
        hi = 0
        for g in range(11):
            wgt, wut, wdt, at = wg[g % 2], wu[g % 2], wd[g % 2], actT[0]
            self.load_w(wup[:, 256 * g:256 * g + 256], D, 256, wgt)
            self.load_w(wup[:, DFF + 256 * g:DFF + 256 * g + 256], D, 256, wut)
            self.load_w(wdn[256 * g:256 * g + 256, :], 256, D, wdt)
            fot = fo[g % 2]
            for k, w_ in enumerate((wgt, wut)):
                ps = self.ps()
                for ch in range(8):
                    self.mm(ps.v(10, 256), xsel.part(ch).all(), w_.part(ch).all(), start=(ch == 0), stop=(ch == 7))
                self.cp(fot[:, 256 * k:256 * k + 256], ps.v(10, 256))
                col = (0 if k == 0 else DFF) + 256 * g
                self.dma(self.dv(self.d["f_conv_p"].ap()[L][:, col:col + 256]), fot[0:2, 256 * k:256 * k + 256], eng="pool")
                self.dma(self.dv(self.d["f_conv_s"].ap()[L].rearrange("s j n -> (s j) n")[:, col:col + 256]),
                         fot[2:10, 256 * k:256 * k + 256], eng="pool")
            for jj in range(2):
                for k, w_ in enumerate((wgt, wut)):
                    f = (0 if k == 0 else 22) + 2 * g + jj
                    self.memset(V(carry.ap[:, k, :], carry.res), 0.0, "dve")
                for (n0, n1) in SBS:
                    n = n1 - n0
                    ys = []
                    for k, w_ in enumerate((wgt, wut)):
                        f = (0 if k == 0 else 22) + 2 * g + jj
                        ps = self.proj_fm(w_, jj * 128, 128, xb, n0, n1)
                        h = hb[hi % 3]
                        y = yb[hi % 3]
                        hi += 1
                        cw = lambda j: V(cwf.ap[:, f, j:j + 1], cwf.res)
                        if n0 < SEQ:
                            self.cp(h[:, 2:2 + n], ps.v(128, n), "act")
                            self.cp(h[:, 0:2], V(carry.ap[:, k, :], carry.res), "pool")
                            self.cp(V(carry.ap[:, k, :], carry.res), h[:, n:n + 2], "pool")
                            hv = lambda j: h[:, j:j + n]
                            yv = y[:, 0:n]
                        else:
                            h3 = V(h.ap[:, 0:4 * 34].rearrange("p (s c) -> p s c", s=4), h.res)
                            self.cp(V(h3.ap[:, :, 2:34], h.res), ps.v(128, 4, 32), "act")
                            self.cp(V(h3.ap[:, :, 0:2], h.res),
                                    V(fhist.ap[:, f, :].rearrange("p (s j) -> p s j", s=4), fhist.res), "pool")
                            hv = lambda j: V(h3.ap[:, :, j:j + 32], h.res)
                            yv = V(y.ap[:, 0:128].rearrange("p (s c) -> p s c", s=4), y.res)
                        self.ts(yv, hv(0), cw(0), ALU.mult)
                        self.stt(yv, hv(1), cw(1), yv, ALU.mult, ALU.add)
                        self.stt(yv, hv(2), cw(2), yv, ALU.mult, ALU.add)
                        ys.append(y)
                    self.act(ys[0][:, 0:n], ys[0][:, 0:n], AF.Silu)
                    self.tt(at.part(jj)[:, n0:n1], ys[0][:, 0:n], ys[1][:, 0:n], ALU.mult, eng="pool")
            for dt_ in range(8):
                for (n0, n1) in SBS:
                    n = n1 - n0
                    ps = self.ps()
                    for jj in range(2):
                        self.mm(ps.v(128, n), wdt.part(jj)[:, dt_ * 128:(dt_ + 1) * 128], at.part(jj)[:, n0:n1],
                                start=(jj == 0), stop=(jj == 1))
                    xc = V(self.xT.ap[:, dt_, n0:n1], self.xres(dt_, 1, n0, n1))
                    if g == 0:
                        self.stt(xc, xc, ALPHA, ps.v(128, n), ALU.mult, ALU.add)
                    else:
                        self.tt(xc, xc, ps.v(128, n), ALU.add)
        ar.release(m0)
        for (n0, n1) in SBS:
            self.layer_norm(n0, n1 - n0, 2, 3)

    def out_norm_gate(self, ot, gate, nw, C, dst_oT, f0, u0):
        ar = self.ar
        m0 = ar.mark()
        o = V(ot.ap[0:C], ot.res)
        sq = ar.alloc([64, 8, 64], BF16)
        ss = ar.alloc([64, 8])
        self.tt(V(sq.ap[0:C], sq.res), o, o, ALU.mult, eng="pool")
        self.red(V(ss.ap[0:C], ss.res), V(sq.ap[0:C], sq.res), ALU.add)
        self.rsqrt(V(ss.ap[0:C], ss.res), V(ss.ap[0:C], ss.res), self.eps6[0:C, 0:1], scale=1.0 / 64)
        self.tt(o, o, V(ss.ap[0:C].unsqueeze(2).to_broadcast([C, 8, 64]), ss.res), ALU.mult)
        self.tt(o, o, V(nw.ap[0:C].unsqueeze(1).to_broadcast([C, 8, 64]), nw.res), ALU.mult, eng="pool")
        self.tt(o, o, gate, ALU.mult)
        self.to_oT(ot, C, dst_oT, f0, u0)
        ar.release(m0)

    def to_oT(self, y, C, dst_oT, f0, u0):
        ps = self.ps()
        y2 = V(y.ap[0:C].rearrange("p a b -> p (a b)") if len(y.ap.shape) == 3 else y.ap[0:C], y.res)
        for k in range(4):
            self.tr(ps.v(128, C, off=k * C), V(y2.ap[:, k * 128:(k + 1) * 128], y.res), self.cv(C_ID, C, parts=C))
        self.cp(V(dst_oT.ap[:, f0:f0 + 4, u0:u0 + C], dst_oT.res), ps.v(128, 4, C))

    def attend(self, qT, q0, nq, blocks, o_dst):
        ar = self.ar
        m0 = ar.mark()
        pts = [ar.alloc([128, 4, 64], BF16) for _ in range(3)]
        oacc = ar.alloc([64, 8, 65])
        nb = len(blocks)
        items = [(h, g0) for h in range(8) for g0 in range(0, nb, 4)]
        po_of = {}

        def emit_pv(p):
            h, g0, pt, grp = p
            ov = po_of[h].v(nq, 65)
            for bi, bl in enumerate(grp):
                nk = bl["nk"]
                gi = g0 + bi
                self.mm(ov, V(pt.ap[0:nk, bi, 0:nq], pt.res), V(bl["v"].ap[:, h, 0:65], bl["v"].res),
                        start=(gi == 0), stop=(gi == nb - 1))
            if g0 + 4 >= nb:
                self.cp(V(oacc.ap[0:nq, h, :], oacc.res), ov)

        pend = None
        for it, (h, g0) in enumerate(items):
            f, b0 = h // 2, (h % 2) * 64
            if g0 == 0:
                po_of[h] = self.ps_po()
            grp = blocks[g0:g0 + 4]
            ps = self.ps()
            for bi, bl in enumerate(grp):
                nk = bl["nk"]
                kt, kc0 = bl["kT"]
                o = ps.v(nk, nq, off=bi * 64)
                last = "q"
                if bl.get("mask") is not None:
                    last = "m"
                elif bl.get("bias") is not None:
                    last = "b"
                self.mm(o, V(kt.ap[:, f, kc0:kc0 + nk], kt.res), V(qT.ap[:, h, q0:q0 + nq], qT.res),
                        start=True, stop=(last == "q"))
                if bl.get("bias") is not None:
                    self.mm(o, self.idb[:, 0:nk], V(bl["bias"].ap[:, h, :], bl["bias"].res), start=False, stop=(last == "b"))
                if bl.get("mask") is not None:
                    self.mm(o, self.idb[:, 0:nk], bl["mask"], start=False, stop=True)
            pt = pts[it % 3]
            nks_ = set(bl["nk"] for bl in grp)
            if len(nks_) == 1 and len(grp) > 1:
                nk = grp[0]["nk"]
                ng = len(grp)
                self.act(V(pt.ap[0:nk, 0:ng, 0:nq], pt.res),
                         V(ps.t[0:nk, 0:ng * 64].rearrange("p (g c) -> p g c", g=ng)[:, :, 0:nq], ps.res), AF.Exp)
            else:
                for bi, bl in enumerate(grp):
                    nk = bl["nk"]
                    self.act(V(pt.ap[0:nk, bi, 0:nq], pt.res), ps.v(nk, nq, off=bi * 64), AF.Exp)
            if pend is not None:
                emit_pv(pend)
            pend = (h, g0, pt, grp)
        emit_pv(pend)
        rec = ar.alloc([64, 8])
        self.recip(V(rec.ap[0:nq], rec.res), V(oacc.ap[0:nq, :, 64], oacc.res))
        self.tt(o_dst, V(oacc.ap[0:nq, :, 0:64], oacc.res),
                V(rec.ap[0:nq].unsqueeze(2).to_broadcast([nq, 8, 64]), rec.res), ALU.mult)
        ar.release(m0)

    def dsa_mask(self, qiT, q0, nq, wi, ksegs, L, maskT, blocks_nk):
        ar = self.ar
        m0 = ar.mark()
        w0o, w1o = self.wstage[0].off, self.wstage[1].off
        sc = self.tile_at(w0o, [64, L], F32)
        tmp = [self.tile_at(w1o, [64, 512], F32), self.tile_at(w1o + 2048, [64, 512], F32)]
        ti = 0
        c0 = 0
        for (kt, kc0, n) in ksegs:
            for s0 in range(0, n, 512):
                sn = min(512, n - s0)
                for hn in range(8):
                    f, b0 = hn // 2, (hn % 2) * 64
                    ps = self.ps()
                    self.mm(ps.v(nq, sn), V(qiT.ap[b0:b0 + 64, f, q0:q0 + nq], qiT.res),
                            V(kt.ap[b0:b0 + 64, kc0 + s0:kc0 + s0 + sn], kt.res))
                    scv = V(sc.ap[0:nq, c0 + s0:c0 + s0 + sn], sc.res)
                    if hn == 0:
                        t = tmp[ti % 2]; ti += 1
                        self.act(V(t.ap[0:nq, 0:sn], t.res), ps.v(nq, sn), AF.Relu)
                        self.ts(scv, V(t.ap[0:nq, 0:sn], t.res), V(wi.ap[:, hn:hn + 1], wi.res), ALU.mult)
                    else:
                        t = tmp[ti % 2]; ti += 1
                        self.act(V(t.ap[0:nq, 0:sn], t.res), ps.v(nq, sn), AF.Relu)
                        self.stt(scv, V(t.ap[0:nq, 0:sn], t.res), V(wi.ap[:, hn:hn + 1], wi.res), scv, ALU.mult, ALU.add)
            c0 += n
        scv = V(sc.ap[0:nq, 0:L], sc.res)
        NIT = 20
        sm = ar.alloc([64, 8])
        wh = ar.alloc([64, 32])
        col = lambda k: V(sm.ap[0:nq, k:k + 1], sm.res)
        lo, w, mid, cnt, t2 = col(0), col(1), col(2), col(3), col(4)
        junk = self.tile_at(w1o + 4096, [64, L], BF16)
        jv = V(junk.ap[0:nq, 0:L], junk.res)
        self.red(col(5), scv, ALU.max)
        self.red(lo, scv, ALU.min)
        self.ts(lo, lo, -1.0, ALU.add)
        self.tt(w, col(5), lo, ALU.subtract)
        self.ts(V(wh.ap[0:nq, 0:NIT], wh.res), self.cv(C_POW2, NIT, parts=nq), w, ALU.mult)
        self.tt(mid, lo, V(wh.ap[0:nq, 0:1], wh.res), ALU.add)
        for it in range(NIT):
            self.ts(jv, scv, mid, ALU.is_gt, op1=ALU.add, accum=cnt)
            self.ts(t2, cnt, 255.5, ALU.is_gt, s2=0.5, op1=ALU.subtract)
            self.stt(mid, t2, V(wh.ap[0:nq, it:it + 1], wh.res), mid, ALU.mult, ALU.add)
        lo = mid
        self.ts(scv, scv, lo, ALU.is_le, s2=-BIG, op1=ALU.mult)
        k0 = 0
        bi = 0
        while bi < len(blocks_nk):
            ps = self.ps()
            grp = blocks_nk[bi:bi + 4]
            kk = k0
            for gi, nk in enumerate(grp):
                self.tr(ps.v(nk, nq, off=gi * 64), V(sc.ap[0:nq, kk:kk + nk], sc.res), self.cv(C_ID, nq, parts=nq))
                kk += nk
            for gi, nk in enumerate(grp):
                self.cp(V(maskT.ap[0:nk, bi + gi, 0:nq], maskT.res), ps.v(nk, nq, off=gi * 64))
            k0 = kk
            bi += 4
        ar.release(m0)

    def gdn_unit(self, C, u0, qnT, knT, vT, gates, zs, S32, Sb, par, oT):
        ar = self.ar
        m0 = ar.mark()
        MUo, SLo, IRo = (C_MU64, C_SL64, C_IR64) if C == 64 else (C_MU32, C_SL32, C_IR32)
        MU2 = self.cv(MUo, 8 * C, parts=C)
        SL = self.cv3(SLo, C, 8, C)
        IR = self.cv3(IRo, C, 8, C)
        idC = self.cv(C_ID, C, parts=C)
        onesC = self.cv(C_ONES, C, parts=C)
        negea, dtb, nwb = par

        def al(dt=F32):
            t = ar.alloc([64, 8, 64], dt)
            return t

        def v3(t, p=C, n=C):
            return V(t.ap[0:p, :, 0:n], t.res)

        def bc(v8, p=C, n=C):
            return V(v8.ap.unsqueeze(2).to_broadcast([p, 8, n]), v8.res)

        st = ar.alloc([64, 48])
        sc_ = lambda k, p=C: V(st.ap[0:p, 8 * k:8 * k + 8], st.res)
        g8, beta, gc, egc, eglgc = sc_(0), sc_(1), sc_(2), sc_(3), sc_(4)
        eglf = ar.alloc([128, 8])
        egl2 = ar.alloc([128, 4])
        bg_ = ar.alloc([64, 8])
        bege = V(bg_.ap[0:C], bg_.res)
        self.tt(g8, V(gates.ap[:, 0:8], gates.res), V(dtb.ap[0:C], dtb.res), ALU.add)
        self.act(g8, g8, AF.Exp)
        self.act(g8, g8, AF.Ln, bias=self.one1[0:C, 0:1])
        self.tt(g8, g8, V(negea.ap[0:C], negea.res), ALU.mult)
        self.act(beta, V(gates.ap[:, 8:16], gates.res), AF.Sigmoid)
        p1 = self.ps()
        self.mm(p1.v(C, 8), self.cv(C_UTRI, C, parts=C), g8)
        self.mm(p1.v(128, 8, off=64), self.cv(C_ONES, 128, parts=C), g8)
        self.cp(gc, p1.v(C, 8), "dve")
        self.act(egc, p1.v(C, 8), AF.Exp)
        self.tt(eglgc, p1.v(C, 8, off=64), gc, ALU.subtract)
        self.act(eglgc, eglgc, AF.Exp)
        self.act(eglf.all(), p1.v(128, 8, off=64), AF.Exp)
        for t_ in range(2):
            self.cp(V(egl2.ap[64 * t_:64 * t_ + 64, :], egl2.res),
                    V(eglf.ap[64 * t_:64 * t_ + 64, :].rearrange("p (f t) -> p f t", t=2)[:, :, t_], eglf.res), "dve")
        self.tt(bege, beta, egc, ALU.mult)
        pk = self.ps()
        pv = self.ps()
        for f in range(4):
            self.tr(pk.vb(C, 128, off=f * 128), V(knT.ap[:, f, u0:u0 + C], knT.res), self.idb.all())
            self.tr(pv.vb(C, 128, off=f * 128), V(vT.ap[:, f, u0:u0 + C], vT.res), self.idb.all())
        kbe, kdec, vb = al(BF16), al(BF16), al(BF16)
        self.tt(v3(kbe, C, 64), pk.vb(C, 8, 64), bc(bege, C, 64), ALU.mult)
        self.tt(v3(kdec, C, 64), pk.vb(C, 8, 64), bc(eglgc, C, 64), ALU.mult)
        self.tt(v3(vb, C, 64), pv.vb(C, 8, 64), bc(beta, C, 64), ALU.mult)
        pkk = (self.ps(), self.ps())
        pkq = (self.ps(), self.ps())
        for h in range(8):
            f, b0 = h // 2, (h % 2) * 64
            kh = V(knT.ap[b0:b0 + 64, f, u0:u0 + C], knT.res)
            qh = V(qnT.ap[b0:b0 + 64, f, u0:u0 + C], qnT.res)
            self.mm(pkk[h % 2].v(C, C, off=f * C), kh, kh)
            self.mm(pkq[h % 2].v(C, C, off=f * C), qh, kh)

        def par(v, p_, n):
            return V(v.ap.rearrange("p (f t) c -> p f t c", t=2)[:, :, p_, :], v.res)
        diag = al()
        self.tt(v3(diag), IR, bc(gc), ALU.mult)
        pG = self.ps()
        self.mm(pG.v(C, 8 * C), idC, MU2, start=True, stop=False)
        for h in range(8):
            self.mm(pG.v(C, C, off=h * C), onesC, V(diag.ap[0:C, h, 0:C], diag.res), start=False, stop=(h == 7))
        Dm = diag
        self.tt(v3(Dm), bc(gc), pG.v(C, 8, C), ALU.subtract)
        self.act(v3(Dm), v3(Dm), AF.Exp)
        A32, at32 = al(), al()
        for p_ in range(2):
            self.tt(par(v3(A32), p_, C), pkk[p_].v(C, 4, C), par(v3(Dm), p_, C), ALU.mult)
        self.tt(v3(A32), v3(A32), bc(beta), ALU.mult)
        self.tt(v3(A32), v3(A32), SL, ALU.mult, eng="pool")
        for p_ in range(2):
            self.tt(par(v3(at32), p_, C), pkq[p_].v(C, 4, C), par(v3(Dm), p_, C), ALU.mult)
        pB = self.ps()
        pT = self.ps()
        for h in range(8):
            self.tr(pB.v(C, C, off=h * C), V(A32.ap[0:C, h, 0:C], A32.res), idC)
            self.tr(pT.v(C, C, off=h * C), V(at32.ap[0:C, h, 0:C], at32.res), idC)
        Ab = [al(BF16), al(BF16)]
        Bb = [al(BF16), al(BF16)]
        Pb = [al(BF16), al(BF16)]
        atT = al(BF16)
        self.cp(v3(Ab[0]), v3(A32), "pool")
        self.cp(v3(Bb[0]), pB.v(C, 8, C), "act")
        self.cp(v3(atT), pT.v(C, 8, C), "act")
        self.tt(v3(Pb[0]), IR, pB.v(C, 8, C), ALU.subtract)
        M = 5 if C == 64 else 4
        cur = 0
        for m in range(1, M + 1):
            nxt = 1 - cur
            pA = self.ps()
            for h in range(8):
                self.mm(pA.v(C, C, off=h * C), V(Bb[cur].ap[0:C, h, 0:C], Bb[cur].res), V(Ab[cur].ap[0:C, h, 0:C], Ab[cur].res))
            if m < M:
                pBm = self.ps()
                for h in range(8):
                    self.mm(pBm.v(C, C, off=h * C), V(Ab[cur].ap[0:C, h, 0:C], Ab[cur].res), V(Bb[cur].ap[0:C, h, 0:C], Bb[cur].res))
            self.cp(v3(Ab[nxt]), pA.v(C, 8, C), "act")
            if m < M:
                self.cp(v3(Bb[nxt]), pBm.v(C, 8, C), "dve")
            pP = self.ps()
            for h in range(8):
                self.mm(pP.v(C, C, off=h * C), V(Ab[nxt].ap[0:C, h, 0:C], Ab[nxt].res), V(Pb[cur].ap[0:C, h, 0:C], Pb[cur].res))
            self.tt(v3(Pb[nxt]), v3(Pb[cur]), pP.v(C, 8, C), ALU.add)
            cur = nxt
        P_ = Pb[cur]
        pval = self.ps()
        pkc = self.ps()
        for h in range(8):
            Ph = V(P_.ap[0:C, h, 0:C], P_.res)
            self.mm(pval.v(C, 64, off=h * 64), Ph, V(vb.ap[0:C, h, :], vb.res))
            self.mm(pkc.v(64, C, off=(h // 2) * C, p0=(h % 2) * 64), V(kbe.ap[0:C, h, :], kbe.res), Ph)
        val32 = A32
        kcT = ar.alloc([128, 4, 64], BF16)
        self.cp(v3(val32, C, 64), pval.v(C, 8, 64), "act")
        self.cp(V(kcT.ap[:, :, 0:C], kcT.res), pkc.v(128, 4, C), "dve")
        pks = (self.ps(), self.ps())
        pqs = (self.ps(), self.ps())
        for h in range(8):
            f, b0 = h // 2, (h % 2) * 64
            Sh = V(Sb.ap[b0:b0 + 64, f, :], Sb.res)
            self.mm(pks[h % 2].v(C, 64, off=f * 64), V(kcT.ap[b0:b0 + 64, f, 0:C], kcT.res), Sh)
            self.mm(pqs[h % 2].v(C, 64, off=f * 64), V(qnT.ap[b0:b0 + 64, f, u0:u0 + C], qnT.res), Sh)
        vnew = al(BF16)
        o1 = at32
        for p_ in range(2):
            self.tt(par(v3(vnew, C, 64), p_, 64), par(v3(val32, C, 64), p_, 64), pks[p_].v(C, 4, 64), ALU.subtract)
            self.tt(par(v3(o1, C, 64), p_, 64), pqs[p_].v(C, 4, 64),
                    V(egc.ap.rearrange("p (f t) -> p f t", t=2)[:, :, p_].unsqueeze(2).to_broadcast([C, 4, 64]), egc.res), ALU.mult)
        pav = self.ps()
        pds = self.ps()
        for h in range(8):
            vh = V(vnew.ap[0:C, h, :], vnew.res)
            self.mm(pav.v(C, 64, off=h * 64), V(atT.ap[0:C, h, 0:C], atT.res), vh)
            self.mm(pds.v(64, 64, off=(h // 2) * 64, p0=(h % 2) * 64), V(kdec.ap[0:C, h, :], kdec.res), vh)
        self.tt(v3(o1, C, 64), v3(o1, C, 64), pav.v(C, 8, 64), ALU.add)
        self.tt(S32.all(), S32.all(), V(egl2.ap.unsqueeze(2).to_broadcast([128, 4, 64]), egl2.res), ALU.mult, eng="pool")
        self.tt(S32.all(), S32.all(), pds.v(128, 4, 64), ALU.add)
        self.cp(Sb.all(), S32.all(), "act")
        self.out_norm_gate(o1, zs, nwb, C, oT, 4, u0)
        ar.release(m0)

    def even_layer_setup(self, e):
        ar = self.ar
        self.cwb = ar.alloc([128, 12, 4])
        self.load_rows_T(self.cwb.all(), self.d["b_conv_w"].ap()[e], 4, 12)
        self.negea = ar.alloc([64, 8])
        self.dtb = ar.alloc([64, 8])
        self.nwb = ar.alloc([64, 64])
        self.bcast_row(self.negea.all(), self.d["b_a_log"].ap()[e], 64)
        self.bcast_row(self.dtb.all(), self.d["b_dt_bias"].ap()[e], 64)
        self.bcast_row(self.nwb.all(), self.d["b_norm_w"].ap()[e], 64)
        self.act(self.negea.all(), self.negea.all(), AF.Exp)
        self.ts(self.negea.all(), self.negea.all(), -1.0, ALU.mult)
        self.S32 = ar.alloc([128, 4, 64])
        self.Sb = ar.alloc([128, 4, 64], BF16)
        self.memset(self.S32.all(), 0.0, "dve")
        self.memset(self.Sb.all(), 0.0, "dve")
        self.ccarry = ar.alloc([128, 12, 3])
        self.memset(self.ccarry.all(), 0.0, "dve")
        self.shist = ar.alloc([128, 12, 12])
        m1 = ar.mark()
        hrow = ar.alloc([12, 1536])
        self.dma(hrow.all(), self.dv(self.d["sb_conv"].ap()[e].rearrange("s j n -> (s j) n")))
        ps = self.ps()
        for f in range(12):
            self.tr(ps.v(128, 12, off=f * 12), hrow[:, f * 128:(f + 1) * 128], self.cv(C_ID, 12, parts=12))
        self.cp(self.shist.all(), ps.v(128, 12, 12))
        ar.release(m1)
        self.Va = V(self.Vbuf.ap.rearrange("p (t h e) -> p t h e", t=16, h=8), self.Vbuf.res)
        self.memset(self.Vbuf.all(), 1.0, "dve")

    def even_block(self, L, e, blk):
        ar = self.ar
        t0, TB, C, NU, smp = blk.t0, blk.TB, blk.C, blk.NU, blk.sample
        W = self.d["w_in_even"].ap()[e]
        mB = ar.mark()
        oT = ar.alloc([128, 8, TB], BF16)
        qaT = ar.alloc([128, 8, TB], BF16)
        self.memset(qaT.all(), 0.0)
        qiT = ar.alloc([128, 4, TB], BF16)
        wi = ar.alloc([64, NU, 8])
        if smp:
            KTs = ar.alloc([128, 4, TB], BF16)
            kiTs = ar.alloc([128, TB], BF16)
            Vs = ar.alloc([32, NS, 8, 66], BF16)
            self.memset(Vs.all(), 1.0, "dve")
        mG = ar.mark()
        qnT = ar.alloc([128, 4, TB], BF16)
        knT = ar.alloc([128, 4, TB], BF16)
        vT = ar.alloc([128, 4, TB], BF16)
        gates = ar.alloc([64, NU, 16])
        zs = ar.alloc([64, NU, 512])
        mP = ar.mark()
        xb = ar.alloc([128, 8, TB], BF16)
        for ch in range(8):
            self.cp(xb.part(ch).all(), V(self.xT.ap[:, ch, t0:t0 + TB], self.xres(ch, 1, t0, t0 + TB)))
        wts = [ar.alloc([128, 8, 256], BF16), ar.alloc([128, 8, 256], BF16)]
        wn = [0]

        def ldw(c0, n):
            wt = wts[wn[0] % 2]
            wn[0] += 1
            self.load_w(W[:, c0:c0 + n], D, n, wt)
            return wt
        ost = [ar.alloc([128, 256]), ar.alloc([128, 256])]
        osi = [0]
        for (c0, dst) in ((0, qaT), (1536, qiT)):
            for half in range(2):
                wt = ldw(c0 + 256 * half, 256)
                for ff in range(2):
                    ps = self.proj_fm(wt, ff * 128, 128, xb, 0, TB)
                    f_ = half * 2 + ff
                    if dst is qaT:
                        self.act(V(dst.ap[0:64, 2 * f_, :], dst.res), ps.v(64, TB), AF.Copy, scale=0.125)
                        self.act(V(dst.ap[64:128, 2 * f_ + 1, :], dst.res), ps.v(64, TB, p0=64), AF.Copy, scale=0.125)
                    else:
                        self.act(dst.part(f_).all(), ps.v(128, TB), AF.Copy, scale=0.125)
        for half in range(2):
            wt = ldw(512 + 256 * half, 256)
            for ff in range(2):
                f = half * 2 + ff
                ps = self.proj_fm(wt, ff * 128, 128, xb, 0, TB)
                if smp:
                    self.cp(KTs.part(f).all(), ps.v(128, TB))
                else:
                    self.cp(self.KT.part(f).cols(t0, t0 + TB).all(), ps.v(128, TB))
            for tt in range(TB // 128):
                ps = self.proj_tm(wt, 0, 256, xb, tt * 128, tt * 128 + 128)
                o_ = ost[osi[0] % 2]; osi[0] += 1
                self.cp(o_.all(), ps.v(128, 256))
                dst = self.d["a_k_s"].ap()[e] if smp else self.d["a_k_p"].ap()[e][t0 + tt * 128:t0 + tt * 128 + 128]
                self.dma(self.dv(dst[:, 256 * half:256 * half + 256]), o_.all(), eng="pool")
        for half in range(2):
            wt = ldw(1024 + 256 * half, 256)
            for tt in range(TB // 128):
                ps = self.proj_tm(wt, 0, 256, xb, tt * 128, tt * 128 + 128)
                o_ = ost[osi[0] % 2]; osi[0] += 1
                self.cp(o_.all(), ps.v(128, 256), "act")
                dst = self.d["a_v_s"].ap()[e] if smp else self.d["a_v_p"].ap()[e][t0 + tt * 128:t0 + tt * 128 + 128]
                self.dma(self.dv(dst[:, 256 * half:256 * half + 256]), o_.all(), eng="pool")
                if not smp:
                    tg = (t0 + tt * 128) // 128
                    self.cp(V(self.Va.ap[:, tg, 4 * half:4 * half + 4, 0:64], self.Vbuf.res), ps.v(128, 4, 64), "dve")
            if smp:
                for s in range(NS):
                    ps = self.proj_tm(wt, 0, 256, xb, s * TS, s * TS + TS)
                    self.cp(V(Vs.ap[:, s, 4 * half:4 * half + 4, 0:64], Vs.res), ps.v(TS, 4, 64))
        wt = wts[wn[0] % 2]; wn[0] += 1
        st_ = self.wstage[self.weng % 2]; self.weng += 1
        sv = V(st_.ap[:, 0:8 * 72].rearrange("p (k n) -> p k n", k=8), st_.res)
        self.dma(sv, self.dv(W[:, 2048:2120].rearrange("(k p) n -> p k n", p=128)))
        self.cp(V(wt.ap[:, :, 0:64], wt.res), V(sv.ap[:, :, 0:64], st_.res), "pool")
        self.cp(V(wt.ap[:, :, 64:128], wt.res), V(sv.ap[:, :, 0:64], st_.res), "pool")
        self.cp(V(wt.ap[:, :, 128:136], wt.res), V(sv.ap[:, :, 64:72], st_.res), "pool")
        ps = self.proj_fm(wt, 0, 128, xb, 0, TB)
        if smp:
            self.cp(kiTs.all(), ps.v(128, TB))
        else:
            self.cp(self.kiT.cols(t0, t0 + TB).all(), ps.v(128, TB))
        for tt in range(TB // 128):
            ps = self.proj_tm(wt, 0, 64, xb, tt * 128, tt * 128 + 128)
            o_ = ost[osi[0] % 2]; osi[0] += 1
            self.cp(o_[:, 0:64], ps.v(128, 64))
            dst = self.d["a_ki_s"].ap()[e] if smp else self.d["a_ki_p"].ap()[e][t0 + tt * 128:t0 + tt * 128 + 128]
            self.dma(self.dv(dst), o_[:, 0:64], eng="pool")
        for u in range(NU):
            ps = self.proj_tm(wt, 128, 8, xb, u * C, u * C + C)
            self.act(V(wi.ap[0:C, u, :], wi.res), ps.v(C, 8), AF.Copy, scale=8 ** -0.5)
        cb = [ar.alloc([128, 3 + 256 + 16]) for _ in range(2)]
        yb = [ar.alloc([128, 256]) for _ in range(2)]
        sqb = [ar.alloc([128, 256], BF16) for _ in range(2)]
        rtb = [ar.alloc([128, 256]) for _ in range(2)]
        ci = 0
        for g in range(6):
            wt = ldw(2120 + 256 * g, 256)
            for ff in range(2):
                f = g * 2 + ff
                ps = self.proj_fm(wt, ff * 128, 128, xb, 0, TB)
                c_ = cb[ci % 2]; y_ = yb[ci % 2]; s_ = sqb[ci % 2]; ci += 1
                cw = lambda j: V(self.cwb.ap[:, f, j:j + 1], self.cwb.res)
                if not smp:
                    self.cp(c_[:, 3:3 + TB], ps.v(128, TB), "act")
                    self.cp(c_[:, 0:3], V(self.ccarry.ap[:, f, :], self.ccarry.res), "pool")
                    self.cp(V(self.ccarry.ap[:, f, :], self.ccarry.res), c_[:, TB:TB + 3], "pool")
                    hv = lambda j: c_[:, j:j + TB]
                    yv = y_[:, 0:TB]
                else:
                    c3 = V(c_.ap[:, 0:NS * 35].rearrange("p (s c) -> p s c", s=NS), c_.res)
                    self.cp(V(c3.ap[:, :, 3:35], c_.res), ps.v(128, NS, TS), "act")
                    self.cp(V(c3.ap[:, :, 0:3], c_.res), V(self.shist.ap[:, f, :].rearrange("p (s j) -> p s j", s=NS), self.shist.res), "pool")
                    hv = lambda j: V(c3.ap[:, :, j:j + TS], c_.res)
                    yv = V(y_.ap[:, 0:TB].rearrange("p (s c) -> p s c", s=NS), y_.res)
                self.ts(yv, hv(0), cw(0), ALU.mult)
                for j in range(1, 4):
                    self.stt(yv, hv(j), cw(j), yv, ALU.mult, ALU.add)
                y2 = y_[:, 0:TB]
                if f >= 8:
                    self.act(vT.part(f - 8).all(), y2, AF.Silu)
                else:
                    self.act(y2, y2, AF.Silu)
                    self.act(s_[:, 0:TB], y2, AF.Square)
                    pq = self.ps()
                    self.mm(pq.v(128, TB), self.blkb.all(), s_[:, 0:TB])
                    rt = rtb[ci % 2]
                    self.rsqrt(rt[:, 0:TB], pq.v(128, TB), self.eps6[:, 0:1])
                    if f < 4:
                        self.stt(qnT.part(f).all(), y2, 0.125, rt[:, 0:TB], ALU.mult, ALU.mult)
                    else:
                        self.tt(knT.part(f - 4).all(), y2, rt[:, 0:TB], ALU.mult)
        if smp or blk.idx == 7:
            nsel = 16 if smp else 4
            xsel = ar.alloc([128, 8, 16], BF16)
            if smp:
                for s in range(NS):
                    self.cp(V(xsel.ap[:, :, 4 * s:4 * s + 4], xsel.res), V(xb.ap[:, :, s * TS + TS - 4:s * TS + TS], xb.res), "dve")
            else:
                self.cp(V(xsel.ap[:, :, 0:4], xsel.res), V(xb.ap[:, :, TB - 4:TB], xb.res), "dve")
            for g in range(6):
                wt = ldw(2120 + 256 * g, 256)
                ps = self.ps()
                for ch in range(8):
                    self.mm(ps.v(nsel, 256), V(xsel.ap[:, ch, 0:nsel], xsel.res), wt.part(ch).all(), start=(ch == 0), stop=(ch == 7))
                o_ = ost[osi[0] % 2]; osi[0] += 1
                self.cp(o_[0:nsel, :], ps.v(nsel, 256))
                if smp:
                    for s in range(NS):
                        self.dma(self.dv(self.d["b_conv_s"].ap()[e, s][:, 256 * g:256 * g + 256]), o_[4 * s + 1:4 * s + 4, :], eng="pool")
                else:
                    self.dma(self.dv(self.d["b_conv_p"].ap()[e][:, 256 * g:256 * g + 256]), o_[1:4, :], eng="pool")
        wt = wts[wn[0] % 2]; wn[0] += 1
        st_ = self.wstage[self.weng % 2]; self.weng += 1
        sv = V(st_.ap[:, 0:8 * 16].rearrange("p (k n) -> p k n", k=8), st_.res)
        self.dma(sv, self.dv(W[:, 3656:3672].rearrange("(k p) n -> p k n", p=128)))
        self.cp(V(wt.ap[:, :, 0:16], wt.res), sv, "pool")
        for u in range(NU):
            ps = self.proj_tm(wt, 0, 16, xb, u * C, u * C + C)
            self.cp(V(gates.ap[0:C, u, :], gates.res), ps.v(C, 16), "dve")
        for half in range(2):
            wt = ldw(3672 + 256 * half, 256)
            for u in range(NU):
                ps = self.proj_tm(wt, 0, 256, xb, u * C, u * C + C)
                self.act(V(zs.ap[0:C, u, 256 * half:256 * half + 256], zs.res), ps.v(C, 256), AF.Silu)
        ar.release(mP)
        if blk.idx == 6:
            self.mark_("b6pro%d" % L)
        par = (self.negea, self.dtb, self.nwb)

        def gdn_all():
            self.bset = ("A", [0, 1, 2, 3, 4])
            for u in range(NU):
                if smp:
                    for t_ in range(2):
                        self.dma(V(self.S32.ap[64 * t_:64 * t_ + 64], self.S32.res),
                                 self.dv(self.d["sb_s"].ap()[e, u].rearrange("(f t) k v -> t k f v", t=2)[t_]))
                    self.cp(self.Sb.all(), self.S32.all(), "act")
                self.gdn_unit(C, u * C, qnT, knT, vT, V(gates.ap[0:C, u, :], gates.res),
                              V(zs.ap[0:C, u, :].rearrange("p (h e) -> p h e", h=8), zs.res), self.S32, self.Sb, par, oT)
                if smp:
                    for t_ in range(2):
                        self.dma(self.dv(self.d["b_s_s"].ap()[e, u].rearrange("(f t) k v -> t k f v", t=2)[t_]),
                                 V(self.S32.ap[64 * t_:64 * t_ + 64], self.S32.res), eng="pool")
            if blk.idx == 7:
                for t_ in range(2):
                    self.dma(self.dv(self.d["b_s_p"].ap()[e].rearrange("(f t) k v -> t k f v", t=2)[t_]),
                             V(self.S32.ap[64 * t_:64 * t_ + 64], self.S32.res), eng="pool")

        def dsa_all():
            self.bset = ("B", [5, 6])
            for u in range(NU):
                if smp:
                    self.dsa_sample_unit(e, u, qaT, qiT, wi, KTs, kiTs, Vs, oT)
                else:
                    self.dsa_prompt_unit(t0 // 64 + u, u, qaT, qiT, wi, oT)

        if smp:
            gdn_all()
            ar.release(mG)
            dsa_all()
        else:
            self.parallel(gdn_all, dsa_all)
            ar.release(mG)
        self.bset = None
        if blk.idx == 6:
            self.mark_("b6units%d" % L)
        self.out_proj_ln(self.d["w_out_even"].ap()[e], oT, t0, TB)
        ar.release(mB)

    def dsa_prompt_unit(self, c, u, qaT, qiT, wi, oT):
        ar = self.ar
        m0 = ar.mark()
        L = 64 * (c + 1)
        nb = (L + 127) // 128
        nks = [128] * (nb - 1) + [L - 128 * (nb - 1)]
        q0 = u * 64
        maskT = None
        if L > 256:
            maskT = ar.alloc([128, nb, 64], BF16)
            self.memset(maskT.all(), 0.0)
            self.dsa_mask(qiT, q0, 64, V(wi.ap[0:64, u, :], wi.res), [(self.kiT, 0, L)], L, maskT, nks)
        blocks = []
        for j in range(nb):
            nk = nks[j]
            di = min(c - 2 * j, 4)
            blocks.append(dict(kT=(self.KT, 128 * j), nk=nk, v=V(self.Va.ap[0:nk, j], self.Vbuf.res),
                               bias=V(self.t5t.ap[:, :, di, 0:64], self.t5t.res),
                               mask=(V(maskT.ap[:, j, 0:64], maskT.res) if maskT is not None else None)))
        oa = ar.alloc([64, 8, 64])
        self.attend(qaT, q0, 64, blocks, V(oa.ap[0:64], oa.res))
        self.to_oT(oa, 64, oT, 0, q0)
        ar.release(m0)

    def load_cache_T(self, src2d, nkeys, dstT, stg):
        for j in range(nkeys // 128):
            st = stg[j % len(stg)]
            self.dma(st.all(), self.dv(src2d[128 * j:128 * j + 128, :]))
            ps = self.ps()
            for f in range(4):
                self.tr(ps.v(128, 128, off=f * 128), st[:, f * 128:(f + 1) * 128], self.idf)
            self.cp(V(dstT.ap[:, :, 128 * j:128 * j + 128], dstT.res), ps.v(128, 4, 128))

    def dsa_sample_unit(self, e, s, qaT, qiT, wi, KTs, kiTs, Vs, oT):
        ar = self.ar
        m0 = ar.mark()
        KTc = ar.alloc([128, 4, PAST], BF16)
        Vc = ar.alloc([128, 8, 8, 66], BF16)
        kiTc = ar.alloc([128, PAST], BF16)
        self.memset(Vc.all(), 1.0, "dve")
        m1 = ar.mark()
        stg = [ar.alloc([128, 512]) for _ in range(3)]
        kstg = ar.alloc([128, 8, 128])
        self.load_cache_T(self.d["ca_k"].ap()[e, s], PAST, KTc, stg)
        for j in range(8):
            st = stg[j % 3]
            self.dma(st.all(), self.dv(self.d["ca_v"].ap()[e, s][128 * j:128 * j + 128, :]))
            self.cp(V(Vc.ap[:, j, :, 0:64], Vc.res), V(st.ap.rearrange("p (h e) -> p h e", h=8), st.res))
        kis = self.d["ca_ki"].ap()[e, s].rearrange("(j p) d -> p j d", p=128)
        self.dma(V(kstg.ap[:, :, 0:64], kstg.res), self.dv(kis))
        self.dma(V(kstg.ap[:, :, 64:128], kstg.res), self.dv(kis))
        for g in range(2):
            ps = self.ps()
            for k in range(4):
                j = 4 * g + k
                self.tr(ps.v(128, 128, off=k * 128), V(kstg.ap[:, j, :], kstg.res), self.idf)
            self.cp(kiTc[:, 512 * g:512 * g + 512], ps.v(128, 512))
        ar.release(m1)
        L = PAST + TS
        nks = [128] * 8 + [TS]
        q0 = s * TS
        maskT = ar.alloc([128, 9, TS], BF16)
        self.memset(maskT.all(), 0.0)
        self.dsa_mask(qiT, q0, TS, V(wi.ap[0:TS, s, :], wi.res), [(kiTc, 0, PAST), (kiTs, q0, TS)], L, maskT, nks)
        blocks = []
        for j in range(8):
            di = min(16 - 2 * j, 4)
            blocks.append(dict(kT=(KTc, 128 * j), nk=128, v=V(Vc.ap[:, j], Vc.res),
                               bias=V(self.t5t.ap[:, :, di, 0:TS], self.t5t.res),
                               mask=V(maskT.ap[:, j, :], maskT.res)))
        blocks.append(dict(kT=(KTs, q0), nk=TS, v=V(Vs.ap[:, s], Vs.res),
                           bias=V(self.t5t.ap[:, :, 0, 0:TS], self.t5t.res),
                           mask=V(maskT.ap[:, 8, :], maskT.res)))
        oa = ar.alloc([64, 8, 64])
        self.attend(qaT, q0, TS, blocks, V(oa.ap[0:TS], oa.res))
        self.to_oT(oa, TS, oT, 0, q0)
        ar.release(m0)

    def out_proj_ln(self, Wout, oT, t0, TB):
        ar = self.ar
        m0 = ar.mark()
        if self.dbg:
            tmp = [ar.alloc([128, TB]), ar.alloc([128, TB])]
            for ch in range(8):
                self.cp(tmp[ch % 2].all(), oT.part(ch).all(), "dve")
                self.dma(self.dv(self.d["dbg_o"].ap()[self.curL][:, ch, t0:t0 + TB]), tmp[ch % 2].all(), eng="pool")
        wts = [ar.alloc([128, 8, 128], BF16) for _ in range(2)]
        for dt_ in range(8):
            wt = wts[dt_ % 2]
            self.load_w(Wout[:, 128 * dt_:128 * dt_ + 128], D, 128, wt)
            ps = self.ps()
            for ch in range(8):
                self.mm(ps.v(128, TB), wt.part(ch).all(), oT.part(ch).all(), start=(ch == 0), stop=(ch == 7))
            xc = V(self.xT.ap[:, dt_, t0:t0 + TB], self.xres(dt_, 1, t0, t0 + TB))
            self.stt(xc, xc, ALPHA, ps.v(128, TB), ALU.mult, ALU.add)
        ar.release(m0)
        self.layer_norm(t0, TB, 0, 1)

    def write_y(self):
        ar = self.ar
        m0 = ar.mark()
        yo = [ar.alloc([128, D]), ar.alloc([128, D])]
        for tt in range(17):
            y_ = yo[tt % 2]
            for g in range(2):
                ps = self.ps()
                for k in range(4):
                    ch = 4 * g + k
                    self.tr(ps.v(128, 128, off=k * 128),
                            V(self.xT.ap[:, ch, tt * 128:(tt + 1) * 128], self.xres(ch, 1, tt * 128, (tt + 1) * 128)), self.idf)
                self.cp(y_[:, 512 * g:512 * g + 512], ps.v(128, 512))
            dst = self.d["y_p"].ap()[tt * 128:(tt + 1) * 128, :] if tt < 16 else self.d["y_s"].ap()
            self.dma(self.dv(dst), y_.all(), eng="pool")
        ar.release(m0)

    def mark_(self, label):
        self.marks.append((label, len(self.P.ops)))

    def build(self):
        import os
        self.marks = []
        self.decl()
        self.setup()
        self.mark_("setup")
        ar = self.ar
        for L in range(self.n_layers):
            mL = ar.mark()
            self.curL = L
            self.load_ln(L)
            if L % 2 == 0:
                self.even_layer_setup(L // 2)
                self.mark_("evsetup%d" % L)
                for blk in BLOCKS:
                    self.even_block(L, L // 2, blk)
                    self.mark_("L%db%d" % (L, blk.idx))
            else:
                self.odd_layer_setup(L // 2)
                for blk in BLOCKS:
                    self.odd_block(L, L // 2, blk)
            ar.release(mL)
            self.mark_("mix%d" % L)
            if self.dbg:
                for ch in range(8):
                    self.dma(self.dv(self.d["dbg_xm"].ap()[L][:, ch, :]), V(self.xT.ap[:, ch, :], self.xT.part(ch).res), eng="pool")
            self.ffn(L)
            if self.dbg:
                for ch in range(8):
                    self.dma(self.dv(self.d["dbg_xo"].ap()[L][:, ch, :]), V(self.xT.ap[:, ch, :], self.xT.part(ch).res), eng="pool")
            self.mark_("ffn%d" % L)
        self.write_y()
        self.mark_("end")
        tr_ = os.environ.get("MK_TRUNC")
        if tr_:
            n = dict(self.marks)[tr_] if tr_ in dict(self.marks) else int(tr_)
            self.P.ops = self.P.ops[:n]
        print("marks", self.marks, flush=True)
        return self.P.emit(self.st)


_CONSTS = None


def _consts():
    global _CONSTS
    if _CONSTS is None:
        _CONSTS = make_consts()
    return _CONSTS


IN_SHARD = {
    "xp": ("x_prompt", lambda i, a: a[i]),
    "xs": ("x_sample", lambda i, a: a[4 * i:4 * i + 4].reshape(NS * TS, D)),
    "ca_k": ("cache_a_k", lambda i, a: a[:, 4 * i:4 * i + 4].reshape(2, NS, PAST, 512)),
    "ca_v": ("cache_a_v", lambda i, a: a[:, 4 * i:4 * i + 4].reshape(2, NS, PAST, 512)),
    "ca_ki": ("cache_a_kidx", lambda i, a: a[:, 4 * i:4 * i + 4]),
    "sb_s": ("state_b_s", lambda i, a: a[:, 4 * i:4 * i + 4]),
    "sb_conv": ("state_b_conv", lambda i, a: a[:, 4 * i:4 * i + 4]),
    "sc_c": ("state_c_c", lambda i, a: a[:, 4 * i:4 * i + 4]),
    "sc_n": ("state_c_n", lambda i, a: a[:, 4 * i:4 * i + 4]),
    "sc_m": ("state_c_m", lambda i, a: a[:, 4 * i:4 * i + 4]),
    "cd_k": ("cache_d_k", lambda i, a: a[:, 4 * i:4 * i + 4].reshape(2, NS, 512, 512)),
    "cd_v": ("cache_d_v", lambda i, a: a[:, 4 * i:4 * i + 4].reshape(2, NS, 512, 512)),
    "sf_conv": ("state_ffn_conv", lambda i, a: a[:, 4 * i:4 * i + 4]),
}
W_NAMES = ["w_in_even", "w_out_even", "t5_bias", "b_conv_w", "b_a_log", "b_dt_bias", "b_norm_w", "w_in_odd", "w_out_odd",
           "c_i_bias", "c_f_bias", "c_norm_w", "d_rel_bias", "ln_mix_g", "ln_mix_b", "ln_ffn_g", "ln_ffn_b",
           "ffn_w_up", "ffn_conv_w", "ffn_w_down"]

OUTS = [("y_p", (8, SEQ, D), "p", 0), ("y_s", (32, TS, D), "s", 0),
        ("a_k_p", (2, 8, SEQ, 8, 64), "p", 2), ("a_k_s", (2, 32, TS, 8, 64), "s", 2),
        ("a_v_p", (2, 8, SEQ, 8, 64), "p", 2), ("a_v_s", (2, 32, TS, 8, 64), "s", 2),
        ("a_ki_p", (2, 8, SEQ, 64), "p", 2), ("a_ki_s", (2, 32, TS, 64), "s", 2),
        ("b_s_p", (2, 8, 8, 64, 64), "p", 2), ("b_s_s", (2, 32, 8, 64, 64), "s", 2),
        ("b_conv_p", (2, 8, 3, 1536), "p", 2), ("b_conv_s", (2, 32, 3, 1536), "s", 2),
        ("c_c_p", (2, 8, 8, 32, 64), "p", 2), ("c_c_s", (2, 32, 8, 32, 64), "s", 2),
        ("c_n_p", (2, 8, 8, 32), "p", 2), ("c_n_s", (2, 32, 8, 32), "s", 2),
        ("c_m_p", (2, 8, 8), "p", 2), ("c_m_s", (2, 32, 8), "s", 2),
        ("d_k_p", (2, 8, 512, 8, 64), "p", 2), ("d_k_s", (2, 32, TS, 8, 64), "s", 2),
        ("d_v_p", (2, 8, 512, 8, 64), "p", 2), ("d_v_s", (2, 32, TS, 8, 64), "s", 2),
        ("f_conv_p", (4, 8, 2, 2 * DFF), "p", 4), ("f_conv_s", (4, 32, 2, 2 * DFF), "s", 4)]


def build_nc(n_layers=DEPTH):
    nc = bass.Bass("TRN2", target_bir_lowering=False)
    st = ExitStack()
    mk = MK(nc, st, n_layers)
    stats = mk.build()
    return nc, stats, st


def make_in_maps(inputs):
    cst, oh_t5, oh_rel = _consts()
    maps = []
    for i in range(8):
        m = {}
        for dn, (kn, fn) in IN_SHARD.items():
            m[dn] = np.ascontiguousarray(fn(i, inputs[kn]), dtype=np.float32)
        for w in W_NAMES:
            m[w] = np.ascontiguousarray(inputs[w], dtype=np.float32)
        m["cst"] = cst
        m["oh_t5"] = oh_t5
        m["oh_rel"] = oh_rel
        maps.append(m)
    return maps


def gather_outputs(results):
    outs = []
    for (dn, shape, grp, ld) in OUTS:
        full = np.zeros(shape, np.float32)
        for i in range(8):
            r = np.asarray(results[i][dn], dtype=np.float32)
            if ld == 0:
                if grp == "p":
                    full[i] = r.reshape(shape[1:])
                else:
                    full[4 * i:4 * i + 4] = r.reshape((4,) + shape[1:])
            else:
                if grp == "p":
                    full[:, i] = r.reshape((shape[0],) + shape[2:])
                else:
                    full[:, 4 * i:4 * i + 4] = r.reshape((shape[0], 4) + shape[2:])
        outs.append(full)
    return tuple(outs)


def kernel(**inputs):
    inputs = {k: np.asarray(v) for k, v in inputs.items()}
    nc, stats, st = build_nc()
    in_maps = make_in_maps(inputs)
    res = run_bass_kernel_spmd(nc, in_maps, core_ids=list(range(8)))
    return gather_outputs(res.results)


def _odd_layer_setup(self, o):
    ar = self.ar
    self.band_tiles(o)
    self.ibias = ar.alloc([64, 8])
    self.fbias = ar.alloc([64, 8])
    self.nwc = ar.alloc([64, 64])
    self.bcast_row(self.ibias.all(), self.d["c_i_bias"].ap()[o], 64)
    self.bcast_row(self.fbias.all(), self.d["c_f_bias"].ap()[o], 64)
    self.bcast_row(self.nwc.all(), self.d["c_norm_w"].ap()[o], 64)
    self.Ca32 = ar.alloc([128, 4, 66])
    self.Cab = ar.alloc([128, 4, 66], BF16)
    self.mprev = ar.alloc([64, 8])
    self.memset(self.Ca32.all(), 0.0)
    self.memset(self.Cab.all(), 0.0)
    self.memset(self.mprev.all(), 0.0)
    self.Vd = V(self.Vbuf.ap[0:64, 0:10 * 528].rearrange("p (t h e) -> p t h e", t=10, h=8), self.Vbuf.res)
    self.memset(self.Vbuf.all(), 1.0)


def _mlstm_unit(self, C, u0, qcT, kcT, vaug, igf, osig, oT):
    ar = self.ar
    m0 = ar.mark()
    MUo, MLo, IRo = (C_MU64, C_ML64, C_IR64) if C == 64 else (C_MU32, C_ML32, C_IR32)
    MU2 = self.cv(MUo, 8 * C, parts=C)
    ML2 = self.cv(MLo, 8 * C, parts=C)
    IR = self.cv3(IRo, C, 8, C)
    idC = self.cv(C_ID, C, parts=C)
    onesC = self.cv(C_ONES, C, parts=C)
    selW = self.cv(C_SEL64W if C == 64 else C_SEL32W, 128, parts=C)

    def al(dt=F32):
        return ar.alloc([64, 8, 64], dt)

    def v3(t, p=C, n=C):
        return V(t.ap[0:p, :, 0:n], t.res)

    def bc(v8, p=C, n=C):
        return V(v8.ap.unsqueeze(2).to_broadcast([p, 8, n]), v8.res)

    def par(v, p_):
        return V(v.ap.rearrange("p (f t) c -> p f t c", t=2)[:, :, p_, :], v.res)

    def par8(v8, p_, n):
        return V(v8.ap.rearrange("p (f t) -> p f t", t=2)[:, :, p_].unsqueeze(2).to_broadcast([C, 4, n]), v8.res)

    st = ar.alloc([64, 80])
    sc_ = lambda k: V(st.ap[0:C, 8 * k:8 * k + 8], st.res)
    ig, lf, fcum, r, linter, m, u, tmp, winter, emm = [sc_(k) for k in range(10)]
    self.tt(ig, V(igf.ap[:, 0:8], igf.res), V(self.ibias.ap[0:C], self.ibias.res), ALU.add)
    self.tt(lf, V(igf.ap[:, 8:16], igf.res), V(self.fbias.ap[0:C], self.fbias.res), ALU.add)
    self.act(lf, lf, AF.Exp, scale=-1.0)
    self.act(lf, lf, AF.Ln, bias=self.one1[0:C, 0:1])
    self.ts(lf, lf, -1.0, ALU.mult)
    p1 = self.ps()
    self.mm(p1.v(C, 8), self.cv(C_UTRI, C, parts=C), lf)
    self.cp(fcum, p1.v(C, 8), "dve")
    self.tt(r, fcum, ig, ALU.subtract)
    self.tt(linter, fcum, V(self.mprev.ap[0:C], self.mprev.res), ALU.add)
    diag = al()
    self.tt(v3(diag), IR, bc(r), ALU.mult)
    pR = self.ps()
    self.mm(pR.v(C, 8 * C), idC, MU2, start=True, stop=False)
    for h in range(8):
        self.mm(pR.v(C, C, off=h * C), onesC, V(diag.ap[0:C, h, 0:C], diag.res), start=False, stop=(h == 7))
    self.red(tmp, pR.v(C, 8, C), ALU.min)
    self.tt(tmp, fcum, tmp, ALU.subtract)
    self.tt(m, linter, tmp, ALU.max)
    self.tt(u, fcum, m, ALU.subtract)
    diag2 = al()
    self.tt(v3(diag2), IR, bc(u), ALU.mult)
    pU = self.ps()
    self.mm(pU.v(C, 8 * C), idC, ML2, start=True, stop=False)
    for h in range(8):
        self.mm(pU.v(C, C, off=h * C), onesC, V(diag2.ap[0:C, h, 0:C], diag2.res), start=False, stop=(h == 7))
    wT = diag
    self.tt(v3(wT), pU.v(C, 8, C), bc(r), ALU.subtract)
    self.act(v3(wT), v3(wT), AF.Exp)
    pqk = (self.ps(), self.ps())
    for h in range(8):
        f, b0 = h // 2, (h % 2) * 64
        self.mm(pqk[h % 2].v(C, C, off=f * C), V(kcT.ap[b0:b0 + 64, f, u0:u0 + C], kcT.res),
                V(qcT.ap[b0:b0 + 64, f, u0:u0 + C], qcT.res))
    qkT = al(BF16)
    for p_ in range(2):
        self.tt(par(v3(qkT), p_), pqk[p_].v(C, 4, C), par(v3(wT), p_), ALU.mult)
    pin = (self.ps(), self.ps())
    pit = (self.ps(), self.ps())
    for h in range(8):
        f, b0 = h // 2, (h % 2) * 64
        self.mm(pin[h // 4].v(C, 65, off=(h % 4) * 65), V(qkT.ap[0:C, h, 0:C], qkT.res), V(vaug.ap[:, h, 0:65], vaug.res))
        self.mm(pit[h % 2].v(C, 65, off=f * 65), V(qcT.ap[b0:b0 + 64, f, u0:u0 + C], qcT.res),
                V(self.Cab.ap[b0:b0 + 64, f, 0:65], self.Cab.res))
    self.tt(winter, linter, m, ALU.subtract)
    self.act(winter, winter, AF.Exp)
    self.act(emm, m, AF.Exp, scale=-1.0)
    tot = ar.alloc([64, 8, 65])
    tv = V(tot.ap[0:C], tot.res)
    for p_ in range(2):
        self.tt(par(tv, p_), pit[p_].v(C, 4, 65), par8(winter, p_, 65), ALU.mult)
    for hf in range(2):
        th = V(tot.ap[0:C, 4 * hf:4 * hf + 4, :], tot.res)
        self.tt(th, th, pin[hf].v(C, 4, 65), ALU.add)
    den = ar.alloc([64, 8])
    dv_ = V(den.ap[0:C], den.res)
    self.ts(dv_, V(tot.ap[0:C, :, 64], tot.res), -1.0, ALU.mult)
    self.tt(dv_, dv_, V(tot.ap[0:C, :, 64], tot.res), ALU.max)
    self.tt(dv_, dv_, emm, ALU.max)
    self.recip(dv_, dv_)
    hh = diag2
    self.tt(v3(hh, C, 64), V(tot.ap[0:C, :, 0:64], tot.res), bc(dv_, C, 64), ALU.mult)
    self.out_norm_gate(hh, osig, self.nwc, C, oT, 0, u0)
    stat = ar.alloc([64, 24])
    sv = lambda k: V(stat.ap[0:C, 8 * k:8 * k + 8], stat.res)
    self.cp(sv(0), m, "dve")
    self.cp(sv(1), u, "dve")
    self.tt(sv(2), linter, m, ALU.subtract)
    pl = self.ps()
    self.mm(pl.v(128, 24), selW, V(stat.ap[0:C, :], stat.res))
    lastb = ar.alloc([128, 24])
    self.cp(lastb.all(), pl.v(128, 24), "dve")
    self.cp(self.mprev.all(), V(lastb.ap[0:64, 0:8], lastb.res), "dve")
    wl = ar.alloc([64, 8])
    wlv = V(wl.ap[0:C], wl.res)
    self.tt(wlv, V(lastb.ap[0:C, 8:16], lastb.res), r, ALU.subtract)
    self.act(wlv, wlv, AF.Exp)
    dec = ar.alloc([128, 8])
    dec2 = ar.alloc([128, 4])
    self.act(dec.all(), V(lastb.ap[:, 16:24], lastb.res), AF.Exp)
    for t_ in range(2):
        self.cp(V(dec2.ap[64 * t_:64 * t_ + 64, :], dec2.res),
                V(dec.ap[64 * t_:64 * t_ + 64, :].rearrange("p (f t) -> p f t", t=2)[:, :, t_], dec.res), "dve")
    pk = self.ps()
    for f in range(4):
        self.tr(pk.vb(C, 128, off=f * 128), V(kcT.ap[:, f, u0:u0 + C], kcT.res), self.idb.all())
    kw = al(BF16)
    self.tt(v3(kw, C, 64), pk.vb(C, 8, 64), bc(wlv, C, 64), ALU.mult)
    pdc = self.ps()
    for h in range(8):
        self.mm(pdc.v(64, 65, off=(h // 2) * 65, p0=(h % 2) * 64), V(kw.ap[0:C, h, :], kw.res), V(vaug.ap[:, h, 0:65], vaug.res))
    cav = V(self.Ca32.ap[:, :, 0:65], self.Ca32.res)
    self.tt(cav, cav, V(dec2.ap.unsqueeze(2).to_broadcast([128, 4, 65]), dec2.res), ALU.mult, eng="pool")
    self.tt(cav, cav, pdc.v(128, 4, 65), ALU.add)
    self.cp(V(self.Cab.ap[:, :, 0:65], self.Cab.res), cav, "act")
    ar.release(m0)


MK.odd_layer_setup = _odd_layer_setup
MK.mlstm_unit = _mlstm_unit


def _band_prompt_unit(self, c, u, qdT, oT):
    ar = self.ar
    m0 = ar.mark()
    blocks = []
    for kc in range(max(0, c - 8), c + 1):
        blocks.append(dict(kT=(self.KT, 64 * kc), nk=64, v=V(self.Vd.ap[:, kc % 10], self.Vbuf.res),
                           bias=V(self.bandt.ap[:, :, c - kc, 0:64], self.bandt.res), mask=None))
    oa = ar.alloc([64, 8, 64])
    self.attend(qdT, u * 64, 64, blocks, V(oa.ap[0:64], oa.res))
    self.to_oT(oa, 64, oT, 4, u * 64)
    ar.release(m0)


def _band_sample_unit(self, o, s, qdT, KdTs, Vds, oT):
    ar = self.ar
    m0 = ar.mark()
    KTc = ar.alloc([128, 4, 512], BF16)
    Vc = ar.alloc([64, 8, 8, 66], BF16)
    self.memset(Vc.all(), 1.0)
    m1 = ar.mark()
    stg = [ar.alloc([128, 512]) for _ in range(3)]
    self.load_cache_T(self.d["cd_k"].ap()[o, s], 512, KTc, stg)
    for kc in range(8):
        st = stg[kc % 3]
        self.dma(st[0:64, :], self.dv(self.d["cd_v"].ap()[o, s][64 * kc:64 * kc + 64, :]))
        self.cp(V(Vc.ap[:, kc, :, 0:64], Vc.res), V(st.ap[0:64].rearrange("p (h e) -> p h e", h=8), st.res))
    ar.release(m1)
    q0 = s * TS
    blocks = []
    for kc in range(8):
        blocks.append(dict(kT=(KTc, 64 * kc), nk=64, v=V(Vc.ap[:, kc], Vc.res),
                           bias=V(self.bandt.ap[:, :, 8 - kc, 0:TS], self.bandt.res), mask=None))
    blocks.append(dict(kT=(KdTs, q0), nk=TS, v=V(Vds.ap[:, s], Vds.res),
                       bias=V(self.bandt.ap[:, :, 0, 0:TS], self.bandt.res), mask=None))
    oa = ar.alloc([64, 8, 64])
    self.attend(qdT, q0, TS, blocks, V(oa.ap[0:TS], oa.res))
    self.to_oT(oa, TS, oT, 4, q0)
    ar.release(m0)


def _odd_block(self, L, o, blk):
    ar = self.ar
    t0, TB, C, NU, smp = blk.t0, blk.TB, blk.C, blk.NU, blk.sample
    W = self.d["w_in_odd"].ap()[o]
    mB = ar.mark()
    oT = ar.alloc([128, 8, TB], BF16)
    qdT = ar.alloc([128, 8, TB], BF16)
    self.memset(qdT.all(), 0.0)
    if smp:
        KdTs = ar.alloc([128, 4, TB], BF16)
        Vds = ar.alloc([32, NS, 8, 66], BF16)
        self.memset(Vds.all(), 1.0)
    mG = ar.mark()
    qcT = ar.alloc([128, 4, TB], BF16)
    kcT = ar.alloc([128, 4, TB], BF16)
    vaug = ar.alloc([64, NU, 8, 66], BF16)
    self.memset(vaug.all(), 1.0)
    igf = ar.alloc([64, NU, 16])
    osig = ar.alloc([64, NU, 512])
    mP = ar.mark()
    xb = ar.alloc([128, 8, TB], BF16)
    for ch in range(8):
        self.cp(xb.part(ch).all(), V(self.xT.ap[:, ch, t0:t0 + TB], self.xres(ch, 1, t0, t0 + TB)))
    wts = [ar.alloc([128, 8, 256], BF16), ar.alloc([128, 8, 256], BF16)]
    wn = [0]

    def ldw(c0, n):
        wt = wts[wn[0] % 2]
        wn[0] += 1
        self.load_w(W[:, c0:c0 + n], D, n, wt)
        return wt
    ost = [ar.alloc([128, 256]), ar.alloc([128, 256])]
    osi = [0]
    wpad = ar.alloc([128, 8, 512], BF16)
    for (c0, dst, scl) in ((0, qcT, 1.0), (256, kcT, 32 ** -0.5)):
        self.memset(wpad.all(), 0.0)
        self.load_w(W[:, c0:c0 + 256], D, 256, wpad, pad_heads=True)
        for f in range(4):
            ps = self.proj_fm(wpad, f * 128, 128, xb, 0, TB)
            self.act(dst.part(f).all(), ps.v(128, TB), AF.Copy, scale=scl)
    for half in range(2):
        wt = ldw(512 + 256 * half, 256)
        for u in range(NU):
            ps = self.proj_tm(wt, 0, 256, xb, u * C, u * C + C)
            self.cp(V(vaug.ap[0:C, u, 4 * half:4 * half + 4, 0:64], vaug.res), ps.v(C, 4, 64))
    wt = wts[wn[0] % 2]; wn[0] += 1
    st_ = self.wstage[self.weng % 2]; self.weng += 1
    sv = V(st_.ap[:, 0:8 * 16].rearrange("p (k n) -> p k n", k=8), st_.res)
    self.dma(sv, self.dv(W[:, 1024:1040].rearrange("(k p) n -> p k n", p=128)))
    self.cp(V(wt.ap[:, :, 0:16], wt.res), sv, "pool")
    for u in range(NU):
        ps = self.proj_tm(wt, 0, 16, xb, u * C, u * C + C)
        self.cp(V(igf.ap[0:C, u, :], igf.res), ps.v(C, 16), "dve")
    for half in range(2):
        wt = ldw(1040 + 256 * half, 256)
        for u in range(NU):
            ps = self.proj_tm(wt, 0, 256, xb, u * C, u * C + C)
            self.act(V(osig.ap[0:C, u, 256 * half:256 * half + 256], osig.res), ps.v(C, 256), AF.Sigmoid)
    for half in range(2):
        wt = ldw(1552 + 256 * half, 256)
        for ff in range(2):
            ps = self.proj_fm(wt, ff * 128, 128, xb, 0, TB)
            f_ = half * 2 + ff
            self.act(V(qdT.ap[0:64, 2 * f_, :], qdT.res), ps.v(64, TB), AF.Copy, scale=0.125)
            self.act(V(qdT.ap[64:128, 2 * f_ + 1, :], qdT.res), ps.v(64, TB, p0=64), AF.Copy, scale=0.125)
    want_out = smp or blk.idx >= 6
    for half in range(2):
        wt = ldw(2064 + 256 * half, 256)
        for ff in range(2):
            f = half * 2 + ff
            ps = self.proj_fm(wt, ff * 128, 128, xb, 0, TB)
            if smp:
                self.cp(KdTs.part(f).all(), ps.v(128, TB))
            else:
                self.cp(self.KT.part(f).cols(t0, t0 + TB).all(), ps.v(128, TB))
        if want_out:
            for tt in range(TB // 128):
                ps = self.proj_tm(wt, 0, 256, xb, tt * 128, tt * 128 + 128)
                o_ = ost[osi[0] % 2]; osi[0] += 1
                self.cp(o_.all(), ps.v(128, 256))
                r0 = t0 + tt * 128 - (SEQ - 512)
                dst = self.d["d_k_s"].ap()[o] if smp else self.d["d_k_p"].ap()[o][r0:r0 + 128]
                self.dma(self.dv(dst[:, 256 * half:256 * half + 256]), o_.all(), eng="pool")
    for half in range(2):
        wt = ldw(2576 + 256 * half, 256)
        for u in range(NU):
            ps = self.proj_tm(wt, 0, 256, xb, u * C, u * C + C)
            if smp:
                self.cp(V(Vds.ap[:, u, 4 * half:4 * half + 4, 0:64], Vds.res), ps.v(C, 4, 64), "dve")
            else:
                c = t0 // 64 + u
                self.cp(V(self.Vd.ap[:, c % 10, 4 * half:4 * half + 4, 0:64], self.Vbuf.res), ps.v(C, 4, 64), "dve")
            if want_out:
                o_ = ost[osi[0] % 2]; osi[0] += 1
                self.cp(o_[0:C, :], ps.v(C, 256), "act")
                if smp:
                    dst = self.d["d_v_s"].ap()[o][u * TS:u * TS + TS]
                else:
                    r0 = t0 + u * 64 - (SEQ - 512)
                    dst = self.d["d_v_p"].ap()[o][r0:r0 + 64]
                self.dma(self.dv(dst[:, 256 * half:256 * half + 256]), o_[0:C, :], eng="pool")
    ar.release(mP)
    def mlstm_all():
        self.bset = ("A", [0, 1, 2, 3, 4])
        for u in range(NU):
            if smp:
                self.memset(self.Ca32.all(), 0.0)
                for t_ in range(2):
                    self.dma(V(self.Ca32.ap[64 * t_:64 * t_ + 32, :, 0:64], self.Ca32.res),
                             self.dv(self.d["sc_c"].ap()[o, u].rearrange("(f t) k v -> t k f v", t=2)[t_]))
                    self.dma(V(self.Ca32.ap[64 * t_:64 * t_ + 32, :, 64], self.Ca32.res),
                             self.dv(self.d["sc_n"].ap()[o, u].rearrange("(f t) k -> t k f", t=2)[t_]), nc_ok=True)
                self.cp(self.Cab.all(), self.Ca32.all(), "act")
                self.bcast_row(self.mprev.all(), self.d["sc_m"].ap()[o, u], 64)
            self.mlstm_unit(C, u * C, qcT, kcT, V(vaug.ap[0:C, u], vaug.res), V(igf.ap[0:C, u, :], igf.res),
                            V(osig.ap[0:C, u, :].rearrange("p (h e) -> p h e", h=8), osig.res), oT)
            if smp or (blk.idx == 7 and u == NU - 1):
                cc = self.d["c_c_s"].ap()[o, u] if smp else self.d["c_c_p"].ap()[o]
                cn = self.d["c_n_s"].ap()[o, u] if smp else self.d["c_n_p"].ap()[o]
                cm = self.d["c_m_s"].ap()[o, u:u + 1] if smp else self.d["c_m_p"].ap()[o:o + 1]
                for t_ in range(2):
                    self.dma(self.dv(cc.rearrange("(f t) k v -> t k f v", t=2)[t_]),
                             V(self.Ca32.ap[64 * t_:64 * t_ + 32, :, 0:64], self.Ca32.res), eng="pool")
                    self.dma(self.dv(cn.rearrange("(f t) k -> t k f", t=2)[t_]),
                             V(self.Ca32.ap[64 * t_:64 * t_ + 32, :, 64], self.Ca32.res), eng="pool", nc_ok=True)
                self.dma(self.dv(cm), V(self.mprev.ap[0:1, :], self.mprev.res), eng="pool")
    def band_all():
        self.bset = ("B", [5, 6])
        for u in range(NU):
            if smp:
                self.band_sample_unit(o, u, qdT, KdTs, Vds, oT)
            else:
                self.band_prompt_unit(t0 // 64 + u, u, qdT, oT)

    if smp:
        mlstm_all()
        ar.release(mG)
        band_all()
    else:
        self.parallel(mlstm_all, band_all)
        ar.release(mG)
    self.bset = None
    self.out_proj_ln(self.d["w_out_odd"].ap()[o], oT, t0, TB)
    ar.release(mB)


MK.band_prompt_unit = _band_prompt_unit
MK.band_sample_unit = _band_sample_unit
MK.odd_block = _odd_block
```

```python
import math
from contextlib import ExitStack
import numpy as np
import concourse.bass as bass
import concourse.mybir as mybir
from concourse.bass_utils import run_bass_kernel_spmd

F32 = mybir.dt.float32
BF16 = mybir.dt.bfloat16
AF = mybir.ActivationFunctionType
ALU = mybir.AluOpType
AX = mybir.AxisListType

SAME_ENGINE_SYNC = False
DEBUG_WHERE = True
N_DMA_SEMS = 32
GRAN = 512

D = 1024
SEQ = 2048
NS = 4
TS = 32
T = SEQ + NS * TS
DEPTH = 4
PAST = 1024
DFF = 2816
ALPHA = (2 * DEPTH) ** 0.25
E_TOT = 4184
O_TOT = 3088
BIG = 30000.0


class _Op:
    __slots__ = ("eng", "fn", "reads", "writes", "is_dma", "deps", "sig", "dsem", "dval", "where", "small")

    def __init__(self, eng, fn, reads, writes, is_dma):
        self.eng = eng
        self.fn = fn
        self.reads = reads
        self.writes = writes
        self.is_dma = is_dma
        self.deps = None
        self.sig = None
        self.dsem = None
        self.dval = None
        self.small = False


class Prog:
    ENGS = ("pe", "act", "dve", "pool", "sp")

    def __init__(self, nc):
        self.nc = nc
        self.ops = []
        self.h = {"pe": nc.tensor, "act": nc.scalar, "dve": nc.vector, "pool": nc.gpsimd, "sp": nc.sync}

    def _where(self):
        import sys
        f = sys._getframe(3)
        out = []
        for _ in range(4):
            if f is None:
                break
            out.append("%s:%d" % (f.f_code.co_name, f.f_lineno))
            f = f.f_back
        return " < ".join(out)

    def op(self, eng, fn, reads=(), writes=(), small=False):
        o = _Op(eng, fn, tuple(reads), tuple(writes), False)
        o.small = small
        o.where = self._where() if DEBUG_WHERE else None
        self.ops.append(o)

    def dma(self, eng, fn, reads=(), writes=()):
        o = _Op(eng, fn, tuple(reads), tuple(writes), True)
        o.where = self._where() if DEBUG_WHERE else None
        self.ops.append(o)

    def emit(self, stack):
        nc = self.nc
        ops = self.ops
        last_w = {}
        readers = {}
        for i, o in enumerate(ops):
            deps = set()
            psr = tuple(r for r in o.reads if isinstance(r, tuple))
            if psr:
                o.reads = tuple(r for r in o.reads if not isinstance(r, tuple))
                o.writes = o.writes + psr
            for r in o.reads:
                w = last_w.get(r)
                if w is not None:
                    deps.add(w)
            for r in o.writes:
                w = last_w.get(r)
                if w is not None:
                    deps.add(w)
                rl = readers.get(r)
                if rl:
                    deps.update(rl)
            deps.discard(i)
            o.deps = deps
            for r in o.reads:
                readers.setdefault(r, []).append(i)
            for r in o.writes:
                last_w[r] = i
                readers[r] = []
        need_sig = [False] * len(ops)
        for o in ops:
            for d in o.deps:
                od = ops[d]
                if od.is_dma:
                    continue
                if od.eng == o.eng and not o.is_dma and (od.eng == "pe" or not (SAME_ENGINE_SYNC or od.small)):
                    continue
                need_sig[d] = True
        sems = {e: stack.enter_context(nc.semaphore("s_" + e)) for e in self.ENGS}
        dsems = [stack.enter_context(nc.semaphore("d%d" % k)) for k in range(N_DMA_SEMS)]
        dsem_cum = [0] * N_DMA_SEMS
        cnt = {e: 0 for e in self.ENGS}
        for i, o in enumerate(ops):
            if not o.is_dma and need_sig[i]:
                cnt[o.eng] += 1
                o.sig = cnt[o.eng]
        waited = {e: {} for e in self.ENGS}
        dwaited = {e: {} for e in self.ENGS}
        n_dma = 0
        n_wait = 0
        for i, o in enumerate(ops):
            e = o.eng
            h = self.h[e]
            need = {}
            dneed = {}
            for d in o.deps:
                od = ops[d]
                if od.is_dma:
                    if dwaited[e].get(od.dsem, 0) < od.dval:
                        dneed[od.dsem] = max(dneed.get(od.dsem, 0), od.dval)
                else:
                    if od.sig is None:
                        continue
                    if od.eng == e and not o.is_dma and (e == "pe" or not (SAME_ENGINE_SYNC or od.small)):
                        continue
                    if waited[e].get(od.eng, 0) < od.sig:
                        need[od.eng] = max(need.get(od.eng, 0), od.sig)
            slot = None
            if o.is_dma:
                slot = n_dma % N_DMA_SEMS
                n_dma += 1
                if dsem_cum[slot] > 0 and dwaited[e].get(slot, 0) < dsem_cum[slot]:
                    dneed[slot] = max(dneed.get(slot, 0), dsem_cum[slot])
            for src, v in need.items():
                h.wait_ge(sems[src], v)
                waited[e][src] = v
                n_wait += 1
            for s_, v in dneed.items():
                h.wait_ge(dsems[s_], v)
                dwaited[e][s_] = v
                n_wait += 1
            try:
                ins = o.fn(h)
            except BaseException:
                print("FAILED OP at", o.where, flush=True)
                raise
            if o.is_dma:
                dsem_cum[slot] += 16
                o.dsem = slot
                o.dval = dsem_cum[slot]
                ins.then_inc(dsems[slot], 16)
            elif o.sig is not None:
                ins.then_inc(sems[e], 1)
        h = self.h["sp"]
        for slot in range(N_DMA_SEMS):
            if dsem_cum[slot] > 0 and dwaited["sp"].get(slot, 0) < dsem_cum[slot]:
                h.wait_ge(dsems[slot], dsem_cum[slot])
        return dict(n_ops=len(ops), n_dma=n_dma, n_wait=n_wait, sig=dict(cnt))


class V:
    __slots__ = ("ap", "res")

    def __init__(self, ap, res):
        self.ap = ap
        self.res = res


class Tile:
    def __init__(self, ap, off, span, isz):
        self.ap = ap
        self.off = off
        self.span = span
        self.isz = isz
        self._res = None

    @property
    def res(self):
        if self._res is None:
            self._res = tuple(range(self.off // GRAN, (self.off + self.span - 1) // GRAN + 1))
        return self._res

    def __getitem__(self, idx):
        return V(self.ap[idx], self.res)

    def all(self):
        return V(self.ap, self.res)

    def part(self, i, n=1):
        shp = self.ap.shape
        stride = 1
        for s in shp[2:]:
            stride *= s
        sb = stride * self.isz
        if n == 1:
            return Tile(self.ap[:, i], self.off + i * sb, sb, self.isz)
        return Tile(self.ap[:, i:i + n], self.off + i * sb, sb * n, self.isz)

    def cols(self, lo, hi):
        return Tile(self.ap[:, lo:hi], self.off + lo * self.isz, (hi - lo) * self.isz, self.isz)


class Arena:
    def __init__(self, nc, kbytes):
        self.nc = nc
        self.nwords = kbytes * 256
        self.t = nc.alloc_sbuf_tensor("arena", [128, self.nwords], F32)
        self.top = 0
        self.peak = 0

    def alloc(self, shape, dt=F32):
        isz = 4 if dt == F32 else 2
        n = 1
        for s in shape[1:]:
            n *= s
        nbytes = (n * isz + 3) // 4 * 4
        off = self.top
        self.top += nbytes
        if self.top > self.peak:
            self.peak = self.top
        assert self.top <= self.nwords * 4, "arena overflow %d" % self.top
        ap = self.t[0:shape[0], off // 4: off // 4 + nbytes // 4]
        if dt != F32:
            ap = ap.bitcast(dt)
            ap = ap[:, 0:n]
        if len(shape) == 3:
            ap = ap.rearrange("p (a b) -> p a b", a=shape[1])
        elif len(shape) == 4:
            ap = ap.rearrange("p (a b c) -> p a b c", a=shape[1], b=shape[2])
        return Tile(ap, off, nbytes, isz)

    def mark(self):
        return self.top

    def release(self, m):
        self.top = m


class PT:
    def __init__(self, t, k):
        self.t = t
        self.res = (("ps", k),)

    def v(self, parts, *shape, off=0, p0=0):
        n = 1
        for s in shape:
            n *= s
        ap = self.t[p0:p0 + parts, off:off + n]
        if len(shape) == 2:
            ap = ap.rearrange("p (a b) -> p a b", a=shape[0])
        elif len(shape) == 3:
            ap = ap.rearrange("p (a b c) -> p a b c", a=shape[0], b=shape[1])
        return V(ap, self.res)


def _res(*vs):
    r = ()
    for v in vs:
        if isinstance(v, V):
            r = r + tuple(v.res)
    return r


def _a(v):
    return v.ap if isinstance(v, V) else v


SMALL_N = 256


def _small(out):
    n = 1
    for s_ in out.ap.shape[1:]:
        n *= s_
    return n <= SMALL_N


class KB:
    def __init__(self, nc, stack):
        self.nc = nc
        self.P = Prog(nc)
        self.st = stack
        self.ar = Arena(nc, 206)
        self.banks = [PT(stack.enter_context(nc.psum_tensor("psb%d" % i, [128, 512], F32)), i) for i in range(8)]
        self.bi = 0
        self.ev = 0
        self.bset = None
        self.bcnt = {}

    def ps(self):
        if self.bset is None:
            b = self.banks[self.bi % 8]
            self.bi += 1
            return b
        name, lst = self.bset
        k = self.bcnt.get(name, 0)
        self.bcnt[name] = k + 1
        return self.banks[lst[k % len(lst)]]

    def ps_po(self):
        if self.bset is None:
            return self.ps()
        return self.banks[7]

    def mm(self, out, lhsT, rhs, start=True, stop=True):
        self.P.op("pe", lambda h: h.matmul(out.ap, lhsT=lhsT.ap, rhs=rhs.ap, start=start, stop=stop),
                  reads=_res(lhsT, rhs), writes=out.res)

    def tr(self, out, in_, ident):
        self.P.op("pe", lambda h: h.transpose(out.ap, in_.ap, ident.ap), reads=_res(in_, ident), writes=out.res)

    def act(self, out, in_, func, bias=None, scale=None, accum=None):
        kw = {}
        if bias is not None:
            kw["bias"] = _a(bias)
        if scale is not None:
            kw["scale"] = _a(scale)
        if accum is not None:
            kw["accum_out"] = accum.ap
        self.P.op("act", lambda h: h.activation(out=out.ap, in_=in_.ap, func=func, **kw),
                  reads=_res(in_, bias, scale), writes=_res(out, accum), small=(_small(out) or accum is not None))

    def ts(self, out, in0, s1, op0, s2=None, op1=None, accum=None, eng="dve"):
        kw = {}
        if op1 is not None:
            kw["op1"] = op1
        if accum is not None:
            kw["accum_out"] = accum.ap
        self.P.op(eng, lambda h: h.tensor_scalar(out=out.ap, in0=in0.ap, scalar1=_a(s1), scalar2=_a(s2), op0=op0, **kw),
                  reads=_res(in0, s1, s2), writes=_res(out, accum), small=(_small(out) or accum is not None))

    def tt(self, out, in0, in1, op, eng="dve"):
        self.P.op(eng, lambda h: h.tensor_tensor(out=out.ap, in0=in0.ap, in1=in1.ap, op=op),
                  reads=_res(in0, in1), writes=out.res, small=_small(out))

    def stt(self, out, in0, sc, in1, op0, op1):
        self.P.op("dve", lambda h: h.scalar_tensor_tensor(out=out.ap, in0=in0.ap, scalar=_a(sc), in1=in1.ap, op0=op0, op1=op1),
                  reads=_res(in0, sc, in1), writes=out.res, small=_small(out))

    def cp(self, out, in_, eng=None):
        if eng is None:
            self.ev += 1
            eng = "act" if self.ev % 2 else "dve"
        if eng == "act":
            self.P.op("act", lambda h: h.copy(out=out.ap, in_=in_.ap), reads=in_.res, writes=out.res, small=_small(out))
        else:
            self.P.op(eng, lambda h: h.tensor_copy(out=out.ap, in_=in_.ap), reads=in_.res, writes=out.res, small=_small(out))

    def memset(self, out, val, eng="dve"):
        self.P.op(eng, lambda h: h.memset(out.ap, val), writes=out.res, small=_small(out))

    def red(self, out, in_, op, eng="dve"):
        self.P.op(eng, lambda h: h.tensor_reduce(out=out.ap, in_=in_.ap, axis=AX.X, op=op), reads=in_.res, writes=out.res, small=_small(out))

    def recip(self, out, in_):
        self.P.op("dve", lambda h: h.reciprocal(out=out.ap, in_=in_.ap), reads=in_.res, writes=out.res, small=_small(out))

    def dma(self, out, in_, eng="sp", nc_ok=False):
        def f(h):
            if nc_ok:
                with self.nc.allow_non_contiguous_dma(reason="small strided"):
                    return h.dma_start(out=out.ap, in_=in_.ap)
            return h.dma_start(out=out.ap, in_=in_.ap)
        self.P.dma(eng, f, reads=in_.res, writes=out.res)

    def rsqrt(self, out, in_, eps_tile, scale=1.0):
        self.act(out, in_, AF.Ln, bias=eps_tile, scale=scale)
        self.act(out, out, AF.Exp, scale=-0.5)


def _pt_vb(self, parts, *shape, off=0):
    n = 1
    for s in shape:
        n *= s
    ap = self.t[0:parts, :].bitcast(BF16)[:, off:off + n]
    if len(shape) == 2:
        ap = ap.rearrange("p (a b) -> p a b", a=shape[0])
    elif len(shape) == 3:
        ap = ap.rearrange("p (a b c) -> p a b c", a=shape[0], b=shape[1])
    return V(ap, self.res)


PT.vb = _pt_vb

C_ID, C_ONES, C_UTRI, C_J, C_BLK = 0, 128, 256, 384, 512
C_MU64, C_ML64, C_SL64 = 640, 1152, 1664
C_MU32, C_ML32, C_SL32 = 2176, 2432, 2688
C_SEL64, C_SEL32 = 2944, 3008
C_IR64, C_IR32 = 3072, 3584
C_SEL64W, C_SEL32W = 3840, 3968
C_POW2 = 4096
NCST = 4128
NR = 704


def _t5_bucket(rel):
    nb = 16
    max_exact = 8
    n = np.abs(rel)
    n_f = np.maximum(n, 1).astype(np.float32)
    large = max_exact + (np.log(n_f / max_exact) / math.log(128 / max_exact) * (nb - max_exact)).astype(np.int32)
    large = np.minimum(large, nb - 1)
    return np.where(rel > 0, nb, 0) + np.where(n < max_exact, n, large)


def make_consts():
    c = np.zeros((128, NCST), np.float32)
    i = np.arange(128)
    c[:, C_ID:C_ID + 128] = np.eye(128)
    c[:, C_ONES:C_ONES + 128] = 1.0
    c[:, C_UTRI:C_UTRI + 128] = (i[:, None] <= i[None, :])
    c[:, C_J:C_J + 128] = np.eye(128)[::-1]
    c[:, C_BLK:C_BLK + 128] = (i[:, None] // 64 == i[None, :] // 64)
    for (cc, o_mu, o_ml, o_sl, o_ir) in ((64, C_MU64, C_ML64, C_SL64, C_IR64), (32, C_MU32, C_ML32, C_SL32, C_IR32)):
        a = np.arange(cc)
        mu = np.where(a[None, :] > a[:, None], BIG, 0.0)
        ml = np.where(a[None, :] < a[:, None], -BIG, 0.0)
        sl = (a[None, :] < a[:, None]).astype(np.float32)
        ir = np.eye(cc)
        c[:cc, o_mu:o_mu + 8 * cc] = np.tile(mu, (1, 8))
        c[:cc, o_ml:o_ml + 8 * cc] = np.tile(ml, (1, 8))
        c[:cc, o_sl:o_sl + 8 * cc] = np.tile(sl, (1, 8))
        c[:cc, o_ir:o_ir + 8 * cc] = np.tile(ir, (1, 8))
    c[63, C_SEL64:C_SEL64 + 64] = 1.0
    c[31, C_SEL32:C_SEL32 + 64] = 1.0
    c[63, C_SEL64W:C_SEL64W + 128] = 1.0
    c[:, C_POW2:C_POW2 + 32] = (0.5 ** (np.arange(32) + 1))[None, :]
    c[31, C_SEL32W:C_SEL32W + 128] = 1.0
    rel = 127 - np.arange(NR)
    oh_t5 = np.zeros((32, NR), np.float32)
    oh_t5[_t5_bucket(rel), np.arange(NR)] = 1.0
    oh_rel = np.zeros((384, NR), np.float32)
    oh_rel[np.clip(rel, -128, 128) + 128, np.arange(NR)] = 1.0
    return c, oh_t5, oh_rel


class Blk:
    def __init__(self, t0, TB, C, sample, idx):
        self.t0 = t0
        self.TB = TB
        self.C = C
        self.sample = sample
        self.idx = idx
        self.NU = TB // C


BLOCKS = [Blk(256 * b, 256, 64, False, b) for b in range(8)] + [Blk(2048, 128, 32, True, 8)]


class MK(KB):
    def __init__(self, nc, stack, n_layers=DEPTH):
        super().__init__(nc, stack)
        self.n_layers = n_layers
        self.d = {}
        self.weng = 0

    def din(self, name, shape):
        t = self.nc.dram_tensor(name, list(shape), F32, kind="ExternalInput")
        self.d[name] = t
        return t

    def dout(self, name, shape):
        t = self.nc.dram_tensor(name, list(shape), F32, kind="ExternalOutput")
        self.d[name] = t
        return t

    def decl(self):
        i, o = self.din, self.dout
        i("xp", (SEQ, D)); i("xs", (NS * TS, D))
        i("ca_k", (2, NS, PAST, 512)); i("ca_v", (2, NS, PAST, 512)); i("ca_ki", (2, NS, PAST, 64))
        i("sb_s", (2, NS, 8, 64, 64)); i("sb_conv", (2, NS, 3, 1536))
        i("sc_c", (2, NS, 8, 32, 64)); i("sc_n", (2, NS, 8, 32)); i("sc_m", (2, NS, 8))
        i("cd_k", (2, NS, 512, 512)); i("cd_v", (2, NS, 512, 512))
        i("sf_conv", (4, NS, 2, 2 * DFF))
        i("w_in_even", (2, D, E_TOT)); i("w_out_even", (2, D, D)); i("t5_bias", (32, 8))
        i("b_conv_w", (2, 4, 1536)); i("b_a_log", (2, 8)); i("b_dt_bias", (2, 8)); i("b_norm_w", (2, 64))
        i("w_in_odd", (2, D, O_TOT)); i("w_out_odd", (2, D, D))
        i("c_i_bias", (2, 8)); i("c_f_bias", (2, 8)); i("c_norm_w", (2, 64)); i("d_rel_bias", (2, 257, 8))
        i("ln_mix_g", (4, D)); i("ln_mix_b", (4, D)); i("ln_ffn_g", (4, D)); i("ln_ffn_b", (4, D))
        i("ffn_w_up", (4, D, 2 * DFF)); i("ffn_conv_w", (4, 3, 2 * DFF)); i("ffn_w_down", (4, DFF, D))
        i("cst", (128, NCST)); i("oh_t5", (32, NR)); i("oh_rel", (384, NR))
        o("y_p", (SEQ, D)); o("y_s", (NS * TS, D))
        o("a_k_p", (2, SEQ, 512)); o("a_k_s", (2, NS * TS, 512)); o("a_v_p", (2, SEQ, 512)); o("a_v_s", (2, NS * TS, 512))
        o("a_ki_p", (2, SEQ, 64)); o("a_ki_s", (2, NS * TS, 64))
        o("b_s_p", (2, 8, 64, 64)); o("b_s_s", (2, NS, 8, 64, 64)); o("b_conv_p", (2, 3, 1536)); o("b_conv_s", (2, NS, 3, 1536))
        o("c_c_p", (2, 8, 32, 64)); o("c_c_s", (2, NS, 8, 32, 64)); o("c_n_p", (2, 8, 32)); o("c_n_s", (2, NS, 8, 32))
        o("c_m_p", (2, 8)); o("c_m_s", (2, NS, 8))
        o("d_k_p", (2, 512, 512)); o("d_k_s", (2, NS * TS, 512)); o("d_v_p", (2, 512, 512)); o("d_v_s", (2, NS * TS, 512))
        o("f_conv_p", (4, 2, 2 * DFF)); o("f_conv_s", (4, NS, 2, 2 * DFF))
        import os
        self.dbg = bool(os.environ.get("MK_DBG"))
        if self.dbg:
            o("dbg_o", (4, 128, 8, T)); o("dbg_xm", (4, 128, 8, T)); o("dbg_xo", (4, 128, 8, T))
        self.scr_t5 = self.nc.dram_tensor("scr_t5", [8, NR], F32, kind="Internal")
        self.scr_rel = self.nc.dram_tensor("scr_rel", [2, 8, NR], F32, kind="Internal")

    def D_(self, name, *res):
        return V(self.d[name].ap(), tuple(res))

    def dv(self, ap, *res):
        return V(ap, tuple(res))

    def parallel(self, fA, fB):
        ar = self.ar
        base = ar.top
        main = self.P.ops
        self.P.ops = []
        ar.peak = ar.top
        fA()
        la = self.P.ops
        assert ar.top == base
        ar.top = ar.peak
        mid_ = ar.top
        self.P.ops = []
        fB()
        lb = self.P.ops
        assert ar.top == mid_
        ar.top = base
        merged = []
        ia = ib = 0
        na, nb = len(la), len(lb)
        while ia < na or ib < nb:
            if ib >= nb or (ia < na and ia * nb <= ib * na):
                merged.append(la[ia]); ia += 1
            else:
                merged.append(lb[ib]); ib += 1
        self.P.ops = main + merged

    def xres(self, ch0, nch, c0, c1):
        r = ()
        for ch in range(ch0, ch0 + nch):
            r = r + self.xT.part(ch).cols(c0, c1).res
        return r

    def xv(self, ch0, nch, c0, c1):
        return V(self.xT.ap[:, ch0:ch0 + nch, c0:c1], self.xres(ch0, nch, c0, c1))

    def cv(self, c0, n, parts=128):
        return V(self.cst.ap[0:parts, c0:c0 + n], self.cst.res)

    def cv3(self, c0, parts, a, b):
        return V(self.cst.ap[0:parts, c0:c0 + a * b].rearrange("p (a b) -> p a b", a=a), self.cst.res)

    def load_w(self, dram_ap, nrows, ncols, dst, pad_heads=False):
        k = nrows // 128
        st = self.wstage[self.weng % 2]
        sv = V(st.ap[:, 0:k * ncols].rearrange("p (k n) -> p k n", k=k), st.res)
        self.dma(sv, self.dv(dram_ap.rearrange("(k p) n -> p k n", p=128)))
        eng = ("pool", "act")[self.weng % 2]
        self.weng += 1
        if pad_heads:
            dstv = V(dst.ap.rearrange("p k (h e) -> p k h e", h=8)[:, :, :, 0:32], dst.res)
            srcv = V(sv.ap.rearrange("p k (h e) -> p k h e", h=8), sv.res)
            self.cp(dstv, srcv, "pool")
        else:
            self.cp(V(dst.ap[:, :, 0:ncols], dst.res), sv, eng)

    def proj_fm(self, wt, c0, ncol, xb, n0, n1):
        ps = self.ps()
        o = ps.v(ncol, n1 - n0)
        for ch in range(8):
            self.mm(o, wt.part(ch)[:, c0:c0 + ncol], xb.part(ch)[:, n0:n1], start=(ch == 0), stop=(ch == 7))
        return ps

    def proj_tm(self, wt, c0, ncol, xb, n0, n1):
        ps = self.ps()
        o = ps.v(n1 - n0, ncol)
        for ch in range(8):
            self.mm(o, xb.part(ch)[:, n0:n1], wt.part(ch)[:, c0:c0 + ncol], start=(ch == 0), stop=(ch == 7))
        return ps

    def bcast_row(self, dst, dram_row_ap, parts):
        self.dma(dst, self.dv(dram_row_ap.partition_broadcast(parts)))

    def setup(self):
        ar = self.ar
        self.xT = ar.alloc([128, 8, T])
        self.cst = ar.alloc([128, NCST])
        self.dma(self.cst.all(), self.D_("cst"))
        self.idb = ar.alloc([128, 128], BF16)
        self.onesb = ar.alloc([128, 128], BF16)
        self.blkb = ar.alloc([128, 128], BF16)
        self.cp(self.idb.all(), self.cv(C_ID, 128), "dve")
        self.cp(self.onesb.all(), self.cv(C_ONES, 128), "dve")
        self.cp(self.blkb.all(), self.cv(C_BLK, 128), "dve")
        self.idf = self.cv(C_ID, 128)
        self.onesf = self.cv(C_ONES, 128)
        self.eps6 = ar.alloc([128, 1]); self.memset(self.eps6.all(), 1e-6, "dve")
        self.eps5 = ar.alloc([128, 1]); self.memset(self.eps5.all(), 1e-5, "dve")
        self.one1 = ar.alloc([128, 1]); self.memset(self.one1.all(), 1.0, "dve")
        self.t5t = ar.alloc([128, 8, 5, 64], BF16)
        self.bandt = ar.alloc([128, 8, 9, 64], BF16)
        self.KT = ar.alloc([128, 4, SEQ], BF16)
        self.Vbuf = ar.alloc([128, 16 * 528], BF16)
        self.kiT = ar.alloc([128, SEQ], BF16)
        self.wstage = [ar.alloc([128, 2048]), ar.alloc([128, 2048])]
        self.lnp = ar.alloc([128, 4, 8])
        m0 = ar.mark()
        xin = [ar.alloc([128, D]), ar.alloc([128, D])]
        for tt in range(17):
            src = self.d["xp"].ap()[tt * 128:(tt + 1) * 128, :] if tt < 16 else self.d["xs"].ap()
            xi = xin[tt % 2]
            self.dma(xi.all(), self.dv(src))
            for g in range(2):
                ps = self.ps()
                for k in range(4):
                    ch = g * 4 + k
                    self.tr(ps.v(128, 128, off=k * 128), xi[:, ch * 128:(ch + 1) * 128], self.idf)
                self.cp(self.xv(g * 4, 4, tt * 128, (tt + 1) * 128), ps.v(128, 4, 128))
        tab = ar.alloc([32, 8])
        oh = ar.alloc([32, NR])
        wr = ar.alloc([8, NR])
        self.dma(tab.all(), self.D_("t5_bias"))
        self.dma(oh.all(), self.D_("oh_t5"))
        for n0 in (0, 352):
            ps = self.ps()
            self.mm(ps.v(8, 352), tab.all(), oh[:, n0:n0 + 352])
            self.cp(wr[:, n0:n0 + 352], ps.v(8, 352))
        self.dma(V(self.scr_t5.ap(), ("scr_t5",)), wr.all())
        hk = ar.alloc([128, 8, 64])
        for di in range(5):
            src = bass.AP(self.scr_t5, 64 * di, [[1, 128], [NR, 8], [1, 64]])
            self.dma(hk.all(), V(src, ("scr_t5",)))
            ps = self.ps()
            self.mm(ps.v(128, 512), self.cv(C_J, 128), V(hk.ap.rearrange("p a b -> p (a b)"), hk.res))
            self.cp(V(self.t5t.ap[:, :, di, :], self.t5t.res), ps.v(128, 8, 64))
        ar.release(m0)

    def band_tiles(self, o):
        ar = self.ar
        m0 = ar.mark()
        tab = ar.alloc([128, 3, 8])
        self.memset(tab.all(), 0.0, "dve")
        src = self.d["d_rel_bias"].ap()[o]
        self.dma(V(tab.ap[:, 0:2, :], tab.res), self.dv(src[0:256, :].rearrange("(k p) h -> p k h", p=128)))
        self.dma(V(tab.ap[0:1, 2, :], tab.res), self.dv(src[256:257, :]))
        wr = ar.alloc([8, NR])
        oh = ar.alloc([128, 3, 352])
        for n0 in (0, 352):
            self.dma(oh.all(), self.dv(self.d["oh_rel"].ap()[:, n0:n0 + 352].rearrange("(k p) n -> p k n", p=128)))
            ps = self.ps()
            for k in range(3):
                self.mm(ps.v(8, 352), tab.part(k).all(), oh.part(k).all(), start=(k == 0), stop=(k == 2))
            self.cp(wr[:, n0:n0 + 352], ps.v(8, 352))
        key = "scr_rel%d" % o
        self.dma(V(self.scr_rel.ap()[o], (key,)), wr.all())
        hk = ar.alloc([128, 8, 64])
        for k in range(9):
            src = bass.AP(self.scr_rel, o * 8 * NR + 64 * k, [[1, 128], [NR, 8], [1, 64]])
            self.dma(hk.all(), V(src, (key,)))
            ps = self.ps()
            self.mm(ps.v(128, 512), self.cv(C_J, 128), V(hk.ap.rearrange("p a b -> p (a b)"), hk.res))
            self.cp(V(self.bandt.ap[:, :, k, :], self.bandt.res), ps.v(128, 8, 64))
        ar.release(m0)

    def load_rows_T(self, dst, src2d, nrows, ncol_tiles):
        ar = self.ar
        m1 = ar.mark()
        nr2 = nrows + (nrows % 2)
        hrow = ar.alloc([nr2, ncol_tiles * 128])
        if nr2 != nrows:
            self.memset(hrow.all(), 0.0)
        self.dma(hrow[0:nrows, :], self.dv(src2d))
        ps = self.ps()
        for f in range(ncol_tiles):
            self.tr(ps.v(128, nr2, off=f * nr2), hrow[:, f * 128:(f + 1) * 128], self.cv(C_ID, nr2, parts=nr2))
        self.cp(dst, V(ps.v(128, ncol_tiles, nr2).ap[:, :, 0:nrows], ps.res))
        ar.release(m1)

    def load_ln(self, L):
        ar = self.ar
        m1 = ar.mark()
        ln4 = ar.alloc([32, 128])
        for k, nm in enumerate(("ln_mix_g", "ln_mix_b", "ln_ffn_g", "ln_ffn_b")):
            self.dma(ln4[8 * k:8 * k + 8, :], self.dv(self.d[nm].ap()[L].rearrange("(c p) -> c p", p=128)))
        ps = self.ps()
        self.tr(ps.v(128, 32), ln4.all(), self.cv(C_ID, 32, parts=32))
        self.cp(self.lnp.all(), ps.v(128, 4, 8))
        ar.release(m1)

    def layer_norm(self, c0, n, gk, bk):
        ar = self.ar
        m0 = ar.mark()
        sq = [ar.alloc([128, n]), ar.alloc([128, n])]
        p1 = self.ps()
        p2 = self.ps()
        for ch in range(8):
            xc = self.xv(ch, 1, c0, c0 + n)
            xc = V(self.xT.ap[:, ch, c0:c0 + n], xc.res)
            s = sq[ch % 2]
            self.act(s.all(), xc, AF.Square)
            self.mm(p1.v(128, n), self.onesf, xc, start=(ch == 0), stop=(ch == 7))
            self.mm(p2.v(128, n), self.onesf, s.all(), start=(ch == 0), stop=(ch == 7))
        mean = ar.alloc([128, n])
        rstd = ar.alloc([128, n])
        self.ts(mean.all(), p1.v(128, n), 1.0 / D, ALU.mult)
        self.tt(rstd.all(), mean.all(), mean.all(), ALU.mult)
        self.stt(rstd.all(), p2.v(128, n), 1.0 / D, rstd.all(), ALU.mult, ALU.subtract)
        self.rsqrt(rstd.all(), rstd.all(), self.eps5[:, 0:1])
        for ch in range(8):
            xc = V(self.xT.ap[:, ch, c0:c0 + n], self.xres(ch, 1, c0, c0 + n))
            self.tt(xc, xc, mean.all(), ALU.subtract)
            self.tt(xc, xc, rstd.all(), ALU.mult, eng=("pool" if ch % 2 else "dve"))
            self.act(xc, xc, AF.Identity, bias=V(self.lnp.ap[:, bk, ch:ch + 1], self.lnp.res),
                     scale=V(self.lnp.ap[:, gk, ch:ch + 1], self.lnp.res))
        ar.release(m0)

    def tile_at(self, off, shape, dt):
        isz = 4 if dt == F32 else 2
        n = 1
        for s in shape[1:]:
            n *= s
        nbytes = n * isz
        ap = self.ar.t[0:shape[0], off // 4: off // 4 + (nbytes + 3) // 4]
        if dt != F32:
            ap = ap.bitcast(dt)[:, 0:n]
        if len(shape) == 3:
            ap = ap.rearrange("p (a b) -> p a b", a=shape[1])
        return Tile(ap, off, nbytes, isz)

    def ffn(self, L):
        ar = self.ar
        m0 = ar.mark()
        SBS = [(512 * i, 512 * (i + 1)) for i in range(4)] + [(SEQ, T)]
        xb = self.tile_at(self.KT.off, [128, 8, T], BF16)
        for ch in range(8):
            for (n0, n1) in SBS:
                self.cp(xb.part(ch)[:, n0:n1], V(self.xT.ap[:, ch, n0:n1], self.xres(ch, 1, n0, n1)))
        cwf = ar.alloc([128, 44, 3])
        for q in range(4):
            self.load_rows_T(V(cwf.ap[:, 11 * q:11 * q + 11, :], cwf.res), self.d["ffn_conv_w"].ap()[L][:, 1408 * q:1408 * q + 1408], 3, 11)
        fhist = ar.alloc([128, 44, 8])
        for q in range(4):
            m1 = ar.mark()
            hrow = ar.alloc([8, 1408])
            self.dma(hrow.all(), self.dv(self.d["sf_conv"].ap()[L].rearrange("s j n -> (s j) n")[:, q * 1408:(q + 1) * 1408]))
            ps = self.ps()
            for f in range(11):
                self.tr(ps.v(128, 8, off=f * 8), hrow[:, f * 128:(f + 1) * 128], self.cv(C_ID, 8, parts=8))
            self.cp(V(fhist.ap[:, q * 11:(q + 1) * 11, :], fhist.res), ps.v(128, 11, 8))
            ar.release(m1)
        xsel = ar.alloc([128, 8, 10], BF16)
        self.cp(V(xsel.ap[:, :, 0:2], xsel.res), V(xb.ap[:, :, SEQ - 2:SEQ], xb.res), "dve")
        for s in range(NS):
            c = SEQ + TS * s + TS - 2
            self.cp(V(xsel.ap[:, :, 2 + 2 * s:4 + 2 * s], xsel.res), V(xb.ap[:, :, c:c + 2], xb.res), "dve")
        wg = [ar.alloc([128, 8, 256], BF16) for _ in range(2)]
        wu = [ar.alloc([128, 8, 256], BF16) for _ in range(2)]
        wd = [ar.alloc([128, 2, D], BF16) for _ in range(2)]
        actT = [ar.alloc([128, 2, T], BF16) for _ in range(1)]
        hb = [ar.alloc([128, 516]) for _ in range(3)]
        yb = [ar.alloc([128, 512]) for _ in range(3)]
        carry = ar.alloc([128, 2, 2])
        fo = [ar.alloc([10, 512]) for _ in range(2)]
        wup = self.d["ffn_w_up"].ap()[L]
        wdn = self.d["ffn_w_down"].ap()[L]
        hi = 0
        for g in range(11):
            wgt, wut, wdt, at = wg[g % 2], wu[g % 2], wd[g % 2], actT[0]
            self.load_w(wup[:, 256 * g:256 * g + 256], D, 256, wgt)
            self.load_w(wup[:, DFF + 256 * g:DFF + 256 * g + 256], D, 256, wut)
            self.load_w(wdn[256 * g:256 * g + 256, :], 256, D, wdt)
            fot = fo[g % 2]
            for k, w_ in enumerate((wgt, wut)):
                ps = self.ps()
                for ch in range(8):
                    self.mm(ps.v(10, 256), xsel.part(ch).all(), w_.part(ch).all(), start=(ch == 0), stop=(ch == 7))
                self.cp(fot[:, 256 * k:256 * k + 256], ps.v(10, 256))
                col = (0 if k == 0 else DFF) + 256 * g
                self.dma(self.dv(self.d["f_conv_p"].ap()[L][:, col:col + 256]), fot[0:2, 256 * k:256 * k + 256], eng="pool")
                self.dma(self.dv(self.d["f_conv_s"].ap()[L].rearrange("s j n -> (s j) n")[:, col:col + 256]),
                         fot[2:10, 256 * k:256 * k + 256], eng="pool")
            for jj in range(2):
                for k, w_ in enumerate((wgt, wut)):
                    f = (0 if k == 0 else 22) + 2 * g + jj
                    self.memset(V(carry.ap[:, k, :], carry.res), 0.0, "dve")
                for (n0, n1) in SBS:
                    n = n1 - n0
                    ys = []
                    for k, w_ in enumerate((wgt, wut)):
                        f = (0 if k == 0 else 22) + 2 * g + jj
                        ps = self.proj_fm(w_, jj * 128, 128, xb, n0, n1)
                        h = hb[hi % 3]
                        y = yb[hi % 3]
                        hi += 1
                        cw = lambda j: V(cwf.ap[:, f, j:j + 1], cwf.res)
                        if n0 < SEQ:
                            self.cp(h[:, 2:2 + n], ps.v(128, n), "act")
                            self.cp(h[:, 0:2], V(carry.ap[:, k, :], carry.res), "pool")
                            self.cp(V(carry.ap[:, k, :], carry.res), h[:, n:n + 2], "pool")
                            hv = lambda j: h[:, j:j + n]
                            yv = y[:, 0:n]
                        else:
                            h3 = V(h.ap[:, 0:4 * 34].rearrange("p (s c) -> p s c", s=4), h.res)
                            self.cp(V(h3.ap[:, :, 2:34], h.res), ps.v(128, 4, 32), "act")
                            self.cp(V(h3.ap[:, :, 0:2], h.res),
                                    V(fhist.ap[:, f, :].rearrange("p (s j) -> p s j", s=4), fhist.res), "pool")
                            hv = lambda j: V(h3.ap[:, :, j:j + 32], h.res)
                            yv = V(y.ap[:, 0:128].rearrange("p (s c) -> p s c", s=4), y.res)
                        self.ts(yv, hv(0), cw(0), ALU.mult)
                        self.stt(yv, hv(1), cw(1), yv, ALU.mult, ALU.add)
                        self.stt(yv, hv(2), cw(2), yv, ALU.mult, ALU.add)
                        ys.append(y)
                    self.act(ys[0][:, 0:n], ys[0][:, 0:n], AF.Silu)
                    self.tt(at.part(jj)[:, n0:n1], ys[0][:, 0:n], ys[1][:, 0:n], ALU.mult, eng="pool")
            for dt_ in range(8):
                for (n0, n1) in SBS:
                    n = n1 - n0
                    ps = self.ps()
                    for jj in range(2):
                        self.mm(ps.v(128, n), wdt.part(jj)[:, dt_ * 128:(dt_ + 1) * 128], at.part(jj)[:, n0:n1],
                                start=(jj == 0), stop=(jj == 1))
                    xc = V(self.xT.ap[:, dt_, n0:n1], self.xres(dt_, 1, n0, n1))
                    if g == 0:
                        self.stt(xc, xc, ALPHA, ps.v(128, n), ALU.mult, ALU.add)
                    else:
                        self.tt(xc, xc, ps.v(128, n), ALU.add)
        ar.release(m0)
        for (n0, n1) in SBS:
            self.layer_norm(n0, n1 - n0, 2, 3)

    def out_norm_gate(self, ot, gate, nw, C, dst_oT, f0, u0):
        ar = self.ar
        m0 = ar.mark()
        o = V(ot.ap[0:C], ot.res)
        sq = ar.alloc([64, 8, 64], BF16)
        ss = ar.alloc([64, 8])
        self.tt(V(sq.ap[0:C], sq.res), o, o, ALU.mult, eng="pool")
        self.red(V(ss.ap[0:C], ss.res), V(sq.ap[0:C], sq.res), ALU.add)
        self.rsqrt(V(ss.ap[0:C], ss.res), V(ss.ap[0:C], ss.res), self.eps6[0:C, 0:1], scale=1.0 / 64)
        self.tt(o, o, V(ss.ap[0:C].unsqueeze(2).to_broadcast([C, 8, 64]), ss.res), ALU.mult)
        self.tt(o, o, V(nw.ap[0:C].unsqueeze(1).to_broadcast([C, 8, 64]), nw.res), ALU.mult, eng="pool")
        self.tt(o, o, gate, ALU.mult)
        self.to_oT(ot, C, dst_oT, f0, u0)
        ar.release(m0)

    def to_oT(self, y, C, dst_oT, f0, u0):
        ps = self.ps()
        y2 = V(y.ap[0:C].rearrange("p a b -> p (a b)") if len(y.ap.shape) == 3 else y.ap[0:C], y.res)
        for k in range(4):
            self.tr(ps.v(128, C, off=k * C), V(y2.ap[:, k * 128:(k + 1) * 128], y.res), self.cv(C_ID, C, parts=C))
        self.cp(V(dst_oT.ap[:, f0:f0 + 4, u0:u0 + C], dst_oT.res), ps.v(128, 4, C))

    def attend(self, qT, q0, nq, blocks, o_dst):
        ar = self.ar
        m0 = ar.mark()
        pts = [ar.alloc([128, 4, 64], BF16) for _ in range(3)]
        oacc = ar.alloc([64, 8, 65])
        nb = len(blocks)
        items = [(h, g0) for h in range(8) for g0 in range(0, nb, 4)]
        po_of = {}

        def emit_pv(p):
            h, g0, pt, grp = p
            ov = po_of[h].v(nq, 65)
            for bi, bl in enumerate(grp):
                nk = bl["nk"]
                gi = g0 + bi
                self.mm(ov, V(pt.ap[0:nk, bi, 0:nq], pt.res), V(bl["v"].ap[:, h, 0:65], bl["v"].res),
                        start=(gi == 0), stop=(gi == nb - 1))
            if g0 + 4 >= nb:
                self.cp(V(oacc.ap[0:nq, h, :], oacc.res), ov)

        pend = None
        for it, (h, g0) in enumerate(items):
            f, b0 = h // 2, (h % 2) * 64
            if g0 == 0:
                po_of[h] = self.ps_po()
            grp = blocks[g0:g0 + 4]
            ps = self.ps()
            for bi, bl in enumerate(grp):
                nk = bl["nk"]
                kt, kc0 = bl["kT"]
                o = ps.v(nk, nq, off=bi * 64)
                last = "q"
                if bl.get("mask") is not None:
                    last = "m"
                elif bl.get("bias") is not None:
                    last = "b"
                self.mm(o, V(kt.ap[:, f, kc0:kc0 + nk], kt.res), V(qT.ap[:, h, q0:q0 + nq], qT.res),
                        start=True, stop=(last == "q"))
                if bl.get("bias") is not None:
                    self.mm(o, self.idb[:, 0:nk], V(bl["bias"].ap[:, h, :], bl["bias"].res), start=False, stop=(last == "b"))
                if bl.get("mask") is not None:
                    self.mm(o, self.idb[:, 0:nk], bl["mask"], start=False, stop=True)
            pt = pts[it % 3]
            nks_ = set(bl["nk"] for bl in grp)
            if len(nks_) == 1 and len(grp) > 1:
                nk = grp[0]["nk"]
                ng = len(grp)
                self.act(V(pt.ap[0:nk, 0:ng, 0:nq], pt.res),
                         V(ps.t[0:nk, 0:ng * 64].rearrange("p (g c) -> p g c", g=ng)[:, :, 0:nq], ps.res), AF.Exp)
            else:
                for bi, bl in enumerate(grp):
                    nk = bl["nk"]
                    self.act(V(pt.ap[0:nk, bi, 0:nq], pt.res), ps.v(nk, nq, off=bi * 64), AF.Exp)
            if pend is not None:
                emit_pv(pend)
            pend = (h, g0, pt, grp)
        emit_pv(pend)
        rec = ar.alloc([64, 8])
        self.recip(V(rec.ap[0:nq], rec.res), V(oacc.ap[0:nq, :, 64], oacc.res))
        self.tt(o_dst, V(oacc.ap[0:nq, :, 0:64], oacc.res),
                V(rec.ap[0:nq].unsqueeze(2).to_broadcast([nq, 8, 64]), rec.res), ALU.mult)
        ar.release(m0)

    def dsa_mask(self, qiT, q0, nq, wi, ksegs, L, maskT, blocks_nk):
        ar = self.ar
        m0 = ar.mark()
        w0o, w1o = self.wstage[0].off, self.wstage[1].off
        sc = self.tile_at(w0o, [64, L], F32)
        tmp = [self.tile_at(w1o, [64, 512], F32), self.tile_at(w1o + 2048, [64, 512], F32)]
        ti = 0
        c0 = 0
        for (kt, kc0, n) in ksegs:
            for s0 in range(0, n, 512):
                sn = min(512, n - s0)
                for hn in range(8):
                    f, b0 = hn // 2, (hn % 2) * 64
                    ps = self.ps()
                    self.mm(ps.v(nq, sn), V(qiT.ap[b0:b0 + 64, f, q0:q0 + nq], qiT.res),
                            V(kt.ap[b0:b0 + 64, kc0 + s0:kc0 + s0 + sn], kt.res))
                    scv = V(sc.ap[0:nq, c0 + s0:c0 + s0 + sn], sc.res)
                    if hn == 0:
                        t = tmp[ti % 2]; ti += 1
                        self.act(V(t.ap[0:nq, 0:sn], t.res), ps.v(nq, sn), AF.Relu)
                        self.ts(scv, V(t.ap[0:nq, 0:sn], t.res), V(wi.ap[:, hn:hn + 1], wi.res), ALU.mult)
                    else:
                        t = tmp[ti % 2]; ti += 1
                        self.act(V(t.ap[0:nq, 0:sn], t.res), ps.v(nq, sn), AF.Relu)
                        self.stt(scv, V(t.ap[0:nq, 0:sn], t.res), V(wi.ap[:, hn:hn + 1], wi.res), scv, ALU.mult, ALU.add)
            c0 += n
        scv = V(sc.ap[0:nq, 0:L], sc.res)
        NIT = 16
        sm = ar.alloc([64, 8])
        wh = ar.alloc([64, 32])
        col = lambda k: V(sm.ap[0:nq, k:k + 1], sm.res)
        lo, w, mid, cnt, t2 = col(0), col(1), col(2), col(3), col(4)
        junk = self.tile_at(w1o + 4096, [64, L], BF16)
        jv = V(junk.ap[0:nq, 0:L], junk.res)
        self.red(col(5), scv, ALU.max)
        self.red(lo, scv, ALU.min)
        self.ts(lo, lo, -1.0, ALU.add)
        self.tt(w, col(5), lo, ALU.subtract)
        self.ts(V(wh.ap[0:nq, 0:NIT], wh.res), self.cv(C_POW2, NIT, parts=nq), w, ALU.mult)
        self.tt(mid, lo, V(wh.ap[0:nq, 0:1], wh.res), ALU.add)
        for it in range(NIT):
            self.ts(jv, scv, mid, ALU.is_gt, op1=ALU.add, accum=cnt)
            self.ts(t2, cnt, 255.5, ALU.is_gt, s2=0.5, op1=ALU.subtract)
            self.stt(mid, t2, V(wh.ap[0:nq, it:it + 1], wh.res), mid, ALU.mult, ALU.add)
        lo = mid
        self.ts(scv, scv, lo, ALU.is_le, s2=-BIG, op1=ALU.mult)
        k0 = 0
        bi = 0
        while bi < len(blocks_nk):
            ps = self.ps()
            grp = blocks_nk[bi:bi + 4]
            kk = k0
            for gi, nk in enumerate(grp):
                self.tr(ps.v(nk, nq, off=gi * 64), V(sc.ap[0:nq, kk:kk + nk], sc.res), self.cv(C_ID, nq, parts=nq))
                kk += nk
            for gi, nk in enumerate(grp):
                self.cp(V(maskT.ap[0:nk, bi + gi, 0:nq], maskT.res), ps.v(nk, nq, off=gi * 64))
            k0 = kk
            bi += 4
        ar.release(m0)

    def gdn_unit(self, C, u0, qnT, knT, vT, gates, zs, S32, Sb, par, oT):
        ar = self.ar
        m0 = ar.mark()
        MUo, SLo, IRo = (C_MU64, C_SL64, C_IR64) if C == 64 else (C_MU32, C_SL32, C_IR32)
        MU2 = self.cv(MUo, 8 * C, parts=C)
        SL = self.cv3(SLo, C, 8, C)
        IR = self.cv3(IRo, C, 8, C)
        idC = self.cv(C_ID, C, parts=C)
        onesC = self.cv(C_ONES, C, parts=C)
        negea, dtb, nwb = par

        def al(dt=F32):
            t = ar.alloc([64, 8, 64], dt)
            return t

        def v3(t, p=C, n=C):
            return V(t.ap[0:p, :, 0:n], t.res)

        def bc(v8, p=C, n=C):
            return V(v8.ap.unsqueeze(2).to_broadcast([p, 8, n]), v8.res)

        st = ar.alloc([64, 48])
        sc_ = lambda k, p=C: V(st.ap[0:p, 8 * k:8 * k + 8], st.res)
        g8, beta, gc, egc, eglgc = sc_(0), sc_(1), sc_(2), sc_(3), sc_(4)
        eglf = ar.alloc([128, 8])
        egl2 = ar.alloc([128, 4])
        bg_ = ar.alloc([64, 8])
        bege = V(bg_.ap[0:C], bg_.res)
        self.tt(g8, V(gates.ap[:, 0:8], gates.res), V(dtb.ap[0:C], dtb.res), ALU.add)
        self.act(g8, g8, AF.Exp)
        self.act(g8, g8, AF.Ln, bias=self.one1[0:C, 0:1])
        self.tt(g8, g8, V(negea.ap[0:C], negea.res), ALU.mult)
        self.act(beta, V(gates.ap[:, 8:16], gates.res), AF.Sigmoid)
        p1 = self.ps()
        self.mm(p1.v(C, 8), self.cv(C_UTRI, C, parts=C), g8)
        self.mm(p1.v(128, 8, off=64), self.cv(C_ONES, 128, parts=C), g8)
        self.cp(gc, p1.v(C, 8), "dve")
        self.act(egc, p1.v(C, 8), AF.Exp)
        self.tt(eglgc, p1.v(C, 8, off=64), gc, ALU.subtract)
        self.act(eglgc, eglgc, AF.Exp)
        self.act(eglf.all(), p1.v(128, 8, off=64), AF.Exp)
        for t_ in range(2):
            self.cp(V(egl2.ap[64 * t_:64 * t_ + 64, :], egl2.res),
                    V(eglf.ap[64 * t_:64 * t_ + 64, :].rearrange("p (f t) -> p f t", t=2)[:, :, t_], eglf.res), "dve")
        self.tt(bege, beta, egc, ALU.mult)
        pk = self.ps()
        pv = self.ps()
        for f in range(4):
            self.tr(pk.vb(C, 128, off=f * 128), V(knT.ap[:, f, u0:u0 + C], knT.res), self.idb.all())
            self.tr(pv.vb(C, 128, off=f * 128), V(vT.ap[:, f, u0:u0 + C], vT.res), self.idb.all())
        kbe, kdec, vb = al(BF16), al(BF16), al(BF16)
        self.tt(v3(kbe, C, 64), pk.vb(C, 8, 64), bc(bege, C, 64), ALU.mult)
        self.tt(v3(kdec, C, 64), pk.vb(C, 8, 64), bc(eglgc, C, 64), ALU.mult)
        self.tt(v3(vb, C, 64), pv.vb(C, 8, 64), bc(beta, C, 64), ALU.mult)
        pkk = (self.ps(), self.ps())
        pkq = (self.ps(), self.ps())
        for h in range(8):
            f, b0 = h // 2, (h % 2) * 64
            kh = V(knT.ap[b0:b0 + 64, f, u0:u0 + C], knT.res)
            qh = V(qnT.ap[b0:b0 + 64, f, u0:u0 + C], qnT.res)
            self.mm(pkk[h % 2].v(C, C, off=f * C), kh, kh)
            self.mm(pkq[h % 2].v(C, C, off=f * C), qh, kh)

        def par(v, p_, n):
            return V(v.ap.rearrange("p (f t) c -> p f t c", t=2)[:, :, p_, :], v.res)
        diag = al()
        self.tt(v3(diag), IR, bc(gc), ALU.mult)
        pG = self.ps()
        self.mm(pG.v(C, 8 * C), idC, MU2, start=True, stop=False)
        for h in range(8):
            self.mm(pG.v(C, C, off=h * C), onesC, V(diag.ap[0:C, h, 0:C], diag.res), start=False, stop=(h == 7))
        Dm = diag
        self.tt(v3(Dm), bc(gc), pG.v(C, 8, C), ALU.subtract)
        self.act(v3(Dm), v3(Dm), AF.Exp)
        A32, at32 = al(), al()
        for p_ in range(2):
            self.tt(par(v3(A32), p_, C), pkk[p_].v(C, 4, C), par(v3(Dm), p_, C), ALU.mult)
        self.tt(v3(A32), v3(A32), bc(beta), ALU.mult)
        self.tt(v3(A32), v3(A32), SL, ALU.mult, eng="pool")
        for p_ in range(2):
            self.tt(par(v3(at32), p_, C), pkq[p_].v(C, 4, C), par(v3(Dm), p_, C), ALU.mult)
        pB = self.ps()
        pT = self.ps()
        for h in range(8):
            self.tr(pB.v(C, C, off=h * C), V(A32.ap[0:C, h, 0:C], A32.res), idC)
            self.tr(pT.v(C, C, off=h * C), V(at32.ap[0:C, h, 0:C], at32.res), idC)
        Ab = [al(BF16), al(BF16)]
        Bb = [al(BF16), al(BF16)]
        Pb = [al(BF16), al(BF16)]
        atT = al(BF16)
        self.cp(v3(Ab[0]), v3(A32), "pool")
        self.cp(v3(Bb[0]), pB.v(C, 8, C), "act")
        self.cp(v3(atT), pT.v(C, 8, C), "act")
        self.tt(v3(Pb[0]), IR, pB.v(C, 8, C), ALU.subtract)
        M = 5 if C == 64 else 4
        cur = 0
        for m in range(1, M + 1):
            nxt = 1 - cur
            pA = self.ps()
            for h in range(8):
                self.mm(pA.v(C, C, off=h * C), V(Bb[cur].ap[0:C, h, 0:C], Bb[cur].res), V(Ab[cur].ap[0:C, h, 0:C], Ab[cur].res))
            if m < M:
                pBm = self.ps()
                for h in range(8):
                    self.mm(pBm.v(C, C, off=h * C), V(Ab[cur].ap[0:C, h, 0:C], Ab[cur].res), V(Bb[cur].ap[0:C, h, 0:C], Bb[cur].res))
            self.cp(v3(Ab[nxt]), pA.v(C, 8, C), "act")
            if m < M:
                self.cp(v3(Bb[nxt]), pBm.v(C, 8, C), "dve")
            pP = self.ps()
            for h in range(8):
                self.mm(pP.v(C, C, off=h * C), V(Ab[nxt].ap[0:C, h, 0:C], Ab[nxt].res), V(Pb[cur].ap[0:C, h, 0:C], Pb[cur].res))
            self.tt(v3(Pb[nxt]), v3(Pb[cur]), pP.v(C, 8, C), ALU.add)
            cur = nxt
        P_ = Pb[cur]
        pval = self.ps()
        pkc = self.ps()
        for h in range(8):
            Ph = V(P_.ap[0:C, h, 0:C], P_.res)
            self.mm(pval.v(C, 64, off=h * 64), Ph, V(vb.ap[0:C, h, :], vb.res))
            self.mm(pkc.v(64, C, off=(h // 2) * C, p0=(h % 2) * 64), V(kbe.ap[0:C, h, :], kbe.res), Ph)
        val32 = A32
        kcT = ar.alloc([128, 4, 64], BF16)
        self.cp(v3(val32, C, 64), pval.v(C, 8, 64), "act")
        self.cp(V(kcT.ap[:, :, 0:C], kcT.res), pkc.v(128, 4, C), "dve")
        pks = (self.ps(), self.ps())
        pqs = (self.ps(), self.ps())
        for h in range(8):
            f, b0 = h // 2, (h % 2) * 64
            Sh = V(Sb.ap[b0:b0 + 64, f, :], Sb.res)
            self.mm(pks[h % 2].v(C, 64, off=f * 64), V(kcT.ap[b0:b0 + 64, f, 0:C], kcT.res), Sh)
            self.mm(pqs[h % 2].v(C, 64, off=f * 64), V(qnT.ap[b0:b0 + 64, f, u0:u0 + C], qnT.res), Sh)
        vnew = al(BF16)
        o1 = at32
        for p_ in range(2):
            self.tt(par(v3(vnew, C, 64), p_, 64), par(v3(val32, C, 64), p_, 64), pks[p_].v(C, 4, 64), ALU.subtract)
            self.tt(par(v3(o1, C, 64), p_, 64), pqs[p_].v(C, 4, 64),
                    V(egc.ap.rearrange("p (f t) -> p f t", t=2)[:, :, p_].unsqueeze(2).to_broadcast([C, 4, 64]), egc.res), ALU.mult)
        pav = self.ps()
        pds = self.ps()
        for h in range(8):
            vh = V(vnew.ap[0:C, h, :], vnew.res)
            self.mm(pav.v(C, 64, off=h * 64), V(atT.ap[0:C, h, 0:C], atT.res), vh)
            self.mm(pds.v(64, 64, off=(h // 2) * 64, p0=(h % 2) * 64), V(kdec.ap[0:C, h, :], kdec.res), vh)
        self.tt(v3(o1, C, 64), v3(o1, C, 64), pav.v(C, 8, 64), ALU.add)
        self.tt(S32.all(), S32.all(), V(egl2.ap.unsqueeze(2).to_broadcast([128, 4, 64]), egl2.res), ALU.mult, eng="pool")
        self.tt(S32.all(), S32.all(), pds.v(128, 4, 64), ALU.add)
        self.cp(Sb.all(), S32.all(), "act")
        self.out_norm_gate(o1, zs, nwb, C, oT, 4, u0)
        ar.release(m0)

    def even_layer_setup(self, e):
        ar = self.ar
        self.cwb = ar.alloc([128, 12, 4])
        self.load_rows_T(self.cwb.all(), self.d["b_conv_w"].ap()[e], 4, 12)
        self.negea = ar.alloc([64, 8])
        self.dtb = ar.alloc([64, 8])
        self.nwb = ar.alloc([64, 64])
        self.bcast_row(self.negea.all(), self.d["b_a_log"].ap()[e], 64)
        self.bcast_row(self.dtb.all(), self.d["b_dt_bias"].ap()[e], 64)
        self.bcast_row(self.nwb.all(), self.d["b_norm_w"].ap()[e], 64)
        self.act(self.negea.all(), self.negea.all(), AF.Exp)
        self.ts(self.negea.all(), self.negea.all(), -1.0, ALU.mult)
        self.S32 = ar.alloc([128, 4, 64])
        self.Sb = ar.alloc([128, 4, 64], BF16)
        self.memset(self.S32.all(), 0.0, "dve")
        self.memset(self.Sb.all(), 0.0, "dve")
        self.ccarry = ar.alloc([128, 12, 3])
        self.memset(self.ccarry.all(), 0.0, "dve")
        self.shist = ar.alloc([128, 12, 12])
        m1 = ar.mark()
        hrow = ar.alloc([12, 1536])
        self.dma(hrow.all(), self.dv(self.d["sb_conv"].ap()[e].rearrange("s j n -> (s j) n")))
        ps = self.ps()
        for f in range(12):
            self.tr(ps.v(128, 12, off=f * 12), hrow[:, f * 128:(f + 1) * 128], self.cv(C_ID, 12, parts=12))
        self.cp(self.shist.all(), ps.v(128, 12, 12))
        ar.release(m1)
        self.Va = V(self.Vbuf.ap.rearrange("p (t h e) -> p t h e", t=16, h=8), self.Vbuf.res)
        self.memset(self.Vbuf.all(), 1.0, "dve")

    def even_block(self, L, e, blk):
        ar = self.ar
        t0, TB, C, NU, smp = blk.t0, blk.TB, blk.C, blk.NU, blk.sample
        W = self.d["w_in_even"].ap()[e]
        mB = ar.mark()
        oT = ar.alloc([128, 8, TB], BF16)
        qaT = ar.alloc([128, 8, TB], BF16)
        self.memset(qaT.all(), 0.0)
        qiT = ar.alloc([128, 4, TB], BF16)
        wi = ar.alloc([64, NU, 8])
        if smp:
            KTs = ar.alloc([128, 4, TB], BF16)
            kiTs = ar.alloc([128, TB], BF16)
            Vs = ar.alloc([32, NS, 8, 66], BF16)
            self.memset(Vs.all(), 1.0, "dve")
        mG = ar.mark()
        qnT = ar.alloc([128, 4, TB], BF16)
        knT = ar.alloc([128, 4, TB], BF16)
        vT = ar.alloc([128, 4, TB], BF16)
        gates = ar.alloc([64, NU, 16])
        zs = ar.alloc([64, NU, 512])
        mP = ar.mark()
        xb = ar.alloc([128, 8, TB], BF16)
        for ch in range(8):
            self.cp(xb.part(ch).all(), V(self.xT.ap[:, ch, t0:t0 + TB], self.xres(ch, 1, t0, t0 + TB)))
        wts = [ar.alloc([128, 8, 256], BF16), ar.alloc([128, 8, 256], BF16)]
        wn = [0]

        def ldw(c0, n):
            wt = wts[wn[0] % 2]
            wn[0] += 1
            self.load_w(W[:, c0:c0 + n], D, n, wt)
            return wt
        ost = [ar.alloc([128, 256]), ar.alloc([128, 256])]
        osi = [0]
        for (c0, dst) in ((0, qaT), (1536, qiT)):
            for half in range(2):
                wt = ldw(c0 + 256 * half, 256)
                for ff in range(2):
                    ps = self.proj_fm(wt, ff * 128, 128, xb, 0, TB)
                    f_ = half * 2 + ff
                    if dst is qaT:
                        self.act(V(dst.ap[0:64, 2 * f_, :], dst.res), ps.v(64, TB), AF.Copy, scale=0.125)
                        self.act(V(dst.ap[64:128, 2 * f_ + 1, :], dst.res), ps.v(64, TB, p0=64), AF.Copy, scale=0.125)
                    else:
                        self.act(dst.part(f_).all(), ps.v(128, TB), AF.Copy, scale=0.125)
        for half in range(2):
            wt = ldw(512 + 256 * half, 256)
            for ff in range(2):
                f = half * 2 + ff
                ps = self.proj_fm(wt, ff * 128, 128, xb, 0, TB)
                if smp:
                    self.cp(KTs.part(f).all(), ps.v(128, TB))
                else:
                    self.cp(self.KT.part(f).cols(t0, t0 + TB).all(), ps.v(128, TB))
            for tt in range(TB // 128):
                ps = self.proj_tm(wt, 0, 256, xb, tt * 128, tt * 128 + 128)
                o_ = ost[osi[0] % 2]; osi[0] += 1
                self.cp(o_.all(), ps.v(128, 256))
                dst = self.d["a_k_s"].ap()[e] if smp else self.d["a_k_p"].ap()[e][t0 + tt * 128:t0 + tt * 128 + 128]
                self.dma(self.dv(dst[:, 256 * half:256 * half + 256]), o_.all(), eng="pool")
        for half in range(2):
            wt = ldw(1024 + 256 * half, 256)
            for tt in range(TB // 128):
                ps = self.proj_tm(wt, 0, 256, xb, tt * 128, tt * 128 + 128)
                o_ = ost[osi[0] % 2]; osi[0] += 1
                self.cp(o_.all(), ps.v(128, 256), "act")
                dst = self.d["a_v_s"].ap()[e] if smp else self.d["a_v_p"].ap()[e][t0 + tt * 128:t0 + tt * 128 + 128]
                self.dma(self.dv(dst[:, 256 * half:256 * half + 256]), o_.all(), eng="pool")
                if not smp:
                    tg = (t0 + tt * 128) // 128
                    self.cp(V(self.Va.ap[:, tg, 4 * half:4 * half + 4, 0:64], self.Vbuf.res), ps.v(128, 4, 64), "dve")
            if smp:
                for s in range(NS):
                    ps = self.proj_tm(wt, 0, 256, xb, s * TS, s * TS + TS)
                    self.cp(V(Vs.ap[:, s, 4 * half:4 * half + 4, 0:64], Vs.res), ps.v(TS, 4, 64))
        wt = wts[wn[0] % 2]; wn[0] += 1
        st_ = self.wstage[self.weng % 2]; self.weng += 1
        sv = V(st_.ap[:, 0:8 * 72].rearrange("p (k n) -> p k n", k=8), st_.res)
        self.dma(sv, self.dv(W[:, 2048:2120].rearrange("(k p) n -> p k n", p=128)))
        self.cp(V(wt.ap[:, :, 0:64], wt.res), V(sv.ap[:, :, 0:64], st_.res), "pool")
        self.cp(V(wt.ap[:, :, 64:128], wt.res), V(sv.ap[:, :, 0:64], st_.res), "pool")
        self.cp(V(wt.ap[:, :, 128:136], wt.res), V(sv.ap[:, :, 64:72], st_.res), "pool")
        ps = self.proj_fm(wt, 0, 128, xb, 0, TB)
        if smp:
            self.cp(kiTs.all(), ps.v(128, TB))
        else:
            self.cp(self.kiT.cols(t0, t0 + TB).all(), ps.v(128, TB))
        for tt in range(TB // 128):
            ps = self.proj_tm(wt, 0, 64, xb, tt * 128, tt * 128 + 128)
            o_ = ost[osi[0] % 2]; osi[0] += 1
            self.cp(o_[:, 0:64], ps.v(128, 64))
            dst = self.d["a_ki_s"].ap()[e] if smp else self.d["a_ki_p"].ap()[e][t0 + tt * 128:t0 + tt * 128 + 128]
            self.dma(self.dv(dst), o_[:, 0:64], eng="pool")
        for u in range(NU):
            ps = self.proj_tm(wt, 128, 8, xb, u * C, u * C + C)
            self.act(V(wi.ap[0:C, u, :], wi.res), ps.v(C, 8), AF.Copy, scale=8 ** -0.5)
        cb = [ar.alloc([128, 3 + 256 + 16]) for _ in range(2)]
        yb = [ar.alloc([128, 256]) for _ in range(2)]
        sqb = [ar.alloc([128, 256], BF16) for _ in range(2)]
        rtb = [ar.alloc([128, 256]) for _ in range(2)]
        ci = 0
        for g in range(6):
            wt = ldw(2120 + 256 * g, 256)
            for ff in range(2):
                f = g * 2 + ff
                ps = self.proj_fm(wt, ff * 128, 128, xb, 0, TB)
                c_ = cb[ci % 2]; y_ = yb[ci % 2]; s_ = sqb[ci % 2]; ci += 1
                cw = lambda j: V(self.cwb.ap[:, f, j:j + 1], self.cwb.res)
                if not smp:
                    self.cp(c_[:, 3:3 + TB], ps.v(128, TB), "act")
                    self.cp(c_[:, 0:3], V(self.ccarry.ap[:, f, :], self.ccarry.res), "pool")
                    self.cp(V(self.ccarry.ap[:, f, :], self.ccarry.res), c_[:, TB:TB + 3], "pool")
                    hv = lambda j: c_[:, j:j + TB]
                    yv = y_[:, 0:TB]
                else:
                    c3 = V(c_.ap[:, 0:NS * 35].rearrange("p (s c) -> p s c", s=NS), c_.res)
                    self.cp(V(c3.ap[:, :, 3:35], c_.res), ps.v(128, NS, TS), "act")
                    self.cp(V(c3.ap[:, :, 0:3], c_.res), V(self.shist.ap[:, f, :].rearrange("p (s j) -> p s j", s=NS), self.shist.res), "pool")
                    hv = lambda j: V(c3.ap[:, :, j:j + TS], c_.res)
                    yv = V(y_.ap[:, 0:TB].rearrange("p (s c) -> p s c", s=NS), y_.res)
                self.ts(yv, hv(0), cw(0), ALU.mult)
                for j in range(1, 4):
                    self.stt(yv, hv(j), cw(j), yv, ALU.mult, ALU.add)
                y2 = y_[:, 0:TB]
                if f >= 8:
                    self.act(vT.part(f - 8).all(), y2, AF.Silu)
                else:
                    self.act(y2, y2, AF.Silu)
                    self.act(s_[:, 0:TB], y2, AF.Square)
                    pq = self.ps()
                    self.mm(pq.v(128, TB), self.blkb.all(), s_[:, 0:TB])
                    rt = rtb[ci % 2]
                    self.rsqrt(rt[:, 0:TB], pq.v(128, TB), self.eps6[:, 0:1])
                    if f < 4:
                        self.stt(qnT.part(f).all(), y2, 0.125, rt[:, 0:TB], ALU.mult, ALU.mult)
                    else:
                        self.tt(knT.part(f - 4).all(), y2, rt[:, 0:TB], ALU.mult)
        if smp or blk.idx == 7:
            nsel = 16 if smp else 4
            xsel = ar.alloc([128, 8, 16], BF16)
            if smp:
                for s in range(NS):
                    self.cp(V(xsel.ap[:, :, 4 * s:4 * s + 4], xsel.res), V(xb.ap[:, :, s * TS + TS - 4:s * TS + TS], xb.res), "dve")
            else:
                self.cp(V(xsel.ap[:, :, 0:4], xsel.res), V(xb.ap[:, :, TB - 4:TB], xb.res), "dve")
            for g in range(6):
                wt = ldw(2120 + 256 * g, 256)
                ps = self.ps()
                for ch in range(8):
                    self.mm(ps.v(nsel, 256), V(xsel.ap[:, ch, 0:nsel], xsel.res), wt.part(ch).all(), start=(ch == 0), stop=(ch == 7))
                o_ = ost[osi[0] % 2]; osi[0] += 1
                self.cp(o_[0:nsel, :], ps.v(nsel, 256))
                if smp:
                    for s in range(NS):
                        self.dma(self.dv(self.d["b_conv_s"].ap()[e, s][:, 256 * g:256 * g + 256]), o_[4 * s + 1:4 * s + 4, :], eng="pool")
                else:
                    self.dma(self.dv(self.d["b_conv_p"].ap()[e][:, 256 * g:256 * g + 256]), o_[1:4, :], eng="pool")
        wt = wts[wn[0] % 2]; wn[0] += 1
        st_ = self.wstage[self.weng % 2]; self.weng += 1
        sv = V(st_.ap[:, 0:8 * 16].rearrange("p (k n) -> p k n", k=8), st_.res)
        self.dma(sv, self.dv(W[:, 3656:3672].rearrange("(k p) n -> p k n", p=128)))
        self.cp(V(wt.ap[:, :, 0:16], wt.res), sv, "pool")
        for u in range(NU):
            ps = self.proj_tm(wt, 0, 16, xb, u * C, u * C + C)
            self.cp(V(gates.ap[0:C, u, :], gates.res), ps.v(C, 16), "dve")
        for half in range(2):
            wt = ldw(3672 + 256 * half, 256)
            for u in range(NU):
                ps = self.proj_tm(wt, 0, 256, xb, u * C, u * C + C)
                self.act(V(zs.ap[0:C, u, 256 * half:256 * half + 256], zs.res), ps.v(C, 256), AF.Silu)
        ar.release(mP)
        if blk.idx == 6:
            self.mark_("b6pro%d" % L)
        par = (self.negea, self.dtb, self.nwb)

        def gdn_all():
            self.bset = ("A", [0, 1, 2, 3, 4])
            for u in range(NU):
                if smp:
                    for t_ in range(2):
                        self.dma(V(self.S32.ap[64 * t_:64 * t_ + 64], self.S32.res),
                                 self.dv(self.d["sb_s"].ap()[e, u].rearrange("(f t) k v -> t k f v", t=2)[t_]))
                    self.cp(self.Sb.all(), self.S32.all(), "act")
                self.gdn_unit(C, u * C, qnT, knT, vT, V(gates.ap[0:C, u, :], gates.res),
                              V(zs.ap[0:C, u, :].rearrange("p (h e) -> p h e", h=8), zs.res), self.S32, self.Sb, par, oT)
                if smp:
                    for t_ in range(2):
                        self.dma(self.dv(self.d["b_s_s"].ap()[e, u].rearrange("(f t) k v -> t k f v", t=2)[t_]),
                                 V(self.S32.ap[64 * t_:64 * t_ + 64], self.S32.res), eng="pool")
            if blk.idx == 7:
                for t_ in range(2):
                    self.dma(self.dv(self.d["b_s_p"].ap()[e].rearrange("(f t) k v -> t k f v", t=2)[t_]),
                             V(self.S32.ap[64 * t_:64 * t_ + 64], self.S32.res), eng="pool")

        def dsa_all():
            self.bset = ("B", [5, 6])
            for u in range(NU):
                if smp:
                    self.dsa_sample_unit(e, u, qaT, qiT, wi, KTs, kiTs, Vs, oT)
                else:
                    self.dsa_prompt_unit(t0 // 64 + u, u, qaT, qiT, wi, oT)

        if smp:
            gdn_all()
            ar.release(mG)
            dsa_all()
        else:
            self.parallel(gdn_all, dsa_all)
            ar.release(mG)
        self.bset = None
        if blk.idx == 6:
            self.mark_("b6units%d" % L)
        self.out_proj_ln(self.d["w_out_even"].ap()[e], oT, t0, TB)
        ar.release(mB)

    def dsa_prompt_unit(self, c, u, qaT, qiT, wi, oT):
        ar = self.ar
        m0 = ar.mark()
        L = 64 * (c + 1)
        nb = (L + 127) // 128
        nks = [128] * (nb - 1) + [L - 128 * (nb - 1)]
        q0 = u * 64
        maskT = None
        if L > 256:
            maskT = ar.alloc([128, nb, 64], BF16)
            self.memset(maskT.all(), 0.0)
            self.dsa_mask(qiT, q0, 64, V(wi.ap[0:64, u, :], wi.res), [(self.kiT, 0, L)], L, maskT, nks)
        blocks = []
        for j in range(nb):
            nk = nks[j]
            di = min(c - 2 * j, 4)
            blocks.append(dict(kT=(self.KT, 128 * j), nk=nk, v=V(self.Va.ap[0:nk, j], self.Vbuf.res),
                               bias=V(self.t5t.ap[:, :, di, 0:64], self.t5t.res),
                               mask=(V(maskT.ap[:, j, 0:64], maskT.res) if maskT is not None else None)))
        oa = ar.alloc([64, 8, 64])
        self.attend(qaT, q0, 64, blocks, V(oa.ap[0:64], oa.res))
        self.to_oT(oa, 64, oT, 0, q0)
        ar.release(m0)

    def load_cache_T(self, src2d, nkeys, dstT, stg):
        for j in range(nkeys // 128):
            st = stg[j % len(stg)]
            self.dma(st.all(), self.dv(src2d[128 * j:128 * j + 128, :]))
            ps = self.ps()
            for f in range(4):
                self.tr(ps.v(128, 128, off=f * 128), st[:, f * 128:(f + 1) * 128], self.idf)
            self.cp(V(dstT.ap[:, :, 128 * j:128 * j + 128], dstT.res), ps.v(128, 4, 128))

    def dsa_sample_unit(self, e, s, qaT, qiT, wi, KTs, kiTs, Vs, oT):
        ar = self.ar
        m0 = ar.mark()
        KTc = ar.alloc([128, 4, PAST], BF16)
        Vc = ar.alloc([128, 8, 8, 66], BF16)
        kiTc = ar.alloc([128, PAST], BF16)
        self.memset(Vc.all(), 1.0, "dve")
        m1 = ar.mark()
        stg = [ar.alloc([128, 512]) for _ in range(3)]
        kstg = ar.alloc([128, 8, 128])
        self.load_cache_T(self.d["ca_k"].ap()[e, s], PAST, KTc, stg)
        for j in range(8):
            st = stg[j % 3]
            self.dma(st.all(), self.dv(self.d["ca_v"].ap()[e, s][128 * j:128 * j + 128, :]))
            self.cp(V(Vc.ap[:, j, :, 0:64], Vc.res), V(st.ap.rearrange("p (h e) -> p h e", h=8), st.res))
        kis = self.d["ca_ki"].ap()[e, s].rearrange("(j p) d -> p j d", p=128)
        self.dma(V(kstg.ap[:, :, 0:64], kstg.res), self.dv(kis))
        self.dma(V(kstg.ap[:, :, 64:128], kstg.res), self.dv(kis))
        for g in range(2):
            ps = self.ps()
            for k in range(4):
                j = 4 * g + k
                self.tr(ps.v(128, 128, off=k * 128), V(kstg.ap[:, j, :], kstg.res), self.idf)
            self.cp(kiTc[:, 512 * g:512 * g + 512], ps.v(128, 512))
        ar.release(m1)
        L = PAST + TS
        nks = [128] * 8 + [TS]
        q0 = s * TS
        maskT = ar.alloc([128, 9, TS], BF16)
        self.memset(maskT.all(), 0.0)
        self.dsa_mask(qiT, q0, TS, V(wi.ap[0:TS, s, :], wi.res), [(kiTc, 0, PAST), (kiTs, q0, TS)], L, maskT, nks)
        blocks = []
        for j in range(8):
            di = min(16 - 2 * j, 4)
            blocks.append(dict(kT=(KTc, 128 * j), nk=128, v=V(Vc.ap[:, j], Vc.res),
                               bias=V(self.t5t.ap[:, :, di, 0:TS], self.t5t.res),
                               mask=V(maskT.ap[:, j, :], maskT.res)))
        blocks.append(dict(kT=(KTs, q0), nk=TS, v=V(Vs.ap[:, s], Vs.res),
                           bias=V(self.t5t.ap[:, :, 0, 0:TS], self.t5t.res),
                           mask=V(maskT.ap[:, 8, :], maskT.res)))
        oa = ar.alloc([64, 8, 64])
        self.attend(qaT, q0, TS, blocks, V(oa.ap[0:TS], oa.res))
        self.to_oT(oa, TS, oT, 0, q0)
        ar.release(m0)

    def out_proj_ln(self, Wout, oT, t0, TB):
        ar = self.ar
        m0 = ar.mark()
        if self.dbg:
            tmp = [ar.alloc([128, TB]), ar.alloc([128, TB])]
            for ch in range(8):
                self.cp(tmp[ch % 2].all(), oT.part(ch).all(), "dve")
                self.dma(self.dv(self.d["dbg_o"].ap()[self.curL][:, ch, t0:t0 + TB]), tmp[ch % 2].all(), eng="pool")
        wts = [ar.alloc([128, 8, 128], BF16) for _ in range(2)]
        for dt_ in range(8):
            wt = wts[dt_ % 2]
            self.load_w(Wout[:, 128 * dt_:128 * dt_ + 128], D, 128, wt)
            ps = self.ps()
            for ch in range(8):
                self.mm(ps.v(128, TB), wt.part(ch).all(), oT.part(ch).all(), start=(ch == 0), stop=(ch == 7))
            xc = V(self.xT.ap[:, dt_, t0:t0 + TB], self.xres(dt_, 1, t0, t0 + TB))
            self.stt(xc, xc, ALPHA, ps.v(128, TB), ALU.mult, ALU.add)
        ar.release(m0)
        self.layer_norm(t0, TB, 0, 1)

    def write_y(self):
        ar = self.ar
        m0 = ar.mark()
        yo = [ar.alloc([128, D]), ar.alloc([128, D])]
        for tt in range(17):
            y_ = yo[tt % 2]
            for g in range(2):
                ps = self.ps()
                for k in range(4):
                    ch = 4 * g + k
                    self.tr(ps.v(128, 128, off=k * 128),
                            V(self.xT.ap[:, ch, tt * 128:(tt + 1) * 128], self.xres(ch, 1, tt * 128, (tt + 1) * 128)), self.idf)
                self.cp(y_[:, 512 * g:512 * g + 512], ps.v(128, 512))
            dst = self.d["y_p"].ap()[tt * 128:(tt + 1) * 128, :] if tt < 16 else self.d["y_s"].ap()
            self.dma(self.dv(dst), y_.all(), eng="pool")
        ar.release(m0)

    def mark_(self, label):
        self.marks.append((label, len(self.P.ops)))

    def build(self):
        import os
        self.marks = []
        self.decl()
        self.setup()
        self.mark_("setup")
        ar = self.ar
        for L in range(self.n_layers):
            mL = ar.mark()
            self.curL = L
            self.load_ln(L)
            if L % 2 == 0:
                self.even_layer_setup(L // 2)
                self.mark_("evsetup%d" % L)
                for blk in BLOCKS:
                    self.even_block(L, L // 2, blk)
                    self.mark_("L%db%d" % (L, blk.idx))
            else:
                self.odd_layer_setup(L // 2)
                for blk in BLOCKS:
                    self.odd_block(L, L // 2, blk)
            ar.release(mL)
            self.mark_("mix%d" % L)
            if self.dbg:
                for ch in range(8):
                    self.dma(self.dv(self.d["dbg_xm"].ap()[L][:, ch, :]), V(self.xT.ap[:, ch, :], self.xT.part(ch).res), eng="pool")
            self.ffn(L)
            if self.dbg:
                for ch in range(8):
                    self.dma(self.dv(self.d["dbg_xo"].ap()[L][:, ch, :]), V(self.xT.ap[:, ch, :], self.xT.part(ch).res), eng="pool")
            self.mark_("ffn%d" % L)
        self.write_y()
        self.mark_("end")
        tr_ = os.environ.get("MK_TRUNC")
        if tr_:
            n = dict(self.marks)[tr_] if tr_ in dict(self.marks) else int(tr_)
            self.P.ops = self.P.ops[:n]
        print("marks", self.marks, flush=True)
        return self.P.emit(self.st)


_CONSTS = None


def _consts():
    global _CONSTS
    if _CONSTS is None:
        _CONSTS = make_consts()
    return _CONSTS


IN_SHARD = {
    "xp": ("x_prompt", lambda i, a: a[i]),
    "xs": ("x_sample", lambda i, a: a[4 * i:4 * i + 4].reshape(NS * TS, D)),
    "ca_k": ("cache_a_k", lambda i, a: a[:, 4 * i:4 * i + 4].reshape(2, NS, PAST, 512)),
    "ca_v": ("cache_a_v", lambda i, a: a[:, 4 * i:4 * i + 4].reshape(2, NS, PAST, 512)),
    "ca_ki": ("cache_a_kidx", lambda i, a: a[:, 4 * i:4 * i + 4]),
    "sb_s": ("state_b_s", lambda i, a: a[:, 4 * i:4 * i + 4]),
    "sb_conv": ("state_b_conv", lambda i, a: a[:, 4 * i:4 * i + 4]),
    "sc_c": ("state_c_c", lambda i, a: a[:, 4 * i:4 * i + 4]),
    "sc_n": ("state_c_n", lambda i, a: a[:, 4 * i:4 * i + 4]),
    "sc_m": ("state_c_m", lambda i, a: a[:, 4 * i:4 * i + 4]),
    "cd_k": ("cache_d_k", lambda i, a: a[:, 4 * i:4 * i + 4].reshape(2, NS, 512, 512)),
    "cd_v": ("cache_d_v", lambda i, a: a[:, 4 * i:4 * i + 4].reshape(2, NS, 512, 512)),
    "sf_conv": ("state_ffn_conv", lambda i, a: a[:, 4 * i:4 * i + 4]),
}
W_NAMES = ["w_in_even", "w_out_even", "t5_bias", "b_conv_w", "b_a_log", "b_dt_bias", "b_norm_w", "w_in_odd", "w_out_odd",
           "c_i_bias", "c_f_bias", "c_norm_w", "d_rel_bias", "ln_mix_g", "ln_mix_b", "ln_ffn_g", "ln_ffn_b",
           "ffn_w_up", "ffn_conv_w", "ffn_w_down"]

OUTS = [("y_p", (8, SEQ, D), "p", 0), ("y_s", (32, TS, D), "s", 0),
        ("a_k_p", (2, 8, SEQ, 8, 64), "p", 2), ("a_k_s", (2, 32, TS, 8, 64), "s", 2),
        ("a_v_p", (2, 8, SEQ, 8, 64), "p", 2), ("a_v_s", (2, 32, TS, 8, 64), "s", 2),
        ("a_ki_p", (2, 8, SEQ, 64), "p", 2), ("a_ki_s", (2, 32, TS, 64), "s", 2),
        ("b_s_p", (2, 8, 8, 64, 64), "p", 2), ("b_s_s", (2, 32, 8, 64, 64), "s", 2),
        ("b_conv_p", (2, 8, 3, 1536), "p", 2), ("b_conv_s", (2, 32, 3, 1536), "s", 2),
        ("c_c_p", (2, 8, 8, 32, 64), "p", 2), ("c_c_s", (2, 32, 8, 32, 64), "s", 2),
        ("c_n_p", (2, 8, 8, 32), "p", 2), ("c_n_s", (2, 32, 8, 32), "s", 2),
        ("c_m_p", (2, 8, 8), "p", 2), ("c_m_s", (2, 32, 8), "s", 2),
        ("d_k_p", (2, 8, 512, 8, 64), "p", 2), ("d_k_s", (2, 32, TS, 8, 64), "s", 2),
        ("d_v_p", (2, 8, 512, 8, 64), "p", 2), ("d_v_s", (2, 32, TS, 8, 64), "s", 2),
        ("f_conv_p", (4, 8, 2, 2 * DFF), "p", 4), ("f_conv_s", (4, 32, 2, 2 * DFF), "s", 4)]


def build_nc(n_layers=DEPTH):
    nc = bass.Bass("TRN2", target_bir_lowering=False)
    st = ExitStack()
    mk = MK(nc, st, n_layers)
    stats = mk.build()
    return nc, stats, st


def make_in_maps(inputs):
    cst, oh_t5, oh_rel = _consts()
    maps = []
    for i in range(8):
        m = {}
        for dn, (kn, fn) in IN_SHARD.items():
            m[dn] = np.ascontiguousarray(fn(i, inputs[kn]), dtype=np.float32)
        for w in W_NAMES:
            m[w] = np.ascontiguousarray(inputs[w], dtype=np.float32)
        m["cst"] = cst
        m["oh_t5"] = oh_t5
        m["oh_rel"] = oh_rel
        maps.append(m)
    return maps


def gather_outputs(results):
    outs = []
    for (dn, shape, grp, ld) in OUTS:
        full = np.zeros(shape, np.float32)
        for i in range(8):
            r = np.asarray(results[i][dn], dtype=np.float32)
            if ld == 0:
                if grp == "p":
                    full[i] = r.reshape(shape[1:])
                else:
                    full[4 * i:4 * i + 4] = r.reshape((4,) + shape[1:])
            else:
                if grp == "p":
                    full[:, i] = r.reshape((shape[0],) + shape[2:])
                else:
                    full[:, 4 * i:4 * i + 4] = r.reshape((shape[0], 4) + shape[2:])
        outs.append(full)
    return tuple(outs)


def kernel(**inputs):
    inputs = {k: np.asarray(v) for k, v in inputs.items()}
    nc, stats, st = build_nc()
    in_maps = make_in_maps(inputs)
    res = run_bass_kernel_spmd(nc, in_maps, core_ids=list(range(8)))
    return gather_outputs(res.results)


def _odd_layer_setup(self, o):
    ar = self.ar
    self.band_tiles(o)
    self.ibias = ar.alloc([64, 8])
    self.fbias = ar.alloc([64, 8])
    self.nwc = ar.alloc([64, 64])
    self.bcast_row(self.ibias.all(), self.d["c_i_bias"].ap()[o], 64)
    self.bcast_row(self.fbias.all(), self.d["c_f_bias"].ap()[o], 64)
    self.bcast_row(self.nwc.all(), self.d["c_norm_w"].ap()[o], 64)
    self.Ca32 = ar.alloc([128, 4, 66])
    self.Cab = ar.alloc([128, 4, 66], BF16)
    self.mprev = ar.alloc([64, 8])
    self.memset(self.Ca32.all(), 0.0)
    self.memset(self.Cab.all(), 0.0)
    self.memset(self.mprev.all(), 0.0)
    self.Vd = V(self.Vbuf.ap[0:64, 0:10 * 528].rearrange("p (t h e) -> p t h e", t=10, h=8), self.Vbuf.res)
    self.memset(self.Vbuf.all(), 1.0)


def _mlstm_unit(self, C, u0, qcT, kcT, vaug, igf, osig, oT):
    ar = self.ar
    m0 = ar.mark()
    MUo, MLo, IRo = (C_MU64, C_ML64, C_IR64) if C == 64 else (C_MU32, C_ML32, C_IR32)
    MU2 = self.cv(MUo, 8 * C, parts=C)
    ML2 = self.cv(MLo, 8 * C, parts=C)
    IR = self.cv3(IRo, C, 8, C)
    idC = self.cv(C_ID, C, parts=C)
    onesC = self.cv(C_ONES, C, parts=C)
    selW = self.cv(C_SEL64W if C == 64 else C_SEL32W, 128, parts=C)

    def al(dt=F32):
        return ar.alloc([64, 8, 64], dt)

    def v3(t, p=C, n=C):
        return V(t.ap[0:p, :, 0:n], t.res)

    def bc(v8, p=C, n=C):
        return V(v8.ap.unsqueeze(2).to_broadcast([p, 8, n]), v8.res)

    def par(v, p_):
        return V(v.ap.rearrange("p (f t) c -> p f t c", t=2)[:, :, p_, :], v.res)

    def par8(v8, p_, n):
        return V(v8.ap.rearrange("p (f t) -> p f t", t=2)[:, :, p_].unsqueeze(2).to_broadcast([C, 4, n]), v8.res)

    st = ar.alloc([64, 80])
    sc_ = lambda k: V(st.ap[0:C, 8 * k:8 * k + 8], st.res)
    ig, lf, fcum, r, linter, m, u, tmp, winter, emm = [sc_(k) for k in range(10)]
    self.tt(ig, V(igf.ap[:, 0:8], igf.res), V(self.ibias.ap[0:C], self.ibias.res), ALU.add)
    self.tt(lf, V(igf.ap[:, 8:16], igf.res), V(self.fbias.ap[0:C], self.fbias.res), ALU.add)
    self.act(lf, lf, AF.Exp, scale=-1.0)
    self.act(lf, lf, AF.Ln, bias=self.one1[0:C, 0:1])
    self.ts(lf, lf, -1.0, ALU.mult)
    p1 = self.ps()
    self.mm(p1.v(C, 8), self.cv(C_UTRI, C, parts=C), lf)
    self.cp(fcum, p1.v(C, 8), "dve")
    self.tt(r, fcum, ig, ALU.subtract)
    self.tt(linter, fcum, V(self.mprev.ap[0:C], self.mprev.res), ALU.add)
    diag = al()
    self.tt(v3(diag), IR, bc(r), ALU.mult)
    pR = self.ps()
    self.mm(pR.v(C, 8 * C), idC, MU2, start=True, stop=False)
    for h in range(8):
        self.mm(pR.v(C, C, off=h * C), onesC, V(diag.ap[0:C, h, 0:C], diag.res), start=False, stop=(h == 7))
    self.red(tmp, pR.v(C, 8, C), ALU.min)
    self.tt(tmp, fcum, tmp, ALU.subtract)
    self.tt(m, linter, tmp, ALU.max)
    self.tt(u, fcum, m, ALU.subtract)
    diag2 = al()
    self.tt(v3(diag2), IR, bc(u), ALU.mult)
    pU = self.ps()
    self.mm(pU.v(C, 8 * C), idC, ML2, start=True, stop=False)
    for h in range(8):
        self.mm(pU.v(C, C, off=h * C), onesC, V(diag2.ap[0:C, h, 0:C], diag2.res), start=False, stop=(h == 7))
    wT = diag
    self.tt(v3(wT), pU.v(C, 8, C), bc(r), ALU.subtract)
    self.act(v3(wT), v3(wT), AF.Exp)
    pqk = (self.ps(), self.ps())
    for h in range(8):
        f, b0 = h // 2, (h % 2) * 64
        self.mm(pqk[h % 2].v(C, C, off=f * C), V(kcT.ap[b0:b0 + 64, f, u0:u0 + C], kcT.res),
                V(qcT.ap[b0:b0 + 64, f, u0:u0 + C], qcT.res))
    qkT = al(BF16)
    for p_ in range(2):
        self.tt(par(v3(qkT), p_), pqk[p_].v(C, 4, C), par(v3(wT), p_), ALU.mult)
    pin = (self.ps(), self.ps())
    pit = (self.ps(), self.ps())
    for h in range(8):
        f, b0 = h // 2, (h % 2) * 64
        self.mm(pin[h // 4].v(C, 65, off=(h % 4) * 65), V(qkT.ap[0:C, h, 0:C], qkT.res), V(vaug.ap[:, h, 0:65], vaug.res))
        self.mm(pit[h % 2].v(C, 65, off=f * 65), V(qcT.ap[b0:b0 + 64, f, u0:u0 + C], qcT.res),
                V(self.Cab.ap[b0:b0 + 64, f, 0:65], self.Cab.res))
    self.tt(winter, linter, m, ALU.subtract)
    self.act(winter, winter, AF.Exp)
    self.act(emm, m, AF.Exp, scale=-1.0)
    tot = ar.alloc([64, 8, 65])
    tv = V(tot.ap[0:C], tot.res)
    for p_ in range(2):
        self.tt(par(tv, p_), pit[p_].v(C, 4, 65), par8(winter, p_, 65), ALU.mult)
    for hf in range(2):
        th = V(tot.ap[0:C, 4 * hf:4 * hf + 4, :], tot.res)
        self.tt(th, th, pin[hf].v(C, 4, 65), ALU.add)
    den = ar.alloc([64, 8])
    dv_ = V(den.ap[0:C], den.res)
    self.ts(dv_, V(tot.ap[0:C, :, 64], tot.res), -1.0, ALU.mult)
    self.tt(dv_, dv_, V(tot.ap[0:C, :, 64], tot.res), ALU.max)
    self.tt(dv_, dv_, emm, ALU.max)
    self.recip(dv_, dv_)
    hh = diag2
    self.tt(v3(hh, C, 64), V(tot.ap[0:C, :, 0:64], tot.res), bc(dv_, C, 64), ALU.mult)
    self.out_norm_gate(hh, osig, self.nwc, C, oT, 0, u0)
    stat = ar.alloc([64, 24])
    sv = lambda k: V(stat.ap[0:C, 8 * k:8 * k + 8], stat.res)
    self.cp(sv(0), m, "dve")
    self.cp(sv(1), u, "dve")
    self.tt(sv(2), linter, m, ALU.subtract)
    pl = self.ps()
    self.mm(pl.v(128, 24), selW, V(stat.ap[0:C, :], stat.res))
    lastb = ar.alloc([128, 24])
    self.cp(lastb.all(), pl.v(128, 24), "dve")
    self.cp(self.mprev.all(), V(lastb.ap[0:64, 0:8], lastb.res), "dve")
    wl = ar.alloc([64, 8])
    wlv = V(wl.ap[0:C], wl.res)
    self.tt(wlv, V(lastb.ap[0:C, 8:16], lastb.res), r, ALU.subtract)
    self.act(wlv, wlv, AF.Exp)
    dec = ar.alloc([128, 8])
    dec2 = ar.alloc([128, 4])
    self.act(dec.all(), V(lastb.ap[:, 16:24], lastb.res), AF.Exp)
    for t_ in range(2):
        self.cp(V(dec2.ap[64 * t_:64 * t_ + 64, :], dec2.res),
                V(dec.ap[64 * t_:64 * t_ + 64, :].rearrange("p (f t) -> p f t", t=2)[:, :, t_], dec.res), "dve")
    pk = self.ps()
    for f in range(4):
        self.tr(pk.vb(C, 128, off=f * 128), V(kcT.ap[:, f, u0:u0 + C], kcT.res), self.idb.all())
    kw = al(BF16)
    self.tt(v3(kw, C, 64), pk.vb(C, 8, 64), bc(wlv, C, 64), ALU.mult)
    pdc = self.ps()
    for h in range(8):
        self.mm(pdc.v(64, 65, off=(h // 2) * 65, p0=(h % 2) * 64), V(kw.ap[0:C, h, :], kw.res), V(vaug.ap[:, h, 0:65], vaug.res))
    cav = V(self.Ca32.ap[:, :, 0:65], self.Ca32.res)
    self.tt(cav, cav, V(dec2.ap.unsqueeze(2).to_broadcast([128, 4, 65]), dec2.res), ALU.mult, eng="pool")
    self.tt(cav, cav, pdc.v(128, 4, 65), ALU.add)
    self.cp(V(self.Cab.ap[:, :, 0:65], self.Cab.res), cav, "act")
    ar.release(m0)


MK.odd_layer_setup = _odd_layer_setup
MK.mlstm_unit = _mlstm_unit


def _band_prompt_unit(self, c, u, qdT, oT):
    ar = self.ar
    m0 = ar.mark()
    blocks = []
    for kc in range(max(0, c - 8), c + 1):
        blocks.append(dict(kT=(self.KT, 64 * kc), nk=64, v=V(self.Vd.ap[:, kc % 10], self.Vbuf.res),
                           bias=V(self.bandt.ap[:, :, c - kc, 0:64], self.bandt.res), mask=None))
    oa = ar.alloc([64, 8, 64])
    self.attend(qdT, u * 64, 64, blocks, V(oa.ap[0:64], oa.res))
    self.to_oT(oa, 64, oT, 4, u * 64)
    ar.release(m0)


def _band_sample_unit(self, o, s, qdT, KdTs, Vds, oT):
    ar = self.ar
    m0 = ar.mark()
    KTc = ar.alloc([128, 4, 512], BF16)
    Vc = ar.alloc([64, 8, 8, 66], BF16)
    self.memset(Vc.all(), 1.0)
    m1 = ar.mark()
    stg = [ar.alloc([128, 512]) for _ in range(3)]
    self.load_cache_T(self.d["cd_k"].ap()[o, s], 512, KTc, stg)
    for kc in range(8):
        st = stg[kc % 3]
        self.dma(st[0:64, :], self.dv(self.d["cd_v"].ap()[o, s][64 * kc:64 * kc + 64, :]))
        self.cp(V(Vc.ap[:, kc, :, 0:64], Vc.res), V(st.ap[0:64].rearrange("p (h e) -> p h e", h=8), st.res))
    ar.release(m1)
    q0 = s * TS
    blocks = []
    for kc in range(8):
        blocks.append(dict(kT=(KTc, 64 * kc), nk=64, v=V(Vc.ap[:, kc], Vc.res),
                           bias=V(self.bandt.ap[:, :, 8 - kc, 0:TS], self.bandt.res), mask=None))
    blocks.append(dict(kT=(KdTs, q0), nk=TS, v=V(Vds.ap[:, s], Vds.res),
                       bias=V(self.bandt.ap[:, :, 0, 0:TS], self.bandt.res), mask=None))
    oa = ar.alloc([64, 8, 64])
    self.attend(qdT, q0, TS, blocks, V(oa.ap[0:TS], oa.res))
    self.to_oT(oa, TS, oT, 4, q0)
    ar.release(m0)


def _odd_block(self, L, o, blk):
    ar = self.ar
    t0, TB, C, NU, smp = blk.t0, blk.TB, blk.C, blk.NU, blk.sample
    W = self.d["w_in_odd"].ap()[o]
    mB = ar.mark()
    oT = ar.alloc([128, 8, TB], BF16)
    qdT = ar.alloc([128, 8, TB], BF16)
    self.memset(qdT.all(), 0.0)
    if smp:
        KdTs = ar.alloc([128, 4, TB], BF16)
        Vds = ar.alloc([32, NS, 8, 66], BF16)
        self.memset(Vds.all(), 1.0)
    mG = ar.mark()
    qcT = ar.alloc([128, 4, TB], BF16)
    kcT = ar.alloc([128, 4, TB], BF16)
    vaug = ar.alloc([64, NU, 8, 66], BF16)
    self.memset(vaug.all(), 1.0)
    igf = ar.alloc([64, NU, 16])
    osig = ar.alloc([64, NU, 512])
    mP = ar.mark()
    xb = ar.alloc([128, 8, TB], BF16)
    for ch in range(8):
        self.cp(xb.part(ch).all(), V(self.xT.ap[:, ch, t0:t0 + TB], self.xres(ch, 1, t0, t0 + TB)))
    wts = [ar.alloc([128, 8, 256], BF16), ar.alloc([128, 8, 256], BF16)]
    wn = [0]

    def ldw(c0, n):
        wt = wts[wn[0] % 2]
        wn[0] += 1
        self.load_w(W[:, c0:c0 + n], D, n, wt)
        return wt
    ost = [ar.alloc([128, 256]), ar.alloc([128, 256])]
    osi = [0]
    wpad = ar.alloc([128, 8, 512], BF16)
    for (c0, dst, scl) in ((0, qcT, 1.0), (256, kcT, 32 ** -0.5)):
        self.memset(wpad.all(), 0.0)
        self.load_w(W[:, c0:c0 + 256], D, 256, wpad, pad_heads=True)
        for f in range(4):
            ps = self.proj_fm(wpad, f * 128, 128, xb, 0, TB)
            self.act(dst.part(f).all(), ps.v(128, TB), AF.Copy, scale=scl)
    for half in range(2):
        wt = ldw(512 + 256 * half, 256)
        for u in range(NU):
            ps = self.proj_tm(wt, 0, 256, xb, u * C, u * C + C)
            self.cp(V(vaug.ap[0:C, u, 4 * half:4 * half + 4, 0:64], vaug.res), ps.v(C, 4, 64))
    wt = wts[wn[0] % 2]; wn[0] += 1
    st_ = self.wstage[self.weng % 2]; self.weng += 1
    sv = V(st_.ap[:, 0:8 * 16].rearrange("p (k n) -> p k n", k=8), st_.res)
    self.dma(sv, self.dv(W[:, 1024:1040].rearrange("(k p) n -> p k n", p=128)))
    self.cp(V(wt.ap[:, :, 0:16], wt.res), sv, "pool")
    for u in range(NU):
        ps = self.proj_tm(wt, 0, 16, xb, u * C, u * C + C)
        self.cp(V(igf.ap[0:C, u, :], igf.res), ps.v(C, 16), "dve")
    for half in range(2):
        wt = ldw(1040 + 256 * half, 256)
        for u in range(NU):
            ps = self.proj_tm(wt, 0, 256, xb, u * C, u * C + C)
            self.act(V(osig.ap[0:C, u, 256 * half:256 * half + 256], osig.res), ps.v(C, 256), AF.Sigmoid)
    for half in range(2):
        wt = ldw(1552 + 256 * half, 256)
        for ff in range(2):
            ps = self.proj_fm(wt, ff * 128, 128, xb, 0, TB)
            f_ = half * 2 + ff
            self.act(V(qdT.ap[0:64, 2 * f_, :], qdT.res), ps.v(64, TB), AF.Copy, scale=0.125)
            self.act(V(qdT.ap[64:128, 2 * f_ + 1, :], qdT.res), ps.v(64, TB, p0=64), AF.Copy, scale=0.125)
    want_out = smp or blk.idx >= 6
    for half in range(2):
        wt = ldw(2064 + 256 * half, 256)
        for ff in range(2):
            f = half * 2 + ff
            ps = self.proj_fm(wt, ff * 128, 128, xb, 0, TB)
            if smp:
                self.cp(KdTs.part(f).all(), ps.v(128, TB))
            else:
                self.cp(self.KT.part(f).cols(t0, t0 + TB).all(), ps.v(128, TB))
        if want_out:
            for tt in range(TB // 128):
                ps = self.proj_tm(wt, 0, 256, xb, tt * 128, tt * 128 + 128)
                o_ = ost[osi[0] % 2]; osi[0] += 1
                self.cp(o_.all(), ps.v(128, 256))
                r0 = t0 + tt * 128 - (SEQ - 512)
                dst = self.d["d_k_s"].ap()[o] if smp else self.d["d_k_p"].ap()[o][r0:r0 + 128]
                self.dma(self.dv(dst[:, 256 * half:256 * half + 256]), o_.all(), eng="pool")
    for half in range(2):
        wt = ldw(2576 + 256 * half, 256)
        for u in range(NU):
            ps = self.proj_tm(wt, 0, 256, xb, u * C, u * C + C)
            if smp:
                self.cp(V(Vds.ap[:, u, 4 * half:4 * half + 4, 0:64], Vds.res), ps.v(C, 4, 64), "dve")
            else:
                c = t0 // 64 + u
                self.cp(V(self.Vd.ap[:, c % 10, 4 * half:4 * half + 4, 0:64], self.Vbuf.res), ps.v(C, 4, 64), "dve")
            if want_out:
                o_ = ost[osi[0] % 2]; osi[0] += 1
                self.cp(o_[0:C, :], ps.v(C, 256), "act")
                if smp:
                    dst = self.d["d_v_s"].ap()[o][u * TS:u * TS + TS]
                else:
                    r0 = t0 + u * 64 - (SEQ - 512)
                    dst = self.d["d_v_p"].ap()[o][r0:r0 + 64]
                self.dma(self.dv(dst[:, 256 * half:256 * half + 256]), o_[0:C, :], eng="pool")
    ar.release(mP)
    def mlstm_all():
        self.bset = ("A", [0, 1, 2, 3, 4])
        for u in range(NU):
            if smp:
                self.memset(self.Ca32.all(), 0.0)
                for t_ in range(2):
                    self.dma(V(self.Ca32.ap[64 * t_:64 * t_ + 32, :, 0:64], self.Ca32.res),
                             self.dv(self.d["sc_c"].ap()[o, u].rearrange("(f t) k v -> t k f v", t=2)[t_]))
                    self.dma(V(self.Ca32.ap[64 * t_:64 * t_ + 32, :, 64], self.Ca32.res),
                             self.dv(self.d["sc_n"].ap()[o, u].rearrange("(f t) k -> t k f", t=2)[t_]), nc_ok=True)
                self.cp(self.Cab.all(), self.Ca32.all(), "act")
                self.bcast_row(self.mprev.all(), self.d["sc_m"].ap()[o, u], 64)
            self.mlstm_unit(C, u * C, qcT, kcT, V(vaug.ap[0:C, u], vaug.res), V(igf.ap[0:C, u, :], igf.res),
                            V(osig.ap[0:C, u, :].rearrange("p (h e) -> p h e", h=8), osig.res), oT)
            if smp or (blk.idx == 7 and u == NU - 1):
                cc = self.d["c_c_s"].ap()[o, u] if smp else self.d["c_c_p"].ap()[o]
                cn = self.d["c_n_s"].ap()[o, u] if smp else self.d["c_n_p"].ap()[o]
                cm = self.d["c_m_s"].ap()[o, u:u + 1] if smp else self.d["c_m_p"].ap()[o:o + 1]
                for t_ in range(2):
                    self.dma(self.dv(cc.rearrange("(f t) k v -> t k f v", t=2)[t_]),
                             V(self.Ca32.ap[64 * t_:64 * t_ + 32, :, 0:64], self.Ca32.res), eng="pool")
                    self.dma(self.dv(cn.rearrange("(f t) k -> t k f", t=2)[t_]),
                             V(self.Ca32.ap[64 * t_:64 * t_ + 32, :, 64], self.Ca32.res), eng="pool", nc_ok=True)
                self.dma(self.dv(cm), V(self.mprev.ap[0:1, :], self.mprev.res), eng="pool")
    def band_all():
        self.bset = ("B", [5, 6])
        for u in range(NU):
            if smp:
                self.band_sample_unit(o, u, qdT, KdTs, Vds, oT)
            else:
                self.band_prompt_unit(t0 // 64 + u, u, qdT, oT)

    if smp:
        mlstm_all()
        ar.release(mG)
        band_all()
    else:
        self.parallel(mlstm_all, band_all)
        ar.release(mG)
    self.bset = None
    self.out_proj_ln(self.d["w_out_odd"].ap()[o], oT, t0, TB)
    ar.release(mB)


MK.band_prompt_unit = _band_prompt_unit
MK.band_sample_unit = _band_sample_unit
MK.odd_block = _odd_block
```
